# Optimizing a Trainium2 kernel written in Bass

```python
import math
import jax, jax.numpy as jnp
from jax import lax
import numpy as np

D_MODEL = 1024
BATCH = 32
SEQ = 2048
DEPTH = 2
DEC_BATCH = 16
DEC_SEQ = 4096
PAST_LEN = 128

N_MIXERS = 2
EPS = 1e-6
D_FF = 4 * D_MODEL
DA_HEADS = 8
DA_HEAD_DIM = D_MODEL // DA_HEADS // 2
DA_V_DIM = 2 * DA_HEAD_DIM
Q_BLOCK = 128
RET_HEADS = 4
RET_QK_DIM = D_MODEL // RET_HEADS
RET_V_DIM = 2 * D_MODEL // RET_HEADS
RET_CHUNK = 128

kernel_name = 'hybrid_diffattn_retention_encoder'

F32 = jnp.float32


def rms_norm(x, g):
    xf = x.astype(F32)
    y = xf * lax.rsqrt(jnp.mean(xf * xf, axis=-1, keepdims=True) + EPS)
    return (y * g.astype(F32)).astype(x.dtype)


def alibi_slopes(n):
    return jnp.asarray(np.array([2.0 ** (-8.0 * (h + 1) / n) for h in range(n)], dtype=np.float32))


def lambda_init_fn(layer_idx):
    return 0.8 - 0.6 * math.exp(-0.3 * layer_idx)


def diff_attention(x, w_in, w_out, lq1, lk1, lq2, lk2, g_sub, layer_idx):
    B, S, _ = x.shape
    H, d = DA_HEADS, DA_HEAD_DIM
    qkv = x @ w_in
    q, k, v = jnp.split(qkv, [H * 2 * d, 2 * H * 2 * d], axis=-1)
    q = q.reshape(B, S, H, 2, d) * (d ** -0.5)
    k = k.reshape(B, S, H, 2, d)
    v = v.reshape(B, S, H, DA_V_DIM)
    lam_init = lambda_init_fn(layer_idx)
    lam = (jnp.exp(jnp.sum(lq1.astype(F32) * lk1.astype(F32)))
           - jnp.exp(jnp.sum(lq2.astype(F32) * lk2.astype(F32))) + lam_init)
    slopes = alibi_slopes(H)
    kpos = jnp.arange(S, dtype=F32)
    nq = S // Q_BLOCK
    qb = q.reshape(B, nq, Q_BLOCK, H, 2, d).transpose(1, 0, 2, 3, 4, 5)

    def block(args):
        q_blk, bi = args
        qpos = (bi * Q_BLOCK + jnp.arange(Q_BLOCK)).astype(F32)
        bias = -slopes[:, None, None] * jnp.abs(qpos[:, None] - kpos[None, :])
        s = jnp.einsum('bqhcd,bkhcd->bhcqk', q_blk, k).astype(F32) + bias[None, :, None]
        p = jax.nn.softmax(s, axis=-1)
        w = p[:, :, 0] - lam * p[:, :, 1]
        return jnp.einsum('bhqk,bkhe->bqhe', w.astype(v.dtype), v)

    o = lax.map(block, (qb, jnp.arange(nq)))
    o = o.transpose(1, 0, 2, 3, 4).reshape(B, S, H, DA_V_DIM)
    o = rms_norm(o, g_sub) * (1.0 - lam_init)
    return o.reshape(B, S, H * DA_V_DIM) @ w_out


def retention_scan(q, k, v, log_gamma):
    B, S, H, dk = q.shape
    dv = v.shape[-1]
    C = RET_CHUNK
    N = S // C
    q = q.reshape(B, N, C, H, dk)
    k = k.reshape(B, N, C, H, dk)
    v = v.reshape(B, N, C, H, dv)
    idx = jnp.arange(C, dtype=F32)
    diff = idx[:, None] - idx[None, :]
    dmask = jnp.where(diff[None] >= 0,
                      jnp.exp(jnp.maximum(diff, 0.0)[None] * log_gamma[:, None, None]), 0.0)
    inner = jnp.einsum('bnihd,bnjhd->bnhij', q, k) * dmask[None, None]
    y_inner = jnp.einsum('bnhij,bnjhe->bnihe', inner, v)
    q_dec = q * jnp.exp((idx[:, None] + 1.0) * log_gamma[None, :])[:, :, None]
    k_dec = k * jnp.exp((C - 1.0 - idx)[:, None] * log_gamma[None, :])[:, :, None]
    chunk_decay = jnp.exp(C * log_gamma)[None, :, None, None]

    def step(state, xs):
        qn, kn, vn = xs
        y = jnp.einsum('bihd,bhde->bihe', qn, state)
        state = state * chunk_decay + jnp.einsum('bjhd,bjhe->bhde', kn, vn)
        return state, y

    init = jnp.zeros((B, H, dk, dv), F32)
    _, y_cross = lax.scan(step, init, (q_dec.swapaxes(0, 1), k_dec.swapaxes(0, 1), v.swapaxes(0, 1)))
    y = y_inner + y_cross.swapaxes(0, 1)
    return y.reshape(B, S, H, dv)


def retention(x, w_in, w_out, decay_fwd, decay_bwd, gn_w, gn_b):
    B, S, _ = x.shape
    H, dk, dv = RET_HEADS, RET_QK_DIM, RET_V_DIM
    proj = x @ w_in
    q, k, v, g = jnp.split(proj, [H * dk, 2 * H * dk, 2 * H * dk + H * dv], axis=-1)
    q = q.reshape(B, S, H, dk).astype(F32) * (dk ** -0.5)
    k = k.reshape(B, S, H, dk).astype(F32)
    v = v.reshape(B, S, H, dv).astype(F32)
    lg_f = jnp.log1p(-jnp.exp(decay_fwd.astype(F32)))
    lg_b = jnp.log1p(-jnp.exp(decay_bwd.astype(F32)))
    y_f = retention_scan(q, k, v, lg_f)
    y_b = retention_scan(q[:, ::-1], k[:, ::-1], v[:, ::-1], lg_b)[:, ::-1]
    y = y_f + y_b
    mu = jnp.mean(y, axis=-1, keepdims=True)
    yc = y - mu
    y = yc * lax.rsqrt(jnp.mean(yc * yc, axis=-1, keepdims=True) + EPS)
    y = y.reshape(B, S, H * dv) * gn_w.astype(F32) + gn_b.astype(F32)
    out = jax.nn.silu(g.astype(F32)) * y
    return out.astype(x.dtype) @ w_out


def squared_relu_mlp(x, w1, w2):
    return jnp.square(jax.nn.relu(x @ w1)) @ w2


def encoder_layer(x, layer_idx, norms, mixer_params, ffn_params):
    g_pre_mix, g_post_mix, g_pre_ffn, g_post_ffn = norms
    h = rms_norm(x, g_pre_mix)
    if layer_idx % N_MIXERS == 0:
        m = diff_attention(h, *mixer_params, layer_idx=layer_idx)
    else:
        m = retention(h, *mixer_params)
    x = x + rms_norm(m, g_post_mix)
    f = squared_relu_mlp(rms_norm(x, g_pre_ffn), *ffn_params)
    return x + rms_norm(f, g_post_ffn)


def setup_inputs(seed: int = 0) -> dict:
    key = jax.random.key(seed)
    ks = iter(jax.random.split(key, 48))

    def nrm(shape, scale):
        return jax.random.normal(next(ks), shape, F32) * scale

    def gain(n):
        return 1.0 + nrm((n,), 0.02)

    H_da, H_r = DA_HEADS, RET_HEADS
    ret_in = 2 * H_r * RET_QK_DIM + 2 * H_r * RET_V_DIM
    base_decay = (-5.0 - jnp.arange(H_r, dtype=F32)) * math.log(2.0)
    d = {}
    d['x_prompt'] = nrm((BATCH, SEQ, D_MODEL), 1.0)
    d['x_sample'] = nrm((DEC_BATCH, DEC_SEQ, D_MODEL), 1.0)
    d['l0_norm_pre_mix'] = gain(D_MODEL)
    d['l0_norm_post_mix'] = gain(D_MODEL)
    d['l0_norm_pre_ffn'] = gain(D_MODEL)
    d['l0_norm_post_ffn'] = gain(D_MODEL)
    d['l0_da_w_in'] = nrm((D_MODEL, 3 * H_da * 2 * DA_HEAD_DIM), D_MODEL ** -0.5)
    d['l0_da_w_out'] = nrm((H_da * DA_V_DIM, D_MODEL), (H_da * DA_V_DIM) ** -0.5)
    d['l0_da_lambda_q1'] = nrm((DA_HEAD_DIM,), 0.1)
    d['l0_da_lambda_k1'] = nrm((DA_HEAD_DIM,), 0.1)
    d['l0_da_lambda_q2'] = nrm((DA_HEAD_DIM,), 0.1)
    d['l0_da_lambda_k2'] = nrm((DA_HEAD_DIM,), 0.1)
    d['l0_da_subln'] = gain(DA_V_DIM)
    d['l0_ffn_w1'] = nrm((D_MODEL, D_FF), D_MODEL ** -0.5)
    d['l0_ffn_w2'] = nrm((D_FF, D_MODEL), D_FF ** -0.5)
    d['l1_norm_pre_mix'] = gain(D_MODEL)
    d['l1_norm_post_mix'] = gain(D_MODEL)
    d['l1_norm_pre_ffn'] = gain(D_MODEL)
    d['l1_norm_post_ffn'] = gain(D_MODEL)
    d['l1_ret_w_in'] = nrm((D_MODEL, ret_in), D_MODEL ** -0.5)
    d['l1_ret_w_out'] = nrm((H_r * RET_V_DIM, D_MODEL), (H_r * RET_V_DIM) ** -0.5)
    d['l1_ret_decay_fwd'] = base_decay + nrm((H_r,), 0.05)
    d['l1_ret_decay_bwd'] = base_decay + nrm((H_r,), 0.05)
    d['l1_ret_gn_w'] = gain(H_r * RET_V_DIM)
    d['l1_ret_gn_b'] = nrm((H_r * RET_V_DIM,), 0.02)
    d['l1_ffn_w1'] = nrm((D_MODEL, D_FF), D_MODEL ** -0.5)
    d['l1_ffn_w2'] = nrm((D_FF, D_MODEL), D_FF ** -0.5)
    return d


def reference(x_prompt, x_sample,
              l0_norm_pre_mix, l0_norm_post_mix, l0_norm_pre_ffn, l0_norm_post_ffn,
              l0_da_w_in, l0_da_w_out, l0_da_lambda_q1, l0_da_lambda_k1,
              l0_da_lambda_q2, l0_da_lambda_k2, l0_da_subln, l0_ffn_w1, l0_ffn_w2,
              l1_norm_pre_mix, l1_norm_post_mix, l1_norm_pre_ffn, l1_norm_post_ffn,
              l1_ret_w_in, l1_ret_w_out, l1_ret_decay_fwd, l1_ret_decay_bwd,
              l1_ret_gn_w, l1_ret_gn_b, l1_ffn_w1, l1_ffn_w2):
    layers = [
        ((l0_norm_pre_mix, l0_norm_post_mix, l0_norm_pre_ffn, l0_norm_post_ffn),
         (l0_da_w_in, l0_da_w_out, l0_da_lambda_q1, l0_da_lambda_k1,
          l0_da_lambda_q2, l0_da_lambda_k2, l0_da_subln),
         (l0_ffn_w1, l0_ffn_w2)),
        ((l1_norm_pre_mix, l1_norm_post_mix, l1_norm_pre_ffn, l1_norm_post_ffn),
         (l1_ret_w_in, l1_ret_w_out, l1_ret_decay_fwd, l1_ret_decay_bwd,
          l1_ret_gn_w, l1_ret_gn_b),
         (l1_ffn_w1, l1_ffn_w2)),
    ]

    def trunk(x):
        for i in range(DEPTH):
            norms, mixer_params, ffn_params = layers[i]
            x = encoder_layer(x, i, norms, mixer_params, ffn_params)
        return x

    y_prompt = trunk(x_prompt)
    y_sample = trunk(x_sample)
    return (y_prompt, y_sample)
```

```python
import math
from contextlib import ExitStack

import numpy as np
import ml_dtypes

import concourse.bass as bass
import concourse.mybir as mybir
from concourse.bass_utils import run_bass_kernel_spmd

F32 = mybir.dt.float32
BF16 = mybir.dt.bfloat16
AF = mybir.ActivationFunctionType
ALU = mybir.AluOpType

D = 1024
DFF = 4096
EPS = 1e-6
NCORES = 8
LAM_INIT0 = 0.8 - 0.6 * math.exp(-0.3 * 0)
SLOPES = [2.0 ** (-(h + 1)) for h in range(8)]


class Buf:
    __slots__ = ("name", "w", "r", "excl")

    def __init__(self, name="", excl=False):
        self.name = name
        self.w = None
        self.r = {}
        self.excl = excl


class Sched:
    ENGS = ("pe", "act", "dve", "pool", "sp")

    def __init__(self):
        self.ops = {e: [] for e in self.ENGS}
        self.cnt = {e: 0 for e in self.ENGS}
        self.seen = {e: {} for e in self.ENGS}
        self.dma_cnt = {}
        self.n_dma_sems = 0
        self.free_sems = []
        self.cur_sems = []

    def new_dma_sem(self):
        if self.free_sems:
            k = self.free_sems.pop()
        else:
            k = "dma%d" % self.n_dma_sems
            self.n_dma_sems += 1
            self.dma_cnt[k] = 0
        self.cur_sems.append(k)
        return k

    def _deps(self, eng, reads, writes):
        deps = {}

        def add(tok):
            k, v, src = tok
            if src == eng and eng == "pe":
                return
            if v > deps.get(k, 0):
                deps[k] = v
        for b in reads:
            if b.w is not None:
                add(b.w)
            if b.excl:
                for k, (v, src) in b.r.items():
                    if src != eng:
                        add((k, v, src))
        for b in writes:
            if b.w is not None:
                add(b.w)
            for k, (v, src) in b.r.items():
                if src == eng:
                    continue
                add((k, v, src))
        waits = []
        seen = self.seen[eng]
        for k, v in deps.items():
            if seen.get(k, 0) >= v:
                continue
            seen[k] = v
            waits.append((k, v))
        return waits

    def op(self, eng, fn, reads=(), writes=()):
        waits = self._deps(eng, reads, writes)
        self.cnt[eng] += 1
        tok = (eng, self.cnt[eng], eng)
        self.ops[eng].append((fn, waits, (eng, 1)))
        for b in reads:
            b.r[eng] = (tok[1], eng)
        for b in writes:
            b.w = tok
            b.r = {}
        return tok

    def dma(self, eng, sem_key, fn, reads=(), writes=()):
        waits = self._deps(eng, reads, writes)
        self.dma_cnt[sem_key] += 16
        tok = (sem_key, self.dma_cnt[sem_key], "dma")
        self.ops[eng].append((fn, waits, (sem_key, 16)))
        for b in reads:
            b.r[sem_key] = (tok[1], "dma")
        for b in writes:
            b.w = tok
            b.r = {}
        return tok

    def barrier(self):
        toks = [(e, self.cnt[e]) for e in self.ENGS if self.cnt[e] > 0]
        toks += [(k, v) for k, v in self.dma_cnt.items() if v > 0]
        for e in self.ENGS:
            waits = []
            for k, v in toks:
                if k == e:
                    continue
                if self.seen[e].get(k, 0) >= v:
                    continue
                self.seen[e][k] = v
                waits.append((k, v))
            if waits:
                self.ops[e].append((None, waits, None))
        self.free_sems.extend(self.cur_sems)
        self.cur_sems = []

    def final_wait(self, eng):
        waits = [(k, v) for k, v in self.dma_cnt.items() if v > 0]
        self.ops[eng].append((None, waits, None))

    def emit(self, nc, block, sems):
        handles = {"pe": "tensor", "act": "scalar", "dve": "vector", "pool": "gpsimd", "sp": "sync"}
        for en in self.ENGS:
            ops = self.ops[en]
            if not ops:
                continue

            def body(e, ops=ops):
                for fn, waits, inc in ops:
                    for k, v in waits:
                        e.wait_ge(sems[k], v)
                    if fn is not None:
                        ins = fn(e)
                        if inc is not None:
                            ins.then_inc(sems[inc[0]], inc[1])
            getattr(block, handles[en])(body)


class Tl:
    def __init__(self, t, name, S):
        self.t = t
        self.b = Buf(name)
        self.S = S
        self._sem = None

    @property
    def sem(self):
        if self._sem is None:
            self._sem = self.S.new_dma_sem()
        return self._sem


def build_program(seqs, phases=("cast", "attn", "ffn0", "ret", "ffn1"), debug_out=()):
    T = sum(seqs)
    seq_off = [sum(seqs[:i]) for i in range(len(seqs))]
    nc = bass.Bass("TRN2", target_bir_lowering=False)
    S = Sched()

    def din(name, shape, dt=F32):
        return nc.dram_tensor(name, list(shape), dt, kind="ExternalInput").ap()

    def dint(name, shape, dt):
        return nc.dram_tensor(name, list(shape), dt, kind="Internal").ap()

    x_in = din("x", [T, D])
    y_out = nc.dram_tensor("y", [T, D], F32, kind="ExternalOutput").ap()
    wnames = {"w0in": (D, 3072), "w0out": (D, D), "w0f1": (D, DFF), "w0f2": (DFF, D),
              "w1in": (D, 6144), "w1out": (2048, D), "w1f1": (D, DFF), "w1f2": (DFF, D)}
    wf = {k: din(k, v) for k, v in wnames.items()}
    wb = {k: dint(k + "_b", v, BF16) for k, v in wnames.items()}
    gn = {}
    for l in (0, 1):
        for nm in ("pre_mix", "post_mix", "pre_ffn", "post_ffn"):
            gn[(l, nm)] = din("g%d_%s" % (l, nm), [D])
    lamv = din("lamv", [4, 64])
    subln = din("subln", [128])
    decay = din("decay", [8])
    gnw = din("gnw", [2048])
    gnb = din("gnb", [2048])
    ident_d = din("ident", [128, 128], BF16)
    ones_d = din("ones", [128, 128], BF16)
    abase_d = din("abase", [128, 5, 512])
    rc_d = din("rc", [128, 6, 128])
    cc_d = din("cc", [128, 2])
    cbt_d = din("cbt", [128, 8, 32])
    ka_d = din("ka", [128, 16, 128], BF16)
    qa_d = din("qa", [128, 512], BF16)

    XA = dint("XA", [T, D], F32)
    XB = dint("XB", [T, D], F32)
    KTs = dint("KTs", [8, 128, T], BF16)
    Vs = dint("Vs", [8, 128, T // 128, 128], BF16)
    KS = dint("KS", [T, 1024], BF16)
    VS = dint("VS", [T, 2048], BF16)
    SG = dint("SG", [T, 2048], BF16)
    SBs = dint("SBs", [T // 128, 128, 4096], BF16)
    dbg = {}
    for nm, shape in debug_out:
        dbg[nm] = nc.dram_tensor(nm, list(shape), BF16 if nm.startswith("dump_") else F32, kind="ExternalOutput").ap()

    dramB = {k: Buf(k) for k in ["XA", "XB", "KTs", "Vs", "KS", "VS", "SG", "SBs", "y"] + list(wb)}

    es = ExitStack()
    with es:
        def sb(name, shape, dt=F32):
            return Tl(es.enter_context(nc.sbuf_tensor("s_" + name, list(shape), dt)), name, S)

        ident = sb("ident", [128, 128], BF16)
        ones = sb("ones", [128, 128], BF16)
        pairs = [es.enter_context(nc.psum_tensor("pair%d" % i, [128, 2, 512], F32)) for i in range(2)]
        banks = [Tl(pairs[i // 2][:, i % 2, :], "bank%d" % i, S) for i in range(4)]
        banks += [Tl(es.enter_context(nc.psum_tensor("bank%d" % i, [128, 512], F32)), "bank%d" % i, S)
                  for i in range(4, 8)]
        for bk_ in banks:
            bk_.b.excl = True
        junk = sb("junk", [128, 1024], BF16)
        st = sb("st", [128, 128], F32)
        stB = [Buf("st%d" % i) for i in range(64)]
        stn = [0]

        def tiny(n=1):
            i = stn[0] % 64
            stn[0] += 1
            return st.t[:, 2 * i:2 * i + n], stB[i]

        cq = [0]
        dmaq = ("sp", "act", "pool")

        def ldq():
            cq[0] += 1
            return "sp"

        S.dma("sp", ident.sem, lambda e: e.dma_start(out=ident.t[:], in_=ident_d), writes=[ident.b])
        S.dma("sp", ones.sem, lambda e: e.dma_start(out=ones.t[:], in_=ones_d), writes=[ones.b])

        if "cast" in phases:
            csem = S.new_dma_sem()
            ncast = [0]
            for k, (r, c) in wnames.items():
                step = 128 * max(1, (1024 * 1024) // (c * 128))
                for r0 in range(0, r, step):
                    r1 = min(r, r0 + step)
                    S.dma("pool", csem,
                          lambda e, k=k, r0=r0, r1=r1: e.dma_start(
                              out=wb[k][r0:r1, :].rearrange("r (a c) -> r a c", c=1024),
                              in_=wf[k][r0:r1, :].rearrange("r (a c) -> r a c", c=1024)),
                          writes=[dramB[k]])
                    ncast[0] += 1
                    if ncast[0] % 8 == 0:
                        S.ops["pool"].append((None, [(csem, S.dma_cnt[csem])], None))

        def load_w(dst, dname, r0, nkc, c0, ncols, col_off=0):
            src = wb[dname][r0:r0 + nkc * 128, c0:c0 + ncols].rearrange("(k p) n -> p k n", p=128)
            step = max(1, nkc // 4)
            for k0 in range(0, nkc, step):
                k1 = min(nkc, k0 + step)
                S.dma("sp", dst.sem,
                      lambda e, k0=k0, k1=k1: e.dma_start(out=dst.t[:, k0:k1, col_off:col_off + ncols],
                                                          in_=src[:, k0:k1, :]),
                      reads=[dramB[dname]], writes=[dst.b])

        def load_gain(dst, vec):
            S.dma("sp", dst.sem, lambda e: e.dma_start(out=dst.t[:], in_=vec.partition_broadcast(128)),
                  writes=[dst.b])

        def rstd_from_ss(ss_ap, ss_b, scale):
            rs, rs_b = tiny(1)
            S.op("act", lambda e: e.activation(out=rs, in_=ss_ap, func=AF.Sqrt, bias=EPS, scale=scale),
                 reads=[ss_b], writes=[rs_b])
            rr, rr_b = tiny(1)
            S.op("dve", lambda e: e.reciprocal(out=rr, in_=rs), reads=[rs_b], writes=[rr_b])
            return rr, rr_b

        def norm_T(xt, gt, h, hT, tcol, pbank):
            norm_A(xt, gt, h)
            norm_B(h, hT, tcol, pbank)

        def norm_A(xt, gt, h):
            ss, ss_b = tiny(1)
            S.op("act", lambda e: e.activation(out=junk.t[:], in_=xt.t[:], func=AF.Square, accum_out=ss),
                 reads=[xt.b], writes=[ss_b])
            rr, rr_b = rstd_from_ss(ss, ss_b, 1.0 / D)
            S.op("dve", lambda e: e.scalar_tensor_tensor(out=h.t[:], in0=xt.t[:], scalar=rr, in1=gt.t[:],
                                                          op0=ALU.mult, op1=ALU.mult),
                 reads=[xt.b, rr_b, gt.b], writes=[h.b])

        def norm_B(h, hT, tcol, pbank):
            pT = pbank.t[:].bitcast(BF16)
            for kc in range(8):
                S.op("pe", lambda e, kc=kc: e.transpose(pT[:, kc * 128:(kc + 1) * 128],
                                                         h.t[:, kc * 128:(kc + 1) * 128], ident.t[:]),
                     reads=[h.b, ident.b], writes=[pbank.b])
            S.op("act", lambda e: e.activation(out=hT.t[:, 0:8, tcol:tcol + 128],
                                               in_=pT.rearrange("p (k n) -> p k n", k=8), func=AF.Copy),
                 reads=[pbank.b], writes=[hT.b])

        def post_norm_store(po, xt, gt, ot, dst, dstB, row0):
            ss, ss_b = tiny(2)
            for hf in range(2):
                S.op("act", lambda e, hf=hf: e.activation(out=junk.t[:, 0:512], in_=po[hf].t[:], func=AF.Square,
                                                           accum_out=ss[:, hf:hf + 1]),
                     reads=[po[hf].b], writes=[ss_b])
            s1, s1_b = tiny(1)
            S.op("dve", lambda e: e.tensor_tensor(out=s1, in0=ss[:, 0:1], in1=ss[:, 1:2], op=ALU.add),
                 reads=[ss_b], writes=[s1_b])
            rr, rr_b = rstd_from_ss(s1, s1_b, 1.0 / D)
            for hf in range(2):
                S.op("dve", lambda e, hf=hf: e.scalar_tensor_tensor(
                    out=ot.t[:, hf * 512:(hf + 1) * 512], in0=po[hf].t[:], scalar=rr,
                    in1=gt.t[:, hf * 512:(hf + 1) * 512], op0=ALU.mult, op1=ALU.mult),
                    reads=[po[hf].b, rr_b, gt.b], writes=[ot.b])
            S.op("pool", lambda e: e.tensor_tensor(out=ot.t[:], in0=ot.t[:], in1=xt.t[:], op=ALU.add),
                 reads=[ot.b, xt.b], writes=[ot.b])
            S.dma("sp", ot.sem, lambda e: e.dma_start(out=dst[row0:row0 + 128, :], in_=ot.t[:]),
                  reads=[ot.b], writes=[dstB])

        def ffn_phase(pfx, src, srcB, dst, dstB, w1n, w2n, gpre_v, gpost_v):
            with ExitStack() as ps:
                def sbl(name, shape, dt=F32):
                    return Tl(ps.enter_context(nc.sbuf_tensor("s_" + pfx + name, list(shape), dt)), pfx + name, S)
                w1 = sbl("w1", [128, 8, DFF], BF16)
                w2 = sbl("w2", [128, 32, D], BF16)
                gpre = sbl("gpre", [128, D])
                gpost = sbl("gpost", [128, D])
                load_gain(gpre, gpre_v)
                load_gain(gpost, gpost_v)
                load_w(w1, w1n, 0, 8, 0, DFF)
                load_w(w2, w2n, 0, 32, 0, D)
                xin = [sbl("xin%d" % i, [128, D]) for i in range(6)]
                hb = [sbl("h%d" % i, [128, D], BF16) for i in range(2)]
                hT = [sbl("hT%d" % i, [128, 8, 256], BF16) for i in range(2)]
                hid = sbl("hid", [128, 32, 256], BF16)
                hidB = [Buf("hid%d" % i) for i in range(16)]
                rt = [sbl("rt%d" % i, [128, 512], BF16) for i in range(2)]
                ot = [sbl("ot%d" % i, [128, D]) for i in range(2)]
                NB = T // 256
                pbT, pf, pob = banks[0], banks[1:3], banks[3:7]

                def loadx(b):
                    for t in range(2):
                        xt = xin[(b * 2 + t) % 6]
                        r0 = b * 256 + t * 128
                        S.dma("sp", xt.sem, lambda e, xt=xt, r0=r0: e.dma_start(out=xt.t[:], in_=src[r0:r0 + 128, :]),
                              reads=[srcB], writes=[xt.b])

                def pre(b):
                    for t in range(2):
                        norm_T(xin[(b * 2 + t) % 6], gpre, hb[t], hT[b % 2], t * 128, pbT)

                def main(b, mid=None):
                    hTb = hT[b % 2]
                    for mp in range(16):
                        if mid is not None:
                            nb_ = b + 1
                            if mp == 2:
                                norm_A(xin[(nb_ * 2 + 0) % 6], gpre, hb[0])
                            elif mp == 7:
                                norm_B(hb[0], hT[nb_ % 2], 0, pbT)
                            elif mp == 8:
                                norm_A(xin[(nb_ * 2 + 1) % 6], gpre, hb[1])
                            elif mp == 13:
                                norm_B(hb[1], hT[nb_ % 2], 128, pbT)
                        pfb = pf[mp % 2]
                        for mi in range(2):
                            m = 2 * mp + mi
                            for kc in range(8):
                                S.op("pe", lambda e, m=m, kc=kc, mi=mi, pfb=pfb: e.matmul(
                                    pfb.t[:, mi * 256:(mi + 1) * 256], w1.t[:, kc, m * 128:(m + 1) * 128],
                                    hTb.t[:, kc, :], start=(kc == 0), stop=(kc == 7)),
                                    reads=[w1.b, hTb.b], writes=[pfb.b])
                        r = rt[mp % 2]
                        S.op("act", lambda e, pfb=pfb, r=r: e.activation(out=r.t[:], in_=pfb.t[:], func=AF.Relu),
                             reads=[pfb.b], writes=[r.b])
                        S.op("pool", lambda e, mp=mp, r=r: e.tensor_tensor(
                            out=hid.t[:, 2 * mp:2 * mp + 2, :], in0=r.t[:].rearrange("p (a n) -> p a n", a=2),
                            in1=r.t[:].rearrange("p (a n) -> p a n", a=2), op=ALU.mult),
                            reads=[r.b], writes=[hidB[mp]])
                    for t in range(2):
                        po = pob[(t % 2) * 2:(t % 2) * 2 + 2]
                        for hf in range(2):
                            for kc in range(32):
                                S.op("pe", lambda e, t=t, hf=hf, kc=kc, po=po: e.matmul(
                                    po[hf].t[:], hid.t[:, kc, t * 128:(t + 1) * 128],
                                    w2.t[:, kc, hf * 512:(hf + 1) * 512], start=(kc == 0), stop=(kc == 31)),
                                    reads=[hidB[kc // 2], w2.b], writes=[po[hf].b])
                        xt = xin[(b * 2 + t) % 6]
                        post_norm_store(po, xt, gpost, ot[t], dst, dstB, b * 256 + t * 128)

                loadx(0)
                if NB > 1:
                    loadx(1)
                pre(0)
                for b in range(NB):
                    if b + 2 < NB:
                        loadx(b + 2)
                    main(b, (lambda b=b: pre(b + 1)) if b + 1 < NB else None)

        def attn_kv_phase(src, srcB):
            with ExitStack() as ps:
                def sbl(name, shape, dt=F32):
                    return Tl(ps.enter_context(nc.sbuf_tensor("s_a" + name, list(shape), dt)), "a" + name, S)
                wkv = sbl("wkv", [128, 8, 2048], BF16)
                gpre = sbl("gpre", [128, D])
                load_gain(gpre, gn[(0, "pre_mix")])
                load_w(wkv, "w0in", 0, 8, 1024, 2048)
                xin = [sbl("xin%d" % i, [128, D]) for i in range(8)]
                hb = [sbl("h%d" % i, [128, D], BF16) for i in range(2)]
                hT = [sbl("hT%d" % i, [128, 8, 512], BF16) for i in range(2)]
                ktst = [sbl("ktst%d" % i, [128, 8, 512], BF16) for i in range(2)]
                vst = [sbl("vst%d" % i, [128, 4, D], BF16) for i in range(2)]
                NB = T // 512
                pbT = banks[0]
                pk = banks[1:8]
                pkn = [0]

                def nextbank():
                    pkn[0] += 1
                    return pk[pkn[0] % 7]

                def pre(b):
                    for t in range(4):
                        xt = xin[(b * 4 + t) % 8]
                        r0 = b * 512 + t * 128
                        S.dma("sp", xt.sem, lambda e, xt=xt, r0=r0: e.dma_start(out=xt.t[:], in_=src[r0:r0 + 128, :]),
                              reads=[srcB], writes=[xt.b])
                        norm_T(xt, gpre, hb[t % 2], hT[b % 2], t * 128, pbT)

                def main(b):
                    hTb = hT[b % 2]
                    kt = ktst[b % 2]
                    vt = vst[b % 2]
                    for hd in range(8):
                        bk = nextbank()
                        for kc in range(8):
                            S.op("pe", lambda e, hd=hd, kc=kc, bk=bk: e.matmul(
                                bk.t[:], wkv.t[:, kc, hd * 128:(hd + 1) * 128], hTb.t[:, kc, :],
                                start=(kc == 0), stop=(kc == 7)), reads=[wkv.b, hTb.b], writes=[bk.b])
                        eng = "act" if hd % 2 == 0 else "dve"
                        if eng == "act":
                            S.op("act", lambda e, hd=hd, bk=bk: e.activation(out=kt.t[:, hd, :], in_=bk.t[:], func=AF.Copy),
                                 reads=[bk.b], writes=[kt.b])
                        else:
                            S.op("dve", lambda e, hd=hd, bk=bk: e.tensor_copy(out=kt.t[:, hd, :], in_=bk.t[:]),
                                 reads=[bk.b], writes=[kt.b])
                    S.dma("sp", kt.sem, lambda e, b=b, kt=kt: e.dma_start(
                        out=KTs[:, :, b * 512:(b + 1) * 512].rearrange("h d t -> d h t"), in_=kt.t[:]),
                        reads=[kt.b], writes=[dramB["KTs"]])
                    for t in range(4):
                        for hf in range(2):
                            bk = nextbank()
                            for kc in range(8):
                                S.op("pe", lambda e, t=t, hf=hf, kc=kc, bk=bk: e.matmul(
                                    bk.t[:], hTb.t[:, kc, t * 128:(t + 1) * 128],
                                    wkv.t[:, kc, 1024 + hf * 512:1024 + (hf + 1) * 512],
                                    start=(kc == 0), stop=(kc == 7)), reads=[wkv.b, hTb.b], writes=[bk.b])
                            if hf == 0:
                                S.op("act", lambda e, t=t, hf=hf, bk=bk: e.activation(
                                    out=vt.t[:, t, hf * 512:(hf + 1) * 512], in_=bk.t[:], func=AF.Copy),
                                    reads=[bk.b], writes=[vt.b])
                            else:
                                S.op("dve", lambda e, t=t, hf=hf, bk=bk: e.tensor_copy(
                                    out=vt.t[:, t, hf * 512:(hf + 1) * 512], in_=bk.t[:]),
                                    reads=[bk.b], writes=[vt.b])
                    for t in range(4):
                        S.dma("sp", vt.sem, lambda e, b=b, vt=vt, t=t: e.dma_start(
                            out=Vs[:, :, b * 4 + t, :].rearrange("h p e -> p h e"),
                            in_=vt.t[:, t, :].rearrange("p (h e) -> p h e", h=8)),
                            reads=[vt.b], writes=[dramB["Vs"]])

                pre(0)
                for b in range(NB):
                    if b + 1 < NB:
                        pre(b + 1)
                    main(b)

        def attn_q_phase(src, srcB, dst, dstB):
            with ExitStack() as ps:
                def sbl(name, shape, dt=F32):
                    return Tl(ps.enter_context(nc.sbuf_tensor("s_b" + name, list(shape), dt)), "b" + name, S)
                wq = sbl("wq", [128, 8, D], BF16)
                wo = sbl("wo", [128, 8, D], BF16)
                gpre = sbl("gpre", [128, D])
                gpost = sbl("gpost", [128, D])
                abase = sbl("abase", [128, 5, 512])
                kaug = sbl("kaug", [128, 16, 128], BF16)
                qaug = sbl("qaug", [128, 512], BF16)
                S.dma("sp", kaug.sem, lambda e: e.dma_start(out=kaug.t[:], in_=ka_d), writes=[kaug.b])
                S.dma("sp", qaug.sem, lambda e: e.dma_start(out=qaug.t[:], in_=qa_d), writes=[qaug.b])
                lamt = sbl("lamt", [128, 256])
                lprod = sbl("lprod", [128, 128])
                cst = sbl("cst", [128, 8])
                load_gain(gpre, gn[(0, "pre_mix")])
                load_gain(gpost, gn[(0, "post_mix")])
                S.dma("sp", abase.sem, lambda e: e.dma_start(out=abase.t[:], in_=abase_d), writes=[abase.b])
                S.dma("sp", lamt.sem, lambda e: e.dma_start(
                    out=lamt.t[:], in_=lamv.rearrange("a b -> (a b)").partition_broadcast(128)), writes=[lamt.b])
                S.dma("sp", cst.sem, lambda e: e.dma_start(out=cst.t[:, 5:6], in_=subln.rearrange("(p o) -> p o", o=1)),
                      writes=[cst.b])
                load_w(wq, "w0in", 0, 8, 0, 1024)
                load_w(wo, "w0out", 0, 8, 0, 1024)
                S.op("dve", lambda e: e.tensor_tensor(out=lprod.t[:, 0:64], in0=lamt.t[:, 0:64], in1=lamt.t[:, 64:128],
                                                      op=ALU.mult), reads=[lamt.b], writes=[lprod.b])
                S.op("dve", lambda e: e.tensor_tensor(out=lprod.t[:, 64:128], in0=lamt.t[:, 128:192],
                                                      in1=lamt.t[:, 192:256], op=ALU.mult),
                     reads=[lamt.b], writes=[lprod.b])
                for i in range(2):
                    S.op("act", lambda e, i=i: e.activation(out=junk.t[:, 0:64], in_=lprod.t[:, i * 64:(i + 1) * 64],
                                                            func=AF.Identity, accum_out=cst.t[:, i:i + 1]),
                         reads=[lprod.b], writes=[cst.b])
                S.op("act", lambda e: e.activation(out=cst.t[:, 2:4], in_=cst.t[:, 0:2], func=AF.Exp),
                     reads=[cst.b], writes=[cst.b])
                S.op("dve", lambda e: e.tensor_tensor(out=cst.t[:, 4:5], in0=cst.t[:, 3:4], in1=cst.t[:, 2:3],
                                                      op=ALU.subtract), reads=[cst.b], writes=[cst.b])
                S.op("dve", lambda e: e.tensor_scalar(out=cst.t[:, 4:5], in0=cst.t[:, 4:5], scalar1=-LAM_INIT0,
                                                      scalar2=None, op0=ALU.add), reads=[cst.b], writes=[cst.b])
                S.op("dve", lambda e: e.tensor_scalar(out=cst.t[:, 6:7], in0=cst.t[:, 5:6], scalar1=(1.0 - LAM_INIT0),
                                                      scalar2=None, op0=ALU.mult), reads=[cst.b], writes=[cst.b])
                nlam = cst.t[:, 4:5]
                gsc = cst.t[:, 6:7]

                xin = [sbl("xin%d" % i, [128, D]) for i in range(4)]
                xr = [sbl("xr%d" % i, [128, D]) for i in range(2)]
                hb = [sbl("h%d" % i, [128, D], BF16) for i in range(2)]
                hT = sbl("hT", [128, 8, 512], BF16)
                QT = [sbl("QT%d" % i, [128, 8, 512], BF16) for i in range(2)]
                kh = [sbl("kh%d" % i, [128, max(seqs)], BF16) for i in range(2)]
                vh = [sbl("vh%d" % i, [128, max(seqs) // 128, 128], BF16) for i in range(2)]
                scs = [sbl("scs%d" % i, [128, 2, 512]) for i in range(1)]
                EsA = [sbl("EsA%d" % i, [128, 512]) for i in range(2)]
                onesf = sbl("onesf", [128, 128])
                S.op("dve", lambda e: e.memset(onesf.t[:], 1.0), writes=[onesf.b])
                ex = [sbl("ex%d" % i, [128, 2, 512], BF16) for i in range(4)]
                R = [sbl("R%d" % i, [128, 512]) for i in range(2)]
                t01 = [sbl("t01%d" % i, [128, 512]) for i in range(2)]
                osb = [sbl("osb%d" % i, [128, 512]) for i in range(2)]
                sq = [sbl("sq%d" % i, [128, 512], BF16) for i in range(2)]
                rs = sbl("rs", [128, 512])
                rr = sbl("rr", [128, 512])
                aT = [sbl("aT%d" % i, [128, 8, 512], BF16) for i in range(2)]
                ot = [sbl("ot%d" % i, [128, D]) for i in range(2)]
                scb = [banks[0:2], banks[2:4]]
                Ob = banks[4:6]
                Zb = banks[6:8]
                cnt = {"u": 0, "hd": 0}

                def pre(si, qb):
                    tok0 = seq_off[si] + qb * 512
                    for t in range(4):
                        xt = xin[t]
                        r0 = tok0 + t * 128
                        S.dma("sp", xt.sem, lambda e, xt=xt, r0=r0: e.dma_start(out=xt.t[:], in_=src[r0:r0 + 128, :]),
                              reads=[srcB], writes=[xt.b])
                        norm_T(xt, gpre, hb[t % 2], hT, t * 128, banks[6])
                    qt = QT[cnt["q"] % 2]
                    for hd in range(8):
                        bk = banks[7] if hd % 2 == 0 else banks[6]
                        for kc in range(8):
                            S.op("pe", lambda e, hd=hd, kc=kc, bk=bk: e.matmul(
                                bk.t[:], wq.t[:, kc, hd * 128:(hd + 1) * 128], hT.t[:, kc, :],
                                start=(kc == 0), stop=(kc == 7)), reads=[wq.b, hT.b], writes=[bk.b])
                        if hd % 2 == 0:
                            S.op("act", lambda e, hd=hd, bk=bk: e.activation(out=qt.t[:, hd, :], in_=bk.t[:], func=AF.Copy),
                                 reads=[bk.b], writes=[qt.b])
                        else:
                            S.op("dve", lambda e, hd=hd, bk=bk: e.tensor_copy(out=qt.t[:, hd, :], in_=bk.t[:]),
                                 reads=[bk.b], writes=[qt.b])
                    cnt["q"] += 1
                    return qt

                def issue_load(g):
                    if g >= len(headlist):
                        return
                    si, qb, hd = headlist[g]
                    Sq = seqs[si]
                    nkc = Sq // 128
                    k_t, v_t = kh[g % 2], vh[g % 2]
                    S.dma("sp", k_t.sem, lambda e: e.dma_start(
                        out=k_t.t[:, 0:Sq], in_=KTs[hd, :, seq_off[si]:seq_off[si] + Sq]),
                        reads=[dramB["KTs"]], writes=[k_t.b])
                    S.dma("sp", v_t.sem, lambda e: e.dma_start(
                        out=v_t.t[:, 0:nkc, :], in_=Vs[hd, :, seq_off[si] // 128:seq_off[si] // 128 + nkc, :]),
                        reads=[dramB["Vs"]], writes=[v_t.b])

                def claim_bank():
                    u = cnt["u"]
                    cnt["u"] += 1
                    return scb[u % 2][0]

                def sched_pre(pend, nxt):
                    si2, qb2 = nxt
                    tok0 = seq_off[si2] + qb2 * 512
                    qtn = QT[cnt["q"] % 2]
                    cnt["q"] += 1
                    for t in range(4):
                        def fa(t=t):
                            xt = xin[t]
                            r0 = tok0 + t * 128
                            S.dma("sp", xt.sem, lambda e: e.dma_start(out=xt.t[:], in_=src[r0:r0 + 128, :]),
                                  reads=[srcB], writes=[xt.b])
                            norm_A(xt, gpre, hb[t % 2])
                        def fb(t=t):
                            norm_B(hb[t % 2], hT, t * 128, claim_bank())
                        pend.append((2 + 8 * t, fa))
                        pend.append((7 + 8 * t, fb))
                    for hd in range(8):
                        def fq(hd=hd):
                            bk = claim_bank()
                            for kc in range(8):
                                S.op("pe", lambda e, kc=kc: e.matmul(
                                    bk.t[:], wq.t[:, kc, hd * 128:(hd + 1) * 128], hT.t[:, kc, :],
                                    start=(kc == 0), stop=(kc == 7)), reads=[wq.b, hT.b], writes=[bk.b])
                            S.op("dve", lambda e: e.tensor_copy(out=qtn.t[:, hd, :], in_=bk.t[:]),
                                 reads=[bk.b], writes=[qtn.b])
                        pend.append((36 + 3 * hd, fq))
                    return qtn

                def main(si, qb, qt, at, gbase, nxt=None):
                    Sq = seqs[si]
                    nkc = Sq // 128
                    units = [(hd, kc) for hd in range(8) for kc in range(nkc)]
                    DPIPE = 2
                    ets = {}
                    pend = []
                    qtn = sched_pre(pend, nxt) if nxt is not None else None

                    def stageA(ui):
                        hd, kc = units[ui]
                        g = gbase + hd
                        k_t = kh[g % 2]
                        slope = SLOPES[hd]
                        u = cnt["u"]
                        cnt["u"] += 1
                        sb2 = scb[u % 2]
                        sc = scs[u % len(scs)]
                        et = ex[ui % 4]
                        ets[ui] = et
                        n = qb * 4 - kc
                        if n >= 1:
                            bidx, sgn, cb = 0, -8.0 * slope, -slope * 128.0 * n
                        elif n <= -4:
                            bidx, sgn, cb = 0, 8.0 * slope, slope * 128.0 * n
                        else:
                            bidx, sgn, cb = 1 - n, -8.0 * slope, 0.0
                        offdiag = (n >= 1 or n <= -4)
                        for c in range(2):
                            S.op("pe", lambda e, c=c: e.matmul(
                                sb2[c].t[:], k_t.t[c * 64:(c + 1) * 64, kc * 128:(kc + 1) * 128],
                                qt.t[c * 64:(c + 1) * 64, hd, :], start=True, stop=(not offdiag)),
                                reads=[k_t.b, qt.b], writes=[sb2[c].b])
                        if offdiag:
                            aidx = hd * 2 + (0 if n >= 1 else 1)
                            for c in range(2):
                                S.op("pe", lambda e, c=c: e.matmul(
                                    sb2[c].t[:], kaug.t[c * 64:(c + 1) * 64, aidx, :],
                                    qaug.t[c * 64:(c + 1) * 64, :], start=False, stop=True),
                                    reads=[kaug.b, qaug.b], writes=[sb2[c].b])
                            S.op("act", lambda e: e.activation(
                                out=et.t[:], in_=pairs[u % 2][:], func=AF.Exp, bias=float(cb), scale=0.125),
                                reads=[sb2[0].b, sb2[1].b], writes=[et.b])
                        else:
                            for c in range(2):
                                S.op("dve", lambda e, c=c: e.scalar_tensor_tensor(
                                    out=sc.t[:, c, :], in0=abase.t[:, bidx, :], scalar=sgn, in1=sb2[c].t[:],
                                    op0=ALU.mult, op1=ALU.add), reads=[abase.b, sb2[c].b], writes=[sc.b])
                            S.op("act", lambda e: e.activation(
                                out=et.t[:], in_=sc.t[:], func=AF.Exp, bias=float(cb), scale=0.125),
                                reads=[sc.b], writes=[et.b])

                    def stageB(ui, step):
                        hd, kc = units[ui]
                        g = gbase + hd
                        v_t = vh[g % 2]
                        et = ets.pop(ui)
                        for c in range(2):
                            S.op("pe", lambda e, c=c: e.matmul(
                                Ob[c].t[:], v_t.t[:, kc, :], et.t[:, c, :], start=(kc == 0), stop=(kc == nkc - 1)),
                                reads=[v_t.b, et.b], writes=[Ob[c].b])
                        es0 = EsA[hd % 2]
                        if kc == 0:
                            S.op("dve", lambda e: e.tensor_copy(out=es0.t[:], in_=et.t[:, 0, :]),
                                 reads=[et.b], writes=[es0.b])
                        else:
                            S.op("dve", lambda e: e.tensor_tensor(out=es0.t[:], in0=es0.t[:], in1=et.t[:, 0, :], op=ALU.add),
                                 reads=[es0.b, et.b], writes=[es0.b])
                        S.op("pe", lambda e: e.matmul(
                            Zb[1].t[:], ones.t[:], et.t[:, 1, :], start=(kc == 0), stop=(kc == nkc - 1)),
                            reads=[ones.b, et.b], writes=[Zb[1].b])
                        if kc != nkc - 1:
                            return
                        o_t = osb[hd % 2]
                        sq_t = sq[hd % 2]
                        for c in range(2):
                            S.op("dve", lambda e, c=c: e.tensor_copy(out=t01[c].t[:], in_=Ob[c].t[:]),
                                 reads=[Ob[c].b], writes=[t01[c].b])
                        S.op("dve", lambda e: e.tensor_copy(out=R[1].t[:], in_=Zb[1].t[:]), reads=[Zb[1].b], writes=[R[1].b])
                        issue_load(g + 2)

                        def fin1b():
                            S.op("pe", lambda e: e.matmul(Zb[0].t[:], onesf.t[:], es0.t[:], start=True, stop=True),
                                 reads=[onesf.b, es0.b], writes=[Zb[0].b])
                            S.op("act", lambda e: e.activation(out=R[0].t[:], in_=Zb[0].t[:], func=AF.Ln),
                                 reads=[Zb[0].b], writes=[R[0].b])
                            S.op("act", lambda e: e.activation(out=R[1].t[:], in_=R[1].t[:], func=AF.Ln),
                                 reads=[R[1].b], writes=[R[1].b])
                            for c in range(2):
                                S.op("act", lambda e, c=c: e.activation(out=R[c].t[:], in_=R[c].t[:], func=AF.Exp, scale=-1.0),
                                     reads=[R[c].b], writes=[R[c].b])
                                S.op("dve", lambda e, c=c: e.tensor_tensor(out=t01[c].t[:], in0=t01[c].t[:], in1=R[c].t[:],
                                                                           op=ALU.mult),
                                     reads=[t01[c].b, R[c].b], writes=[t01[c].b])
                            S.op("dve", lambda e: e.scalar_tensor_tensor(out=o_t.t[:], in0=t01[1].t[:], scalar=nlam,
                                                                          in1=t01[0].t[:], op0=ALU.mult, op1=ALU.add),
                                 reads=[t01[0].b, t01[1].b, cst.b], writes=[o_t.b])
                            S.op("act", lambda e: e.activation(out=sq_t.t[:], in_=o_t.t[:], func=AF.Square),
                                 reads=[o_t.b], writes=[sq_t.b])
                        pend.append((step + 2, fin1b))

                        def fin2():
                            u = cnt["u"]
                            cnt["u"] += 1
                            bk = scb[u % 2][0]
                            S.op("pe", lambda e: e.matmul(bk.t[:], ones.t[:], sq_t.t[:], start=True, stop=True),
                                 reads=[ones.b, sq_t.b], writes=[bk.b])
                            S.op("act", lambda e: e.activation(out=rs.t[:], in_=bk.t[:], func=AF.Ln, bias=EPS,
                                                               scale=1.0 / 128), reads=[bk.b], writes=[rs.b])
                            S.op("act", lambda e: e.activation(out=rr.t[:], in_=rs.t[:], func=AF.Exp, scale=-0.5),
                                 reads=[rs.b], writes=[rr.b])
                            S.op("dve", lambda e: e.scalar_tensor_tensor(
                                out=at.t[:, hd, :], in0=o_t.t[:], scalar=gsc, in1=rr.t[:], op0=ALU.mult, op1=ALU.mult),
                                reads=[o_t.b, rr.b, cst.b], writes=[at.b])
                        pend.append((step + 7, fin2))

                    nU = len(units)
                    for step in range(nU + DPIPE):
                        if step < nU:
                            stageA(step)
                        if step - DPIPE >= 0:
                            stageB(step - DPIPE, step)
                        pend.sort(key=lambda x: x[0])
                        while pend and pend[0][0] <= step:
                            pend.pop(0)[1]()
                    pend.sort(key=lambda x: x[0])
                    while pend:
                        pend.pop(0)[1]()
                    return qtn

                def post(si, qb, at):
                    tok0 = seq_off[si] + qb * 512
                    for t in range(4):
                        po = scb[t % 2]
                        xt = xr[t % 2]
                        r0 = tok0 + t * 128
                        S.dma("sp", xt.sem, lambda e, xt=xt, r0=r0: e.dma_start(out=xt.t[:], in_=src[r0:r0 + 128, :]),
                              reads=[srcB], writes=[xt.b])
                        for hf in range(2):
                            for hd in range(8):
                                S.op("pe", lambda e, t=t, hf=hf, hd=hd, po=po: e.matmul(
                                    po[hf].t[:], at.t[:, hd, t * 128:(t + 1) * 128],
                                    wo.t[:, hd, hf * 512:(hf + 1) * 512], start=(hd == 0), stop=(hd == 7)),
                                    reads=[at.b, wo.b], writes=[po[hf].b])
                        post_norm_store(po, xt, gpost, ot[t % 2], dst, dstB, r0)

                cnt["q"] = 0
                blocks = [(si, qb) for si in range(len(seqs)) for qb in range(seqs[si] // 512)]
                headlist = [(si, qb, hd) for (si, qb) in blocks for hd in range(8)]
                issue_load(0)
                issue_load(1)
                qts = {0: pre(*blocks[0])}
                for i, (si, qb) in enumerate(blocks):
                    at = aT[i % 2]
                    qts[i + 1] = main(si, qb, qts[i], at, i * 8, blocks[i + 1] if i + 1 < len(blocks) else None)
                    post(si, qb, at)

        def ret_phases(src, srcB, dst, dstB):
            bkn = [0]

            def nb():
                bkn[0] += 1
                return banks[bkn[0] % 8]

            rtb = sb("rtb", [128, 40])
            ccs = sb("ccs", [128, 2])
            S.dma("sp", rtb.sem, lambda e: e.dma_start(out=rtb.t[:, 0:8], in_=decay.partition_broadcast(128)),
                  writes=[rtb.b])
            S.dma("sp", ccs.sem, lambda e: e.dma_start(out=ccs.t[:], in_=cc_d), writes=[ccs.b])
            S.op("act", lambda e: e.activation(out=rtb.t[:, 8:16], in_=rtb.t[:, 0:8], func=AF.Exp),
                 reads=[rtb.b], writes=[rtb.b])
            S.op("act", lambda e: e.activation(out=rtb.t[:, 16:24], in_=rtb.t[:, 8:16], func=AF.Ln, bias=1.0, scale=-1.0),
                 reads=[rtb.b], writes=[rtb.b])
            S.op("act", lambda e: e.activation(out=rtb.t[:, 24:28], in_=rtb.t[:, 16:20], func=AF.Exp, scale=ccs.t[:, 0:1]),
                 reads=[rtb.b, ccs.b], writes=[rtb.b])
            S.op("act", lambda e: e.activation(out=rtb.t[:, 28:32], in_=rtb.t[:, 20:24], func=AF.Exp, scale=ccs.t[:, 1:2]),
                 reads=[rtb.b, ccs.b], writes=[rtb.b])
            S.op("act", lambda e: e.activation(out=rtb.t[:, 32:40], in_=rtb.t[:, 16:24], func=AF.Exp, scale=128.0),
                 reads=[rtb.b], writes=[rtb.b])
            if "ret_setup_only" in phases:
                return
            lg = lambda i: rtb.t[:, 16 + i:17 + i]
            gC = lambda i: rtb.t[:, 32 + i:33 + i]

            with ExitStack() as ps:
                def sbl(name, shape, dt=F32):
                    return Tl(ps.enter_context(nc.sbuf_tensor("s_r" + name, list(shape), dt)), "r" + name, S)
                wk = sbl("wkv", [128, 8, 3072], BF16)
                wg = sbl("wg", [128, 8, 2048], BF16)
                gpre = sbl("gpre", [128, D])
                load_gain(gpre, gn[(1, "pre_mix")])
                load_w(wk, "w1in", 0, 8, 1024, 3072)
                load_w(wg, "w1in", 0, 8, 4096, 2048)
                Sb = sbl("Sb", [128, 8, 512])
                Sbf = [sbl("Sbf%d" % i, [128, 8, 512], BF16) for i in range(2)]
                xin = [sbl("xin%d" % i, [128, D]) for i in range(3)]
                hb = sbl("h", [128, D], BF16)
                hT = [sbl("hT%d" % i, [128, 8, 128], BF16) for i in range(2)]
                kraw = [sbl("kraw%d" % i, [128, 1024], BF16) for i in range(2)]
                kb = sbl("kb", [128, 1024], BF16)
                vv = [sbl("v%d" % i, [128, 2048], BF16) for i in range(2)]
                sg = [sbl("sg%d" % i, [128, 2048], BF16) for i in range(2)]
                chunks = []
                for si in range(len(seqs)):
                    n_ = seqs[si] // 128
                    for n in range(n_ - 1, -1, -1):
                        chunks.append((si, n, n == n_ - 1))

                def loadx(i):
                    si, n, first = chunks[i]
                    r0 = seq_off[si] + n * 128
                    xt = xin[i % 3]
                    S.dma("sp", xt.sem, lambda e: e.dma_start(out=xt.t[:], in_=src[r0:r0 + 128, :]),
                          reads=[srcB], writes=[xt.b])

                def pre(i):
                    norm_T(xin[i % 3], gpre, hb, hT[i % 2], 0, nb())

                def main(i, mid=None):
                    si, n, first = chunks[i]
                    r0 = seq_off[si] + n * 128
                    cg = r0 // 128
                    hTi = hT[i % 2]
                    kr, v_, sg_ = kraw[i % 2], vv[i % 2], sg[i % 2]
                    for grp in range(10):
                        if mid is not None and grp == 2:
                            norm_A(xin[(i + 1) % 3], gpre, hb)
                        if mid is not None and grp == 7:
                            norm_B(hb, hT[(i + 1) % 2], 0, nb())
                        bk = nb()
                        for kc in range(8):
                            wsrc, gcol = (wk, grp) if grp < 6 else (wg, grp - 6)
                            S.op("pe", lambda e, gcol=gcol, kc=kc, bk=bk, wsrc=wsrc: e.matmul(
                                bk.t[:], hTi.t[:, kc, :], wsrc.t[:, kc, gcol * 512:(gcol + 1) * 512],
                                start=(kc == 0), stop=(kc == 7)), reads=[hTi.b, wsrc.b], writes=[bk.b])
                        if grp < 2:
                            S.op("act", lambda e, grp=grp, bk=bk: e.activation(
                                out=kr.t[:, grp * 512:(grp + 1) * 512], in_=bk.t[:], func=AF.Copy),
                                reads=[bk.b], writes=[kr.b])
                            for hh in range(2):
                                h_ = grp * 2 + hh
                                S.op("act", lambda e, grp=grp, bk=bk, hh=hh, h_=h_: e.activation(
                                    out=kb.t[:, h_ * 256:(h_ + 1) * 256], in_=bk.t[:, hh * 256:(hh + 1) * 256],
                                    func=AF.Copy, scale=rtb.t[:, 28 + h_:29 + h_]),
                                    reads=[bk.b, rtb.b], writes=[kb.b])
                        elif grp < 6:
                            g2 = grp - 2
                            S.op("dve", lambda e, g2=g2, bk=bk: e.tensor_copy(out=v_.t[:, g2 * 512:(g2 + 1) * 512], in_=bk.t[:]),
                                 reads=[bk.b], writes=[v_.b])
                        else:
                            g2 = grp - 6
                            S.op("act", lambda e, g2=g2, bk=bk: e.activation(
                                out=sg_.t[:, g2 * 512:(g2 + 1) * 512], in_=bk.t[:], func=AF.Silu),
                                reads=[bk.b], writes=[sg_.b])
                    S.dma("sp", kr.sem, lambda e: e.dma_start(out=KS[r0:r0 + 128, :], in_=kr.t[:]),
                          reads=[kr.b], writes=[dramB["KS"]])
                    S.dma("sp", v_.sem, lambda e: e.dma_start(out=VS[r0:r0 + 128, :], in_=v_.t[:]),
                          reads=[v_.b], writes=[dramB["VS"]])
                    S.dma("sp", sg_.sem, lambda e: e.dma_start(out=SG[r0:r0 + 128, :], in_=sg_.t[:]),
                          reads=[sg_.b], writes=[dramB["SG"]])
                    if first:
                        S.op("dve", lambda e: e.memset(Sb.t[:].rearrange("p a b -> p (a b)"), 0.0), writes=[Sb.b])
                    sbf = Sbf[i % 2]
                    for q4 in range(4):
                        S.op("pool", lambda e, q4=q4: e.tensor_copy(out=sbf.t[:, 2 * q4:2 * q4 + 2, :],
                                                                    in_=Sb.t[:, 2 * q4:2 * q4 + 2, :]),
                             reads=[Sb.b], writes=[sbf.b])
                    S.dma("sp", sbf.sem, lambda e: e.dma_start(out=SBs[cg], in_=sbf.t[:].rearrange("p a b -> p (a b)")),
                          reads=[sbf.b], writes=[dramB["SBs"]])
                    for h_ in range(4):
                        for dc in range(2):
                            bk = nb()
                            S.op("pe", lambda e, h_=h_, dc=dc, bk=bk: e.matmul(
                                bk.t[:], kb.t[:, h_ * 256 + dc * 128:h_ * 256 + (dc + 1) * 128],
                                v_.t[:, h_ * 512:(h_ + 1) * 512], start=True, stop=True),
                                reads=[kb.b, v_.b], writes=[bk.b])
                            S.op("dve", lambda e, h_=h_, dc=dc, bk=bk: e.scalar_tensor_tensor(
                                out=Sb.t[:, h_ * 2 + dc, :], in0=Sb.t[:, h_ * 2 + dc, :], scalar=gC(4 + h_),
                                in1=bk.t[:], op0=ALU.mult, op1=ALU.add),
                                reads=[Sb.b, bk.b, rtb.b], writes=[Sb.b])

                loadx(0)
                if len(chunks) > 1:
                    loadx(1)
                pre(0)
                for i in range(len(chunks)):
                    if i + 2 < len(chunks):
                        loadx(i + 2)
                    main(i, (lambda i=i: pre(i + 1)) if i + 1 < len(chunks) else None)
            S.barrier()
            if "reta_only" in phases:
                return

            with ExitStack() as ps:
                def sbl(name, shape, dt=F32):
                    return Tl(ps.enter_context(nc.sbuf_tensor("s_q" + name, list(shape), dt)), "q" + name, S)
                wq = sbl("wq", [128, 8, 1024], BF16)
                wo = sbl("wo", [128, 16, 1024], BF16)
                gpre = sbl("gpre", [128, D])
                gpost = sbl("gpost", [128, D])
                GW = sbl("GW", [128, 2048])
                GB = sbl("GB", [128, 2048])
                rc = sbl("rc", [128, 6, 128])
                TfT = sbl("TfT", [128, 8, 128])
                TbT = sbl("TbT", [128, 8, 128])
                MT = sbl("MT", [128, 4, 128])
                mtmp = sbl("mtmp", [128, 128])
                load_gain(gpre, gn[(1, "pre_mix")])
                load_gain(gpost, gn[(1, "post_mix")])
                load_gain(GW, gnw)
                load_gain(GB, gnb)
                S.dma("sp", rc.sem, lambda e: e.dma_start(out=rc.t[:], in_=rc_d), writes=[rc.b])
                load_w(wq, "w1in", 0, 8, 0, 1024)
                load_w(wo, "w1out", 0, 16, 0, 1024)
                for kc in range(8):
                    h_ = kc // 2
                    S.op("act", lambda e, kc=kc, h_=h_: e.activation(out=TfT.t[:, kc, :], in_=rc.t[:, 0, :], func=AF.Exp,
                                                                      scale=lg(h_)), reads=[rc.b, rtb.b], writes=[TfT.b])
                    S.op("act", lambda e, kc=kc, h_=h_: e.activation(out=TbT.t[:, kc, :], in_=rc.t[:, 1, :], func=AF.Exp,
                                                                      scale=lg(4 + h_)), reads=[rc.b, rtb.b], writes=[TbT.b])
                for h_ in range(4):
                    S.op("act", lambda e, h_=h_: e.activation(out=mtmp.t[:], in_=rc.t[:, 2, :], func=AF.Exp, scale=lg(h_)),
                         reads=[rc.b, rtb.b], writes=[mtmp.b])
                    S.op("dve", lambda e, h_=h_: e.tensor_tensor(out=MT.t[:, h_, :], in0=mtmp.t[:], in1=rc.t[:, 3, :],
                                                                  op=ALU.mult), reads=[mtmp.b, rc.b], writes=[MT.b])
                    S.op("act", lambda e, h_=h_: e.activation(out=mtmp.t[:], in_=rc.t[:, 4, :], func=AF.Exp, scale=lg(4 + h_)),
                         reads=[rc.b, rtb.b], writes=[mtmp.b])
                    S.op("dve", lambda e, h_=h_: e.tensor_tensor(out=mtmp.t[:], in0=mtmp.t[:], in1=rc.t[:, 5, :],
                                                                  op=ALU.mult), reads=[mtmp.b, rc.b], writes=[mtmp.b])
                    S.op("dve", lambda e, h_=h_: e.tensor_tensor(out=MT.t[:, h_, :], in0=MT.t[:, h_, :], in1=mtmp.t[:],
                                                                  op=ALU.add), reads=[mtmp.b, MT.b], writes=[MT.b])
                Sf = sbl("Sf", [128, 8, 512])
                Sff = sbl("Sff", [128, 8, 512], BF16)
                Sbn = sbl("Sbn", [128, 8, 512], BF16)
                xin = [sbl("xin%d" % i, [128, D]) for i in range(3)]
                hb = sbl("h", [128, D], BF16)
                hT = sbl("hT", [128, 8, 128], BF16)
                kraw = [sbl("kraw%d" % i, [128, 1024], BF16) for i in range(2)]
                vv = [sbl("v%d" % i, [128, 2048], BF16) for i in range(2)]
                sg = [sbl("sg%d" % i, [128, 2048], BF16) for i in range(3)]
                kf = sbl("kf", [128, 1024], BF16)
                qtok = sbl("qtok", [128, 1024], BF16)
                qT = sbl("qT", [128, 8, 128], BF16)
                qfT = sbl("qfT", [128, 8, 128], BF16)
                qbT = sbl("qbT", [128, 8, 128], BF16)
                kT = sbl("kT", [128, 8, 128], BF16)
                Am = sbl("Am", [128, 4, 128], BF16)
                Us = [sbl("U%d" % i, [128, 2048]) for i in range(2)]
                z = sbl("z", [128, 2048], BF16)
                zT = sbl("zT", [128, 16, 128], BF16)
                ot = [sbl("ot%d" % i, [128, D]) for i in range(2)]
                bst = sbl("bst", [128, 4, 6])
                mv = sbl("mv", [128, 4, 2])
                gst = sbl("gst", [128, 12])
                chunks = [(si, n) for si in range(len(seqs)) for n in range(seqs[si] // 128)]
                b4n = [0]

                def nb():
                    b4n[0] += 1
                    return banks[b4n[0] % 4]
                ybs = {}

                def pre(i):
                    si, n = chunks[i]
                    r0 = seq_off[si] + n * 128
                    xt = xin[i % 3]
                    S.dma("sp", xt.sem, lambda e: e.dma_start(out=xt.t[:], in_=src[r0:r0 + 128, :]),
                          reads=[srcB], writes=[xt.b])
                    kr, v_, sg_ = kraw[i % 2], vv[i % 2], sg[i % 3]
                    S.dma("sp", kr.sem, lambda e: e.dma_start(out=kr.t[:], in_=KS[r0:r0 + 128, :]),
                          reads=[dramB["KS"]], writes=[kr.b])
                    S.dma("sp", v_.sem, lambda e: e.dma_start(out=v_.t[:], in_=VS[r0:r0 + 128, :]),
                          reads=[dramB["VS"]], writes=[v_.b])
                    S.dma("sp", sg_.sem, lambda e: e.dma_start(out=sg_.t[:], in_=SG[r0:r0 + 128, :]),
                          reads=[dramB["SG"]], writes=[sg_.b])

                def main(i):
                    si, n = chunks[i]
                    r0 = seq_off[si] + n * 128
                    cg = r0 // 128
                    xt = xin[i % 3]
                    U = Us[i % 2]
                    kr, v_, sg_ = kraw[i % 2], vv[i % 2], sg[i % 3]
                    norm_B(hb, hT, 0, nb())
                    S.dma("sp", Sbn.sem, lambda e: e.dma_start(out=Sbn.t[:].rearrange("p a b -> p (a b)"), in_=SBs[cg]),
                          reads=[dramB["SBs"]], writes=[Sbn.b])
                    if n == 0:
                        S.op("dve", lambda e: e.memset(Sf.t[:].rearrange("p a b -> p (a b)"), 0.0), writes=[Sf.b])
                        S.op("dve", lambda e: e.memset(Sff.t[:].rearrange("p a b -> p (a b)"), 0.0), writes=[Sff.b])
                    for grp in range(2):
                        bk = nb()
                        for kc in range(8):
                            S.op("pe", lambda e, grp=grp, kc=kc, bk=bk: e.matmul(
                                bk.t[:], hT.t[:, kc, :], wq.t[:, kc, grp * 512:(grp + 1) * 512],
                                start=(kc == 0), stop=(kc == 7)), reads=[hT.b, wq.b], writes=[bk.b])
                        S.op("act", lambda e, grp=grp, bk=bk: e.activation(
                            out=qtok.t[:, grp * 512:(grp + 1) * 512], in_=bk.t[:], func=AF.Copy, scale=1.0 / 16),
                            reads=[bk.b], writes=[qtok.b])
                    bq = nb()
                    pq = bq.t[:].bitcast(BF16)
                    for kc in range(8):
                        S.op("pe", lambda e, kc=kc: e.transpose(pq[:, kc * 128:(kc + 1) * 128],
                                                                 qtok.t[:, kc * 128:(kc + 1) * 128], ident.t[:]),
                             reads=[qtok.b, ident.b], writes=[bq.b])
                    pq3 = pq.rearrange("p (k n) -> p k n", k=8)
                    S.op("act", lambda e: e.activation(out=qT.t[:], in_=pq3, func=AF.Copy), reads=[bq.b], writes=[qT.b])
                    S.op("dve", lambda e: e.tensor_tensor(out=qfT.t[:], in0=qT.t[:], in1=TfT.t[:], op=ALU.mult),
                         reads=[qT.b, TfT.b], writes=[qfT.b])
                    S.op("dve", lambda e: e.tensor_tensor(out=qbT.t[:], in0=qT.t[:], in1=TbT.t[:], op=ALU.mult),
                         reads=[qT.b, TbT.b], writes=[qbT.b])
                    bkk = nb()
                    pk_ = bkk.t[:].bitcast(BF16)
                    for kc in range(8):
                        S.op("pe", lambda e, kc=kc: e.transpose(pk_[:, kc * 128:(kc + 1) * 128],
                                                                 kr.t[:, kc * 128:(kc + 1) * 128], ident.t[:]),
                             reads=[kr.b, ident.b], writes=[bkk.b])
                    S.op("act", lambda e: e.activation(out=kT.t[:], in_=pk_.rearrange("p (k n) -> p k n", k=8), func=AF.Copy),
                         reads=[bkk.b], writes=[kT.b])
                    for h_ in range(4):
                        S.op("act", lambda e, h_=h_: e.activation(
                            out=kf.t[:, h_ * 256:(h_ + 1) * 256], in_=kr.t[:, h_ * 256:(h_ + 1) * 256],
                            func=AF.Copy, scale=rtb.t[:, 24 + h_:25 + h_]),
                            reads=[kr.b, rtb.b], writes=[kf.b])
                    ba = nb()
                    for h_ in range(4):
                        for dc in range(2):
                            S.op("pe", lambda e, h_=h_, dc=dc: e.matmul(
                                ba.t[:, h_ * 128:(h_ + 1) * 128], kT.t[:, h_ * 2 + dc, :], qT.t[:, h_ * 2 + dc, :],
                                start=(dc == 0), stop=(dc == 1)), reads=[kT.b, qT.b], writes=[ba.b])
                    S.op("dve", lambda e: e.tensor_tensor(out=Am.t[:], in0=ba.t[:].rearrange("p (h n) -> p h n", h=4),
                                                          in1=MT.t[:], op=ALU.mult), reads=[ba.b, MT.b], writes=[Am.b])
                    yb = []
                    ybs[i] = yb
                    for h_ in range(4):
                        bk = banks[4 + h_]
                        yb.append(bk)
                        S.op("pe", lambda e, h_=h_, bk=bk: e.matmul(bk.t[:], Am.t[:, h_, :], v_.t[:, h_ * 512:(h_ + 1) * 512],
                                                                   start=True, stop=False),
                             reads=[Am.b, v_.b], writes=[bk.b])
                        for dc in range(2):
                            S.op("pe", lambda e, h_=h_, dc=dc, bk=bk: e.matmul(
                                bk.t[:], qfT.t[:, h_ * 2 + dc, :], Sff.t[:, h_ * 2 + dc, :], start=False, stop=False),
                                reads=[qfT.b, Sff.b], writes=[bk.b])
                        for dc in range(2):
                            S.op("pe", lambda e, h_=h_, dc=dc, bk=bk: e.matmul(
                                bk.t[:], qbT.t[:, h_ * 2 + dc, :], Sbn.t[:, h_ * 2 + dc, :], start=False, stop=(dc == 1)),
                                reads=[qbT.b, Sbn.b], writes=[bk.b])

                def part2(i):
                    si, n = chunks[i]
                    U = Us[i % 2]
                    kr, v_, sg_ = kraw[i % 2], vv[i % 2], sg[i % 3]
                    yb = ybs.pop(i)
                    for h_ in range(4):
                        bk = yb[h_]
                        S.op("dve", lambda e, h_=h_, bk=bk: e.bn_stats(out=bst.t[:, h_, :], in_=bk.t[:]),
                             reads=[bk.b], writes=[bst.b])
                        S.op("dve", lambda e, h_=h_: e.bn_aggr(out=mv.t[:, h_, :], in_=bst.t[:, h_, :]),
                             reads=[bst.b], writes=[mv.b])
                    S.op("act", lambda e: e.activation(out=gst.t[:, 0:4], in_=mv.t[:, :, 1], func=AF.Sqrt, bias=EPS, scale=1.0),
                         reads=[mv.b], writes=[gst.b])
                    S.op("dve", lambda e: e.reciprocal(out=gst.t[:, 4:8], in_=gst.t[:, 0:4]), reads=[gst.b], writes=[gst.b])
                    S.op("dve", lambda e: e.scalar_tensor_tensor(out=gst.t[:, 8:12], in0=mv.t[:, :, 0], scalar=-1.0,
                                                                  in1=gst.t[:, 4:8], op0=ALU.mult, op1=ALU.mult),
                         reads=[gst.b, mv.b], writes=[gst.b])
                    for h_ in range(4):
                        S.op("act", lambda e, h_=h_: e.activation(
                            out=U.t[:, h_ * 512:(h_ + 1) * 512], in_=yb[h_].t[:], func=AF.Identity,
                            bias=gst.t[:, 8 + h_:9 + h_], scale=gst.t[:, 4 + h_:5 + h_]),
                            reads=[yb[h_].b, gst.b], writes=[U.b])
                    for h_ in range(4):
                        for dc in range(2):
                            bk = nb()
                            S.op("pe", lambda e, h_=h_, dc=dc, bk=bk: e.matmul(
                                bk.t[:], kf.t[:, h_ * 256 + dc * 128:h_ * 256 + (dc + 1) * 128],
                                v_.t[:, h_ * 512:(h_ + 1) * 512], start=True, stop=True),
                                reads=[kf.b, v_.b], writes=[bk.b])
                            S.op("dve", lambda e, h_=h_, dc=dc, bk=bk: e.scalar_tensor_tensor(
                                out=Sf.t[:, h_ * 2 + dc, :], in0=Sf.t[:, h_ * 2 + dc, :], scalar=gC(h_),
                                in1=bk.t[:], op0=ALU.mult, op1=ALU.add),
                                reads=[Sf.b, bk.b, rtb.b], writes=[Sf.b])
                    S.op("act", lambda e: e.activation(out=Sff.t[:].rearrange("p a b -> p (a b)"),
                                                       in_=Sf.t[:].rearrange("p a b -> p (a b)"), func=AF.Copy),
                         reads=[Sf.b], writes=[Sff.b])
                def mainB(i):
                    si, n = chunks[i]
                    r0 = seq_off[si] + n * 128
                    xt = xin[i % 3]
                    sg_ = sg[i % 3]
                    U = Us[i % 2]
                    for hf in range(2):
                        sl = slice(hf * 1024, (hf + 1) * 1024)
                        S.op("dve", lambda e, sl=sl: e.tensor_tensor(out=U.t[:, sl], in0=U.t[:, sl], in1=GW.t[:, sl], op=ALU.mult),
                             reads=[U.b, GW.b], writes=[U.b])
                        S.op("dve", lambda e, sl=sl: e.tensor_tensor(out=U.t[:, sl], in0=U.t[:, sl], in1=GB.t[:, sl], op=ALU.add),
                             reads=[U.b, GB.b], writes=[U.b])
                    S.op("dve", lambda e: e.tensor_tensor(out=z.t[:], in0=U.t[:], in1=sg_.t[:], op=ALU.mult),
                         reads=[U.b, sg_.b], writes=[z.b])

                def mainB_pe(i):
                    si, n = chunks[i]
                    r0 = seq_off[si] + n * 128
                    xt = xin[i % 3]
                    for half in range(2):
                        bz = nb()
                        pz = bz.t[:].bitcast(BF16)
                        for kc in range(8):
                            S.op("pe", lambda e, kc=kc, half=half, pz=pz, bz=bz: e.transpose(
                                pz[:, kc * 128:(kc + 1) * 128],
                                z.t[:, (half * 8 + kc) * 128:(half * 8 + kc + 1) * 128], ident.t[:]),
                                reads=[z.b, ident.b], writes=[bz.b])
                        S.op("act", lambda e, half=half, pz=pz, bz=bz: e.activation(
                            out=zT.t[:, half * 8:(half + 1) * 8, :], in_=pz.rearrange("p (k n) -> p k n", k=8), func=AF.Copy),
                            reads=[bz.b], writes=[zT.b])
                    po = [nb(), nb()]
                    for hf in range(2):
                        for kc in range(16):
                            S.op("pe", lambda e, hf=hf, kc=kc: e.matmul(
                                po[hf].t[:], zT.t[:, kc, :], wo.t[:, kc, hf * 512:(hf + 1) * 512],
                                start=(kc == 0), stop=(kc == 15)), reads=[zT.b, wo.b], writes=[po[hf].b])
                    post_norm_store(po, xt, gpost, ot[i % 2], dst, dstB, r0)

                pre(0)
                if len(chunks) > 1:
                    pre(1)
                norm_A(xin[0], gpre, hb)
                main(0)
                if len(chunks) > 1:
                    norm_A(xin[1], gpre, hb)
                part2(0)
                for i in range(len(chunks)):
                    if i + 2 < len(chunks):
                        pre(i + 2)
                    mainB(i)
                    if i + 1 < len(chunks):
                        main(i + 1)
                    if i + 2 < len(chunks):
                        norm_A(xin[(i + 2) % 3], gpre, hb)
                    mainB_pe(i)
                    if i + 1 < len(chunks):
                        part2(i + 1)

        cur, curB = x_in, Buf("x")
        S.barrier()
        if "attnkv" in phases:
            attn_kv_phase(cur, curB)
        if "attn" in phases:
            attn_kv_phase(cur, curB)
            S.barrier()
            nxt = XA if len(phases) > 2 else y_out
            nxtB = dramB["XA"] if nxt is XA else dramB["y"]
            attn_q_phase(cur, curB, nxt, nxtB)
            S.barrier()
            cur, curB = nxt, nxtB
        if "ffn0" in phases:
            nxt = XB if ("ret" in phases or "ffn1" in phases) else y_out
            nxtB = dramB["XB"] if nxt is XB else dramB["y"]
            ffn_phase("f0", cur, curB, nxt, nxtB, "w0f1", "w0f2", gn[(0, "pre_ffn")], gn[(0, "post_ffn")])
            S.barrier()
            cur, curB = nxt, nxtB
        if "ret" in phases:
            nxt = XA if "ffn1" in phases else y_out
            nxtB = dramB["XA"] if nxt is XA else dramB["y"]
            ret_phases(cur, curB, nxt, nxtB)
            S.barrier()
            cur, curB = nxt, nxtB
        if "ffn1" in phases:
            ffn_phase("f1", cur, curB, y_out, dramB["y"], "w1f1", "w1f2", gn[(1, "pre_ffn")], gn[(1, "post_ffn")])

        S.barrier()
        scratch = {"KTs": KTs, "Vs": Vs, "KS": KS, "VS": VS, "SG": SG, "SBs": SBs}
        for nm in dbg:
            if nm.startswith("dump_"):
                dsem = S.new_dma_sem()
                S.dma("sp", dsem, lambda e, nm=nm: e.dma_start(out=dbg[nm], in_=scratch[nm[5:]]), writes=[Buf()])
        S.final_wait("sp")
        sems = {}
        for k in list(S.ENGS) + list(S.dma_cnt.keys()):
            sems[k] = es.enter_context(nc.semaphore(k))
        block = es.enter_context(nc.Block())
        S.emit(nc, block, sems)
    return nc


def host_consts():
    c = {}
    c["ident"] = np.eye(128, dtype=np.float32).astype(ml_dtypes.bfloat16)
    c["ones"] = np.ones((128, 128), dtype=np.float32).astype(ml_dtypes.bfloat16)
    kj = np.arange(128, dtype=np.float32)[:, None]
    qi = np.arange(512, dtype=np.float32)[None, :]
    ab = np.zeros((128, 5, 512), np.float32)
    ab[:, 0, :] = qi - kj
    for j in range(4):
        ab[:, 1 + j, :] = np.abs(qi - kj - 128.0 * j)
    c["abase"] = ab
    i = np.arange(128, dtype=np.float32)
    rc = np.zeros((128, 6, 128), np.float32)
    rc[:, 0, :] = (i + 1.0)[None, :]
    rc[:, 1, :] = (128.0 - i)[None, :]
    dif = i[None, :] - i[:, None]
    rc[:, 2, :] = np.maximum(dif, 0.0)
    rc[:, 3, :] = (dif >= 0).astype(np.float32)
    rc[:, 4, :] = np.maximum(-dif, 0.0)
    rc[:, 5, :] = (dif <= 0).astype(np.float32)
    c["rc"] = rc
    cc = np.zeros((128, 2), np.float32)
    cc[:, 0] = 127.0 - i
    cc[:, 1] = i
    c["cc"] = cc
    ka = np.zeros((128, 16, 128), np.float32)
    qa = np.zeros((128, 512), np.float32)
    kjv = np.arange(128, dtype=np.float32)
    qiv = np.arange(512, dtype=np.float32)
    for base in (0, 64):
        qa[base + 0] = 1.0
        qa[base + 1] = 2.0 * np.floor(qiv / 2.0)
        qa[base + 2] = qiv - 2.0 * np.floor(qiv / 2.0)
        for h in range(8):
            for side, sg_ in ((0, 1.0), (1, -1.0)):
                ka[base + 0, h * 2 + side] = sg_ * 8.0 * SLOPES[h] * kjv
                ka[base + 1, h * 2 + side] = -sg_ * 8.0 * SLOPES[h]
                ka[base + 2, h * 2 + side] = -sg_ * 8.0 * SLOPES[h]
    c["ka"] = ka.astype(ml_dtypes.bfloat16)
    c["qa"] = qa.astype(ml_dtypes.bfloat16)
    cbt = np.zeros((128, 8, 32), np.float32)
    for h in range(8):
        cbt[:, h, :] = -SLOPES[h] * 128.0 * np.arange(32, dtype=np.float32)[None, :]
    c["cbt"] = cbt
    return c


def make_in_maps(inputs, x_cores):
    f = lambda a: np.ascontiguousarray(np.asarray(a, dtype=np.float32))
    base = {
        "w0in": f(inputs["l0_da_w_in"]), "w0out": f(inputs["l0_da_w_out"]),
        "w0f1": f(inputs["l0_ffn_w1"]), "w0f2": f(inputs["l0_ffn_w2"]),
        "w1in": f(inputs["l1_ret_w_in"]), "w1out": f(inputs["l1_ret_w_out"]),
        "w1f1": f(inputs["l1_ffn_w1"]), "w1f2": f(inputs["l1_ffn_w2"]),
        "lamv": np.stack([f(inputs["l0_da_lambda_q1"]), f(inputs["l0_da_lambda_k1"]),
                          f(inputs["l0_da_lambda_q2"]), f(inputs["l0_da_lambda_k2"])]),
        "subln": f(inputs["l0_da_subln"]),
        "decay": np.concatenate([f(inputs["l1_ret_decay_fwd"]), f(inputs["l1_ret_decay_bwd"])]),
        "gnw": f(inputs["l1_ret_gn_w"]), "gnb": f(inputs["l1_ret_gn_b"]),
    }
    base["g0_pre_mix"] = f(inputs["l0_norm_pre_mix"])
    base["g0_post_mix"] = f(inputs["l0_norm_post_mix"])
    base["g0_pre_ffn"] = f(inputs["l0_norm_pre_ffn"])
    base["g0_post_ffn"] = f(inputs["l0_norm_post_ffn"])
    base["g1_pre_mix"] = f(inputs["l1_norm_pre_mix"])
    base["g1_post_mix"] = f(inputs["l1_norm_post_mix"])
    base["g1_pre_ffn"] = f(inputs["l1_norm_pre_ffn"])
    base["g1_post_ffn"] = f(inputs["l1_norm_post_ffn"])
    base.update(host_consts())
    maps = []
    for xc in x_cores:
        m = dict(base)
        m["x"] = xc
        maps.append(m)
    return maps


_PROG = {}


def kernel(**inputs):
    xp = np.asarray(inputs["x_prompt"], dtype=np.float32)
    xs = np.asarray(inputs["x_sample"], dtype=np.float32)
    seqs = [2048] * 4 + [4096] * 2
    x_cores = []
    for c in range(NCORES):
        x_cores.append(np.ascontiguousarray(np.concatenate(
            [xp[4 * c:4 * c + 4].reshape(-1, D), xs[2 * c:2 * c + 2].reshape(-1, D)], axis=0)))
    key = tuple(seqs)
    if key not in _PROG:
        _PROG[key] = build_program(seqs)
    nc = _PROG[key]
    res = run_bass_kernel_spmd(nc, make_in_maps(inputs, x_cores), core_ids=list(range(NCORES)))
    yp = np.empty_like(xp)
    ys = np.empty_like(xs)
    for c in range(NCORES):
        y = np.asarray(res.results[c]["y"], dtype=np.float32)
        yp[4 * c:4 * c + 4] = y[:8192].reshape(4, 2048, D)
        ys[2 * c:2 * c + 2] = y[8192:].reshape(2, 4096, D)
    return (yp, ys)
```

```python
import math
from contextlib import ExitStack

import numpy as np
import ml_dtypes

import concourse.bass as bass
import concourse.mybir as mybir
from concourse.bass_utils import run_bass_kernel_spmd

F32 = mybir.dt.float32
BF16 = mybir.dt.bfloat16
AF = mybir.ActivationFunctionType
ALU = mybir.AluOpType

D = 1024
DFF = 4096
EPS = 1e-6
NCORES = 8
LAM_INIT0 = 0.8 - 0.6 * math.exp(-0.3 * 0)
SLOPES = [2.0 ** (-(h + 1)) for h in range(8)]


class Buf:
    __slots__ = ("name", "w", "r", "excl")

    def __init__(self, name="", excl=False):
        self.name = name
        self.w = None
        self.r = {}
        self.excl = excl


class Sched:
    ENGS = ("pe", "act", "dve", "pool", "sp")

    def __init__(self):
        self.ops = {e: [] for e in self.ENGS}
        self.cnt = {e: 0 for e in self.ENGS}
        self.seen = {e: {} for e in self.ENGS}
        self.dma_cnt = {}
        self.n_dma_sems = 0
        self.free_sems = []
        self.cur_sems = []

    def new_dma_sem(self):
        if self.free_sems:
            k = self.free_sems.pop()
        else:
            k = "dma%d" % self.n_dma_sems
            self.n_dma_sems += 1
            self.dma_cnt[k] = 0
        self.cur_sems.append(k)
        return k

    def _deps(self, eng, reads, writes):
        deps = {}

        def add(tok):
            k, v, src = tok
            if src == eng and eng == "pe":
                return
            if v > deps.get(k, 0):
                deps[k] = v
        for b in reads:
            if b.w is not None:
                add(b.w)
            if b.excl:
                for k, (v, src) in b.r.items():
                    if src != eng:
                        add((k, v, src))
        for b in writes:
            if b.w is not None:
                add(b.w)
            for k, (v, src) in b.r.items():
                if src == eng:
                    continue
                add((k, v, src))
        waits = []
        seen = self.seen[eng]
        for k, v in deps.items():
            if seen.get(k, 0) >= v:
                continue
            seen[k] = v
            waits.append((k, v))
        return waits

    def op(self, eng, fn, reads=(), writes=()):
        waits = self._deps(eng, reads, writes)
        self.cnt[eng] += 1
        tok = (eng, self.cnt[eng], eng)
        self.ops[eng].append((fn, waits, (eng, 1)))
        for b in reads:
            b.r[eng] = (tok[1], eng)
        for b in writes:
            b.w = tok
            b.r = {}
        return tok

    def dma(self, eng, sem_key, fn, reads=(), writes=()):
        waits = self._deps(eng, reads, writes)
        self.dma_cnt[sem_key] += 16
        tok = (sem_key, self.dma_cnt[sem_key], "dma")
        self.ops[eng].append((fn, waits, (sem_key, 16)))
        for b in reads:
            b.r[sem_key] = (tok[1], "dma")
        for b in writes:
            b.w = tok
            b.r = {}
        return tok

    def barrier(self):
        toks = [(e, self.cnt[e]) for e in self.ENGS if self.cnt[e] > 0]
        toks += [(k, v) for k, v in self.dma_cnt.items() if v > 0]
        for e in self.ENGS:
            waits = []
            for k, v in toks:
                if k == e:
                    continue
                if self.seen[e].get(k, 0) >= v:
                    continue
                self.seen[e][k] = v
                waits.append((k, v))
            if waits:
                self.ops[e].append((None, waits, None))
        self.free_sems.extend(self.cur_sems)
        self.cur_sems = []

    def final_wait(self, eng):
        waits = [(k, v) for k, v in self.dma_cnt.items() if v > 0]
        self.ops[eng].append((None, waits, None))

    def emit(self, nc, block, sems):
        handles = {"pe": "tensor", "act": "scalar", "dve": "vector", "pool": "gpsimd", "sp": "sync"}
        for en in self.ENGS:
            ops = self.ops[en]
            if not ops:
                continue

            def body(e, ops=ops):
                for fn, waits, inc in ops:
                    for k, v in waits:
                        e.wait_ge(sems[k], v)
                    if fn is not None:
                        ins = fn(e)
                        if inc is not None:
                            ins.then_inc(sems[inc[0]], inc[1])
            getattr(block, handles[en])(body)


class Tl:
    def __init__(self, t, name, S):
        self.t = t
        self.b = Buf(name)
        self.S = S
        self._sem = None

    @property
    def sem(self):
        if self._sem is None:
            self._sem = self.S.new_dma_sem()
        return self._sem


def build_program(seqs, phases=("cast", "attn", "ffn0", "ret", "ffn1"), debug_out=()):
    T = sum(seqs)
    seq_off = [sum(seqs[:i]) for i in range(len(seqs))]
    nc = bass.Bass("TRN2", target_bir_lowering=False)
    S = Sched()

    def din(name, shape, dt=F32):
        return nc.dram_tensor(name, list(shape), dt, kind="ExternalInput").ap()

    def dint(name, shape, dt):
        return nc.dram_tensor(name, list(shape), dt, kind="Internal").ap()

    x_in = din("x", [T, D])
    y_out = nc.dram_tensor("y", [T, D], F32, kind="ExternalOutput").ap()
    wnames = {"w0in": (D, 3072), "w0out": (D, D), "w0f1": (D, DFF), "w0f2": (DFF, D),
              "w1in": (D, 6144), "w1out": (2048, D), "w1f1": (D, DFF), "w1f2": (DFF, D)}
    wf = {k: din(k, v) for k, v in wnames.items()}
    wb = {k: dint(k + "_b", v, BF16) for k, v in wnames.items()}
    gn = {}
    for l in (0, 1):
        for nm in ("pre_mix", "post_mix", "pre_ffn", "post_ffn"):
            gn[(l, nm)] = din("g%d_%s" % (l, nm), [D])
    lamv = din("lamv", [4, 64])
    subln = din("subln", [128])
    decay = din("decay", [8])
    gnw = din("gnw", [2048])
    gnb = din("gnb", [2048])
    ident_d = din("ident", [128, 128], BF16)
    ones_d = din("ones", [128, 128], BF16)
    abase_d = din("abase", [128, 5, 512])
    rc_d = din("rc", [128, 6, 128])
    cc_d = din("cc", [128, 2])
    cbt_d = din("cbt", [128, 8, 32])
    ka_d = din("ka", [128, 16, 128], BF16)
    qa_d = din("qa", [128, 512], BF16)

    XA = dint("XA", [T, D], F32)
    XB = dint("XB", [T, D], F32)
    KTs = dint("KTs", [8, 128, T], BF16)
    Vs = dint("Vs", [8, 128, T // 128, 128], BF16)
    KS = dint("KS", [T, 1024], BF16)
    VS = dint("VS", [T, 2048], BF16)
    SG = dint("SG", [T, 2048], BF16)
    SBs = dint("SBs", [T // 128, 128, 4096], BF16)
    dbg = {}
    for nm, shape in debug_out:
        dbg[nm] = nc.dram_tensor(nm, list(shape), BF16 if nm.startswith("dump_") else F32, kind="ExternalOutput").ap()

    dramB = {k: Buf(k) for k in ["XA", "XB", "KTs", "Vs", "KS", "VS", "SG", "SBs", "y"] + list(wb)}

    es = ExitStack()
    with es:
        def sb(name, shape, dt=F32):
            return Tl(es.enter_context(nc.sbuf_tensor("s_" + name, list(shape), dt)), name, S)

        ident = sb("ident", [128, 128], BF16)
        ones = sb("ones", [128, 128], BF16)
        pairs = [es.enter_context(nc.psum_tensor("pair%d" % i, [128, 2, 512], F32)) for i in range(2)]
        banks = [Tl(pairs[i // 2][:, i % 2, :], "bank%d" % i, S) for i in range(4)]
        banks += [Tl(es.enter_context(nc.psum_tensor("bank%d" % i, [128, 512], F32)), "bank%d" % i, S)
                  for i in range(4, 8)]
        for bk_ in banks:
            bk_.b.excl = True
        junk = sb("junk", [128, 1024], BF16)
        st = sb("st", [128, 128], F32)
        stB = [Buf("st%d" % i) for i in range(64)]
        stn = [0]

        def tiny(n=1):
            i = stn[0] % 64
            stn[0] += 1
            return st.t[:, 2 * i:2 * i + n], stB[i]

        cq = [0]
        dmaq = ("sp", "act", "pool")

        def ldq():
            cq[0] += 1
            return "sp"

        S.dma("sp", ident.sem, lambda e: e.dma_start(out=ident.t[:], in_=ident_d), writes=[ident.b])
        S.dma("sp", ones.sem, lambda e: e.dma_start(out=ones.t[:], in_=ones_d), writes=[ones.b])

        def cast_weights(keys):
            csem = S.new_dma_sem()
            ncast = 0
            for k in keys:
                r, c = wnames[k]
                step = 128 * max(1, (1024 * 1024) // (c * 128))
                for r0 in range(0, r, step):
                    r1 = min(r, r0 + step)
                    S.dma("pool", csem,
                          lambda e, k=k, r0=r0, r1=r1: e.dma_start(
                              out=wb[k][r0:r1, :].rearrange("r (a c) -> r a c", c=1024),
                              in_=wf[k][r0:r1, :].rearrange("r (a c) -> r a c", c=1024)),
                          writes=[dramB[k]])
                    ncast += 1
                    if ncast % 8 == 0:
                        S.ops["pool"].append((None, [(csem, S.dma_cnt[csem])], None))
            for k in keys:
                dramB[k].w = (csem, S.dma_cnt[csem], "dma")

        if "cast" in phases:
            cast_weights(["w0in", "w0out"])

        def load_w(dst, dname, r0, nkc, c0, ncols, col_off=0):
            src = wb[dname][r0:r0 + nkc * 128, c0:c0 + ncols].rearrange("(k p) n -> p k n", p=128)
            step = max(1, nkc // 4)
            for k0 in range(0, nkc, step):
                k1 = min(nkc, k0 + step)
                S.dma("sp", dst.sem,
                      lambda e, k0=k0, k1=k1: e.dma_start(out=dst.t[:, k0:k1, col_off:col_off + ncols],
                                                          in_=src[:, k0:k1, :]),
                      reads=[dramB[dname]], writes=[dst.b])

        def load_gain(dst, vec):
            S.dma("sp", dst.sem, lambda e: e.dma_start(out=dst.t[:], in_=vec.partition_broadcast(128)),
                  writes=[dst.b])

        def rstd_from_ss(ss_ap, ss_b, scale):
            rs, rs_b = tiny(1)
            S.op("act", lambda e: e.activation(out=rs, in_=ss_ap, func=AF.Sqrt, bias=EPS, scale=scale),
                 reads=[ss_b], writes=[rs_b])
            rr, rr_b = tiny(1)
            S.op("dve", lambda e: e.reciprocal(out=rr, in_=rs), reads=[rs_b], writes=[rr_b])
            return rr, rr_b

        def norm_T(xt, gt, h, hT, tcol, pbank):
            norm_A(xt, gt, h)
            norm_B(h, hT, tcol, pbank)

        def norm_A(xt, gt, h):
            ss, ss_b = tiny(1)
            S.op("act", lambda e: e.activation(out=junk.t[:], in_=xt.t[:], func=AF.Square, accum_out=ss),
                 reads=[xt.b], writes=[ss_b])
            rr, rr_b = rstd_from_ss(ss, ss_b, 1.0 / D)
            S.op("dve", lambda e: e.scalar_tensor_tensor(out=h.t[:], in0=xt.t[:], scalar=rr, in1=gt.t[:],
                                                          op0=ALU.mult, op1=ALU.mult),
                 reads=[xt.b, rr_b, gt.b], writes=[h.b])

        def norm_B(h, hT, tcol, pbank):
            pT = pbank.t[:].bitcast(BF16)
            for kc in range(8):
                S.op("pe", lambda e, kc=kc: e.transpose(pT[:, kc * 128:(kc + 1) * 128],
                                                         h.t[:, kc * 128:(kc + 1) * 128], ident.t[:]),
                     reads=[h.b, ident.b], writes=[pbank.b])
            S.op("act", lambda e: e.activation(out=hT.t[:, 0:8, tcol:tcol + 128],
                                               in_=pT.rearrange("p (k n) -> p k n", k=8), func=AF.Copy),
                 reads=[pbank.b], writes=[hT.b])

        def post_norm_store(po, xt, gt, ot, dst, dstB, row0):
            ss, ss_b = tiny(2)
            for hf in range(2):
                S.op("act", lambda e, hf=hf: e.activation(out=junk.t[:, 0:512], in_=po[hf].t[:], func=AF.Square,
                                                           accum_out=ss[:, hf:hf + 1]),
                     reads=[po[hf].b], writes=[ss_b])
            s1, s1_b = tiny(1)
            S.op("dve", lambda e: e.tensor_tensor(out=s1, in0=ss[:, 0:1], in1=ss[:, 1:2], op=ALU.add),
                 reads=[ss_b], writes=[s1_b])
            rr, rr_b = rstd_from_ss(s1, s1_b, 1.0 / D)
            for hf in range(2):
                S.op("dve", lambda e, hf=hf: e.scalar_tensor_tensor(
                    out=ot.t[:, hf * 512:(hf + 1) * 512], in0=po[hf].t[:], scalar=rr,
                    in1=gt.t[:, hf * 512:(hf + 1) * 512], op0=ALU.mult, op1=ALU.mult),
                    reads=[po[hf].b, rr_b, gt.b], writes=[ot.b])
            S.op("pool", lambda e: e.tensor_tensor(out=ot.t[:], in0=ot.t[:], in1=xt.t[:], op=ALU.add),
                 reads=[ot.b, xt.b], writes=[ot.b])
            S.dma("sp", ot.sem, lambda e: e.dma_start(out=dst[row0:row0 + 128, :], in_=ot.t[:]),
                  reads=[ot.b], writes=[dstB])

        def ffn_phase(pfx, src, srcB, dst, dstB, w1n, w2n, gpre_v, gpost_v):
            with ExitStack() as ps:
                def sbl(name, shape, dt=F32):
                    return Tl(ps.enter_context(nc.sbuf_tensor("s_" + pfx + name, list(shape), dt)), pfx + name, S)
                w1 = sbl("w1", [128, 8, DFF], BF16)
                w2 = sbl("w2", [128, 32, D], BF16)
                gpre = sbl("gpre", [128, D])
                gpost = sbl("gpost", [128, D])
                load_gain(gpre, gpre_v)
                load_gain(gpost, gpost_v)
                load_w(w1, w1n, 0, 8, 0, DFF)
                load_w(w2, w2n, 0, 32, 0, D)
                xin = [sbl("xin%d" % i, [128, D]) for i in range(6)]
                hb = [sbl("h%d" % i, [128, D], BF16) for i in range(2)]
                hT = [sbl("hT%d" % i, [128, 8, 256], BF16) for i in range(2)]
                hid = sbl("hid", [128, 32, 256], BF16)
                hidB = [Buf("hid%d" % i) for i in range(16)]
                rt = [sbl("rt%d" % i, [128, 512], BF16) for i in range(2)]
                ot = [sbl("ot%d" % i, [128, D]) for i in range(2)]
                NB = T // 256
                pbT, pf, pob = banks[0], banks[1:3], banks[3:7]

                def loadx(b):
                    for t in range(2):
                        xt = xin[(b * 2 + t) % 6]
                        r0 = b * 256 + t * 128
                        S.dma("sp", xt.sem, lambda e, xt=xt, r0=r0: e.dma_start(out=xt.t[:], in_=src[r0:r0 + 128, :]),
                              reads=[srcB], writes=[xt.b])

                def pre(b):
                    for t in range(2):
                        norm_T(xin[(b * 2 + t) % 6], gpre, hb[t], hT[b % 2], t * 128, pbT)

                def main(b, mid=None):
                    hTb = hT[b % 2]
                    for mp in range(16):
                        if mid is not None:
                            nb_ = b + 1
                            if mp == 2:
                                norm_A(xin[(nb_ * 2 + 0) % 6], gpre, hb[0])
                            elif mp == 7:
                                norm_B(hb[0], hT[nb_ % 2], 0, pbT)
                            elif mp == 8:
                                norm_A(xin[(nb_ * 2 + 1) % 6], gpre, hb[1])
                            elif mp == 13:
                                norm_B(hb[1], hT[nb_ % 2], 128, pbT)
                        pfb = pf[mp % 2]
                        for mi in range(2):
                            m = 2 * mp + mi
                            for kc in range(8):
                                S.op("pe", lambda e, m=m, kc=kc, mi=mi, pfb=pfb: e.matmul(
                                    pfb.t[:, mi * 256:(mi + 1) * 256], w1.t[:, kc, m * 128:(m + 1) * 128],
                                    hTb.t[:, kc, :], start=(kc == 0), stop=(kc == 7)),
                                    reads=[w1.b, hTb.b], writes=[pfb.b])
                        r = rt[mp % 2]
                        S.op("act", lambda e, pfb=pfb, r=r: e.activation(out=r.t[:], in_=pfb.t[:], func=AF.Relu),
                             reads=[pfb.b], writes=[r.b])
                        S.op("pool", lambda e, mp=mp, r=r: e.tensor_tensor(
                            out=hid.t[:, 2 * mp:2 * mp + 2, :], in0=r.t[:].rearrange("p (a n) -> p a n", a=2),
                            in1=r.t[:].rearrange("p (a n) -> p a n", a=2), op=ALU.mult),
                            reads=[r.b], writes=[hidB[mp]])
                    for t in range(2):
                        po = pob[(t % 2) * 2:(t % 2) * 2 + 2]
                        for hf in range(2):
                            for kc in range(32):
                                S.op("pe", lambda e, t=t, hf=hf, kc=kc, po=po: e.matmul(
                                    po[hf].t[:], hid.t[:, kc, t * 128:(t + 1) * 128],
                                    w2.t[:, kc, hf * 512:(hf + 1) * 512], start=(kc == 0), stop=(kc == 31)),
                                    reads=[hidB[kc // 2], w2.b], writes=[po[hf].b])
                        xt = xin[(b * 2 + t) % 6]
                        post_norm_store(po, xt, gpost, ot[t], dst, dstB, b * 256 + t * 128)

                loadx(0)
                if NB > 1:
                    loadx(1)
                pre(0)
                for b in range(NB):
                    if b + 2 < NB:
                        loadx(b + 2)
                    main(b, (lambda b=b: pre(b + 1)) if b + 1 < NB else None)

        def attn_kv_phase(src, srcB):
            with ExitStack() as ps:
                def sbl(name, shape, dt=F32):
                    return Tl(ps.enter_context(nc.sbuf_tensor("s_a" + name, list(shape), dt)), "a" + name, S)
                wkv = sbl("wkv", [128, 8, 2048], BF16)
                gpre = sbl("gpre", [128, D])
                load_gain(gpre, gn[(0, "pre_mix")])
                load_w(wkv, "w0in", 0, 8, 1024, 2048)
                xin = [sbl("xin%d" % i, [128, D]) for i in range(8)]
                hb = [sbl("h%d" % i, [128, D], BF16) for i in range(2)]
                hT = [sbl("hT%d" % i, [128, 8, 512], BF16) for i in range(2)]
                ktst = [sbl("ktst%d" % i, [128, 8, 512], BF16) for i in range(2)]
                vst = [sbl("vst%d" % i, [128, 4, D], BF16) for i in range(2)]
                NB = T // 512
                pbT = banks[0]
                pk = banks[1:8]
                pkn = [0]

                def nextbank():
                    pkn[0] += 1
                    return pk[pkn[0] % 7]

                def loadx(b):
                    for t in range(4):
                        xt = xin[(b * 4 + t) % 8]
                        r0 = b * 512 + t * 128
                        S.dma("sp", xt.sem, lambda e, xt=xt, r0=r0: e.dma_start(out=xt.t[:], in_=src[r0:r0 + 128, :]),
                              reads=[srcB], writes=[xt.b])

                def pre(b):
                    loadx(b)
                    for t in range(4):
                        norm_T(xin[(b * 4 + t) % 8], gpre, hb[t % 2], hT[b % 2], t * 128, pbT)

                def mid_norm(b, g):
                    if b + 1 >= NB:
                        return
                    t, ph = g // 4, g % 4
                    if ph == 0:
                        norm_A(xin[((b + 1) * 4 + t) % 8], gpre, hb[t % 2])
                    elif ph == 2:
                        norm_B(hb[t % 2], hT[(b + 1) % 2], t * 128, pbT)

                def main(b):
                    hTb = hT[b % 2]
                    kt = ktst[b % 2]
                    vt = vst[b % 2]
                    for hd in range(8):
                        mid_norm(b, hd)
                        bk = nextbank()
                        for kc in range(8):
                            S.op("pe", lambda e, hd=hd, kc=kc, bk=bk: e.matmul(
                                bk.t[:], wkv.t[:, kc, hd * 128:(hd + 1) * 128], hTb.t[:, kc, :],
                                start=(kc == 0), stop=(kc == 7)), reads=[wkv.b, hTb.b], writes=[bk.b])
                        eng = "act" if hd % 2 == 0 else "dve"
                        if eng == "act":
                            S.op("act", lambda e, hd=hd, bk=bk: e.activation(out=kt.t[:, hd, :], in_=bk.t[:], func=AF.Copy),
                                 reads=[bk.b], writes=[kt.b])
                        else:
                            S.op("dve", lambda e, hd=hd, bk=bk: e.tensor_copy(out=kt.t[:, hd, :], in_=bk.t[:]),
                                 reads=[bk.b], writes=[kt.b])
                    S.dma("sp", kt.sem, lambda e, b=b, kt=kt: e.dma_start(
                        out=KTs[:, :, b * 512:(b + 1) * 512].rearrange("h d t -> d h t"), in_=kt.t[:]),
                        reads=[kt.b], writes=[dramB["KTs"]])
                    for t in range(4):
                        for hf in range(2):
                            mid_norm(b, 8 + t * 2 + hf)
                            bk = nextbank()
                            for kc in range(8):
                                S.op("pe", lambda e, t=t, hf=hf, kc=kc, bk=bk: e.matmul(
                                    bk.t[:], hTb.t[:, kc, t * 128:(t + 1) * 128],
                                    wkv.t[:, kc, 1024 + hf * 512:1024 + (hf + 1) * 512],
                                    start=(kc == 0), stop=(kc == 7)), reads=[wkv.b, hTb.b], writes=[bk.b])
                            if hf == 0:
                                S.op("act", lambda e, t=t, hf=hf, bk=bk: e.activation(
                                    out=vt.t[:, t, hf * 512:(hf + 1) * 512], in_=bk.t[:], func=AF.Copy),
                                    reads=[bk.b], writes=[vt.b])
                            else:
                                S.op("dve", lambda e, t=t, hf=hf, bk=bk: e.tensor_copy(
                                    out=vt.t[:, t, hf * 512:(hf + 1) * 512], in_=bk.t[:]),
                                    reads=[bk.b], writes=[vt.b])
                    for t in range(4):
                        S.dma("sp", vt.sem, lambda e, b=b, vt=vt, t=t: e.dma_start(
                            out=Vs[:, :, b * 4 + t, :].rearrange("h p e -> p h e"),
                            in_=vt.t[:, t, :].rearrange("p (h e) -> p h e", h=8)),
                            reads=[vt.b], writes=[dramB["Vs"]])

                pre(0)
                for b in range(NB):
                    if b + 1 < NB:
                        loadx(b + 1)
                    main(b)

        def attn_q_phase(src, srcB, dst, dstB):
            with ExitStack() as ps:
                def sbl(name, shape, dt=F32):
                    return Tl(ps.enter_context(nc.sbuf_tensor("s_b" + name, list(shape), dt)), "b" + name, S)
                wq = sbl("wq", [128, 8, D], BF16)
                wo = sbl("wo", [128, 8, D], BF16)
                gpre = sbl("gpre", [128, D])
                gpost = sbl("gpost", [128, D])
                abase = sbl("abase", [128, 5, 512])
                kaug = sbl("kaug", [128, 16, 128], BF16)
                qaug = sbl("qaug", [128, 512], BF16)
                S.dma("sp", kaug.sem, lambda e: e.dma_start(out=kaug.t[:], in_=ka_d), writes=[kaug.b])
                S.dma("sp", qaug.sem, lambda e: e.dma_start(out=qaug.t[:], in_=qa_d), writes=[qaug.b])
                lamt = sbl("lamt", [128, 256])
                lprod = sbl("lprod", [128, 128])
                cst = sbl("cst", [128, 8])
                load_gain(gpre, gn[(0, "pre_mix")])
                load_gain(gpost, gn[(0, "post_mix")])
                S.dma("sp", abase.sem, lambda e: e.dma_start(out=abase.t[:], in_=abase_d), writes=[abase.b])
                S.dma("sp", lamt.sem, lambda e: e.dma_start(
                    out=lamt.t[:], in_=lamv.rearrange("a b -> (a b)").partition_broadcast(128)), writes=[lamt.b])
                S.dma("sp", cst.sem, lambda e: e.dma_start(out=cst.t[:, 5:6], in_=subln.rearrange("(p o) -> p o", o=1)),
                      writes=[cst.b])
                load_w(wq, "w0in", 0, 8, 0, 1024)
                load_w(wo, "w0out", 0, 8, 0, 1024)
                S.op("dve", lambda e: e.tensor_tensor(out=lprod.t[:, 0:64], in0=lamt.t[:, 0:64], in1=lamt.t[:, 64:128],
                                                      op=ALU.mult), reads=[lamt.b], writes=[lprod.b])
                S.op("dve", lambda e: e.tensor_tensor(out=lprod.t[:, 64:128], in0=lamt.t[:, 128:192],
                                                      in1=lamt.t[:, 192:256], op=ALU.mult),
                     reads=[lamt.b], writes=[lprod.b])
                for i in range(2):
                    S.op("act", lambda e, i=i: e.activation(out=junk.t[:, 0:64], in_=lprod.t[:, i * 64:(i + 1) * 64],
                                                            func=AF.Identity, accum_out=cst.t[:, i:i + 1]),
                         reads=[lprod.b], writes=[cst.b])
                S.op("act", lambda e: e.activation(out=cst.t[:, 2:4], in_=cst.t[:, 0:2], func=AF.Exp),
                     reads=[cst.b], writes=[cst.b])
                S.op("dve", lambda e: e.tensor_tensor(out=cst.t[:, 4:5], in0=cst.t[:, 3:4], in1=cst.t[:, 2:3],
                                                      op=ALU.subtract), reads=[cst.b], writes=[cst.b])
                S.op("dve", lambda e: e.tensor_scalar(out=cst.t[:, 4:5], in0=cst.t[:, 4:5], scalar1=-LAM_INIT0,
                                                      scalar2=None, op0=ALU.add), reads=[cst.b], writes=[cst.b])
                S.op("dve", lambda e: e.tensor_scalar(out=cst.t[:, 6:7], in0=cst.t[:, 5:6], scalar1=(1.0 - LAM_INIT0),
                                                      scalar2=None, op0=ALU.mult), reads=[cst.b], writes=[cst.b])
                nlam = cst.t[:, 4:5]
                gsc = cst.t[:, 6:7]

                xin = [sbl("xin%d" % i, [128, D]) for i in range(4)]
                xr = [sbl("xr%d" % i, [128, D]) for i in range(2)]
                hb = [sbl("h%d" % i, [128, D], BF16) for i in range(2)]
                hT = sbl("hT", [128, 8, 512], BF16)
                QT = [sbl("QT%d" % i, [128, 8, 512], BF16) for i in range(2)]
                kh = [sbl("kh%d" % i, [128, max(seqs)], BF16) for i in range(2)]
                vh = [sbl("vh%d" % i, [128, max(seqs) // 128, 128], BF16) for i in range(2)]
                scs = [sbl("scs%d" % i, [128, 2, 512]) for i in range(1)]
                EsA = [sbl("EsA%d" % i, [128, 512]) for i in range(2)]
                onesf = sbl("onesf", [128, 128])
                S.op("dve", lambda e: e.memset(onesf.t[:], 1.0), writes=[onesf.b])
                ex = [sbl("ex%d" % i, [128, 2, 512], BF16) for i in range(4)]
                R = [sbl("R%d" % i, [128, 512]) for i in range(2)]
                t01 = [sbl("t01%d" % i, [128, 512]) for i in range(2)]
                osb = [sbl("osb%d" % i, [128, 512]) for i in range(2)]
                sq = [sbl("sq%d" % i, [128, 512], BF16) for i in range(2)]
                rs = sbl("rs", [128, 512])
                rr = sbl("rr", [128, 512])
                aT = [sbl("aT%d" % i, [128, 8, 512], BF16) for i in range(2)]
                ot = [sbl("ot%d" % i, [128, D]) for i in range(2)]
                scb = [banks[0:2], banks[2:4]]
                Ob = banks[4:6]
                Zb = banks[6:8]
                cnt = {"u": 0, "hd": 0}

                def pre(si, qb):
                    tok0 = seq_off[si] + qb * 512
                    for t in range(4):
                        xt = xin[t]
                        r0 = tok0 + t * 128
                        S.dma("sp", xt.sem, lambda e, xt=xt, r0=r0: e.dma_start(out=xt.t[:], in_=src[r0:r0 + 128, :]),
                              reads=[srcB], writes=[xt.b])
                        norm_T(xt, gpre, hb[t % 2], hT, t * 128, banks[6])
                    qt = QT[cnt["q"] % 2]
                    for hd in range(8):
                        bk = banks[7] if hd % 2 == 0 else banks[6]
                        for kc in range(8):
                            S.op("pe", lambda e, hd=hd, kc=kc, bk=bk: e.matmul(
                                bk.t[:], wq.t[:, kc, hd * 128:(hd + 1) * 128], hT.t[:, kc, :],
                                start=(kc == 0), stop=(kc == 7)), reads=[wq.b, hT.b], writes=[bk.b])
                        if hd % 2 == 0:
                            S.op("act", lambda e, hd=hd, bk=bk: e.activation(out=qt.t[:, hd, :], in_=bk.t[:], func=AF.Copy),
                                 reads=[bk.b], writes=[qt.b])
                        else:
                            S.op("dve", lambda e, hd=hd, bk=bk: e.tensor_copy(out=qt.t[:, hd, :], in_=bk.t[:]),
                                 reads=[bk.b], writes=[qt.b])
                    cnt["q"] += 1
                    return qt

                def issue_load(g):
                    if g >= len(headlist):
                        return
                    si, qb, hd = headlist[g]
                    Sq = seqs[si]
                    nkc = Sq // 128
                    k_t, v_t = kh[g % 2], vh[g % 2]
                    S.dma("sp", k_t.sem, lambda e: e.dma_start(
                        out=k_t.t[:, 0:Sq], in_=KTs[hd, :, seq_off[si]:seq_off[si] + Sq]),
                        reads=[dramB["KTs"]], writes=[k_t.b])
                    S.dma("sp", v_t.sem, lambda e: e.dma_start(
                        out=v_t.t[:, 0:nkc, :], in_=Vs[hd, :, seq_off[si] // 128:seq_off[si] // 128 + nkc, :]),
                        reads=[dramB["Vs"]], writes=[v_t.b])

                def claim_bank():
                    u = cnt["u"]
                    cnt["u"] += 1
                    return scb[u % 2][0]

                def sched_pre(pend, nxt):
                    si2, qb2 = nxt
                    tok0 = seq_off[si2] + qb2 * 512
                    qtn = QT[cnt["q"] % 2]
                    cnt["q"] += 1
                    for t in range(4):
                        def fa(t=t):
                            xt = xin[t]
                            r0 = tok0 + t * 128
                            S.dma("sp", xt.sem, lambda e: e.dma_start(out=xt.t[:], in_=src[r0:r0 + 128, :]),
                                  reads=[srcB], writes=[xt.b])
                            norm_A(xt, gpre, hb[t % 2])
                        def fb(t=t):
                            norm_B(hb[t % 2], hT, t * 128, claim_bank())
                        pend.append((2 + 8 * t, fa))
                        pend.append((7 + 8 * t, fb))
                    for hd in range(8):
                        def fq(hd=hd):
                            bk = claim_bank()
                            for kc in range(8):
                                S.op("pe", lambda e, kc=kc: e.matmul(
                                    bk.t[:], wq.t[:, kc, hd * 128:(hd + 1) * 128], hT.t[:, kc, :],
                                    start=(kc == 0), stop=(kc == 7)), reads=[wq.b, hT.b], writes=[bk.b])
                            S.op("dve", lambda e: e.tensor_copy(out=qtn.t[:, hd, :], in_=bk.t[:]),
                                 reads=[bk.b], writes=[qtn.b])
                        pend.append((36 + 3 * hd, fq))
                    return qtn

                def main(si, qb, qt, at, gbase, nxt=None):
                    Sq = seqs[si]
                    nkc = Sq // 128
                    units = [(hd, kc) for hd in range(8) for kc in range(nkc)]
                    DPIPE = 2
                    ets = {}
                    pend = []
                    qtn = sched_pre(pend, nxt) if nxt is not None else None

                    def stageA(ui):
                        hd, kc = units[ui]
                        g = gbase + hd
                        k_t = kh[g % 2]
                        slope = SLOPES[hd]
                        u = cnt["u"]
                        cnt["u"] += 1
                        sb2 = scb[u % 2]
                        sc = scs[u % len(scs)]
                        et = ex[ui % 4]
                        ets[ui] = et
                        n = qb * 4 - kc
                        if n >= 1:
                            bidx, sgn, cb = 0, -8.0 * slope, -slope * 128.0 * n
                        elif n <= -4:
                            bidx, sgn, cb = 0, 8.0 * slope, slope * 128.0 * n
                        else:
                            bidx, sgn, cb = 1 - n, -8.0 * slope, 0.0
                        offdiag = (n >= 1 or n <= -4)
                        for c in range(2):
                            S.op("pe", lambda e, c=c: e.matmul(
                                sb2[c].t[:], k_t.t[c * 64:(c + 1) * 64, kc * 128:(kc + 1) * 128],
                                qt.t[c * 64:(c + 1) * 64, hd, :], start=True, stop=(not offdiag)),
                                reads=[k_t.b, qt.b], writes=[sb2[c].b])
                        if offdiag:
                            aidx = hd * 2 + (0 if n >= 1 else 1)
                            for c in range(2):
                                S.op("pe", lambda e, c=c: e.matmul(
                                    sb2[c].t[:], kaug.t[c * 64:(c + 1) * 64, aidx, :],
                                    qaug.t[c * 64:(c + 1) * 64, :], start=False, stop=True),
                                    reads=[kaug.b, qaug.b], writes=[sb2[c].b])
                            S.op("act", lambda e: e.activation(
                                out=et.t[:], in_=pairs[u % 2][:], func=AF.Exp, bias=float(cb), scale=0.125),
                                reads=[sb2[0].b, sb2[1].b], writes=[et.b])
                        else:
                            for c in range(2):
                                S.op("dve", lambda e, c=c: e.scalar_tensor_tensor(
                                    out=sc.t[:, c, :], in0=abase.t[:, bidx, :], scalar=sgn, in1=sb2[c].t[:],
                                    op0=ALU.mult, op1=ALU.add), reads=[abase.b, sb2[c].b], writes=[sc.b])
                            S.op("act", lambda e: e.activation(
                                out=et.t[:], in_=sc.t[:], func=AF.Exp, bias=float(cb), scale=0.125),
                                reads=[sc.b], writes=[et.b])

                    def stageB(ui, step):
                        hd, kc = units[ui]
                        g = gbase + hd
                        v_t = vh[g % 2]
                        et = ets.pop(ui)
                        for c in range(2):
                            S.op("pe", lambda e, c=c: e.matmul(
                                Ob[c].t[:], v_t.t[:, kc, :], et.t[:, c, :], start=(kc == 0), stop=(kc == nkc - 1)),
                                reads=[v_t.b, et.b], writes=[Ob[c].b])
                        es0 = EsA[hd % 2]
                        if kc == 0:
                            S.op("dve", lambda e: e.tensor_copy(out=es0.t[:], in_=et.t[:, 0, :]),
                                 reads=[et.b], writes=[es0.b])
                        else:
                            S.op("dve", lambda e: e.tensor_tensor(out=es0.t[:], in0=es0.t[:], in1=et.t[:, 0, :], op=ALU.add),
                                 reads=[es0.b, et.b], writes=[es0.b])
                        S.op("pe", lambda e: e.matmul(
                            Zb[1].t[:], ones.t[:], et.t[:, 1, :], start=(kc == 0), stop=(kc == nkc - 1)),
                            reads=[ones.b, et.b], writes=[Zb[1].b])
                        if kc != nkc - 1:
                            return
                        o_t = osb[hd % 2]
                        sq_t = sq[hd % 2]
                        for c in range(2):
                            S.op("dve", lambda e, c=c: e.tensor_copy(out=t01[c].t[:], in_=Ob[c].t[:]),
                                 reads=[Ob[c].b], writes=[t01[c].b])
                        S.op("dve", lambda e: e.tensor_copy(out=R[1].t[:], in_=Zb[1].t[:]), reads=[Zb[1].b], writes=[R[1].b])
                        issue_load(g + 2)

                        def fin1b():
                            S.op("pe", lambda e: e.matmul(Zb[0].t[:], onesf.t[:], es0.t[:], start=True, stop=True),
                                 reads=[onesf.b, es0.b], writes=[Zb[0].b])
                            S.op("act", lambda e: e.activation(out=R[0].t[:], in_=Zb[0].t[:], func=AF.Ln),
                                 reads=[Zb[0].b], writes=[R[0].b])
                            S.op("act", lambda e: e.activation(out=R[1].t[:], in_=R[1].t[:], func=AF.Ln),
                                 reads=[R[1].b], writes=[R[1].b])
                            for c in range(2):
                                S.op("act", lambda e, c=c: e.activation(out=R[c].t[:], in_=R[c].t[:], func=AF.Exp, scale=-1.0),
                                     reads=[R[c].b], writes=[R[c].b])
                                S.op("dve", lambda e, c=c: e.tensor_tensor(out=t01[c].t[:], in0=t01[c].t[:], in1=R[c].t[:],
                                                                           op=ALU.mult),
                                     reads=[t01[c].b, R[c].b], writes=[t01[c].b])
                            S.op("dve", lambda e: e.scalar_tensor_tensor(out=o_t.t[:], in0=t01[1].t[:], scalar=nlam,
                                                                          in1=t01[0].t[:], op0=ALU.mult, op1=ALU.add),
                                 reads=[t01[0].b, t01[1].b, cst.b], writes=[o_t.b])
                            S.op("act", lambda e: e.activation(out=sq_t.t[:], in_=o_t.t[:], func=AF.Square),
                                 reads=[o_t.b], writes=[sq_t.b])
                        pend.append((step + 2, fin1b))

                        def fin2():
                            u = cnt["u"]
                            cnt["u"] += 1
                            bk = scb[u % 2][0]
                            S.op("pe", lambda e: e.matmul(bk.t[:], ones.t[:], sq_t.t[:], start=True, stop=True),
                                 reads=[ones.b, sq_t.b], writes=[bk.b])
                            S.op("act", lambda e: e.activation(out=rs.t[:], in_=bk.t[:], func=AF.Ln, bias=EPS,
                                                               scale=1.0 / 128), reads=[bk.b], writes=[rs.b])
                            S.op("act", lambda e: e.activation(out=rr.t[:], in_=rs.t[:], func=AF.Exp, scale=-0.5),
                                 reads=[rs.b], writes=[rr.b])
                            S.op("dve", lambda e: e.scalar_tensor_tensor(
                                out=at.t[:, hd, :], in0=o_t.t[:], scalar=gsc, in1=rr.t[:], op0=ALU.mult, op1=ALU.mult),
                                reads=[o_t.b, rr.b, cst.b], writes=[at.b])
                        pend.append((step + 7, fin2))

                    nU = len(units)
                    for step in range(nU + DPIPE):
                        if step < nU:
                            stageA(step)
                        if step - DPIPE >= 0:
                            stageB(step - DPIPE, step)
                        pend.sort(key=lambda x: x[0])
                        while pend and pend[0][0] <= step:
                            pend.pop(0)[1]()
                    pend.sort(key=lambda x: x[0])
                    while pend:
                        pend.pop(0)[1]()
                    return qtn

                def post(si, qb, at):
                    tok0 = seq_off[si] + qb * 512
                    for t in range(4):
                        po = scb[t % 2]
                        xt = xr[t % 2]
                        r0 = tok0 + t * 128
                        S.dma("sp", xt.sem, lambda e, xt=xt, r0=r0: e.dma_start(out=xt.t[:], in_=src[r0:r0 + 128, :]),
                              reads=[srcB], writes=[xt.b])
                        for hf in range(2):
                            for hd in range(8):
                                S.op("pe", lambda e, t=t, hf=hf, hd=hd, po=po: e.matmul(
                                    po[hf].t[:], at.t[:, hd, t * 128:(t + 1) * 128],
                                    wo.t[:, hd, hf * 512:(hf + 1) * 512], start=(hd == 0), stop=(hd == 7)),
                                    reads=[at.b, wo.b], writes=[po[hf].b])
                        post_norm_store(po, xt, gpost, ot[t % 2], dst, dstB, r0)

                cnt["q"] = 0
                blocks = [(si, qb) for si in range(len(seqs)) for qb in range(seqs[si] // 512)]
                headlist = [(si, qb, hd) for (si, qb) in blocks for hd in range(8)]
                issue_load(0)
                issue_load(1)
                qts = {0: pre(*blocks[0])}
                for i, (si, qb) in enumerate(blocks):
                    at = aT[i % 2]
                    qts[i + 1] = main(si, qb, qts[i], at, i * 8, blocks[i + 1] if i + 1 < len(blocks) else None)
                    post(si, qb, at)

        def ret_phases(src, srcB, dst, dstB):
            bkn = [0]

            def nb():
                bkn[0] += 1
                return banks[bkn[0] % 8]

            rtb = sb("rtb", [128, 40])
            ccs = sb("ccs", [128, 2])
            S.dma("sp", rtb.sem, lambda e: e.dma_start(out=rtb.t[:, 0:8], in_=decay.partition_broadcast(128)),
                  writes=[rtb.b])
            S.dma("sp", ccs.sem, lambda e: e.dma_start(out=ccs.t[:], in_=cc_d), writes=[ccs.b])
            S.op("act", lambda e: e.activation(out=rtb.t[:, 8:16], in_=rtb.t[:, 0:8], func=AF.Exp),
                 reads=[rtb.b], writes=[rtb.b])
            S.op("act", lambda e: e.activation(out=rtb.t[:, 16:24], in_=rtb.t[:, 8:16], func=AF.Ln, bias=1.0, scale=-1.0),
                 reads=[rtb.b], writes=[rtb.b])
            S.op("act", lambda e: e.activation(out=rtb.t[:, 24:28], in_=rtb.t[:, 16:20], func=AF.Exp, scale=ccs.t[:, 0:1]),
                 reads=[rtb.b, ccs.b], writes=[rtb.b])
            S.op("act", lambda e: e.activation(out=rtb.t[:, 28:32], in_=rtb.t[:, 20:24], func=AF.Exp, scale=ccs.t[:, 1:2]),
                 reads=[rtb.b, ccs.b], writes=[rtb.b])
            S.op("act", lambda e: e.activation(out=rtb.t[:, 32:40], in_=rtb.t[:, 16:24], func=AF.Exp, scale=128.0),
                 reads=[rtb.b], writes=[rtb.b])
            if "ret_setup_only" in phases:
                return
            lg = lambda i: rtb.t[:, 16 + i:17 + i]
            gC = lambda i: rtb.t[:, 32 + i:33 + i]

            with ExitStack() as ps:
                def sbl(name, shape, dt=F32):
                    return Tl(ps.enter_context(nc.sbuf_tensor("s_r" + name, list(shape), dt)), "r" + name, S)
                wk = sbl("wkv", [128, 8, 3072], BF16)
                wg = sbl("wg", [128, 8, 2048], BF16)
                gpre = sbl("gpre", [128, D])
                load_gain(gpre, gn[(1, "pre_mix")])
                load_w(wk, "w1in", 0, 8, 1024, 3072)
                load_w(wg, "w1in", 0, 8, 4096, 2048)
                Sb = sbl("Sb", [128, 8, 512])
                Sbf = [sbl("Sbf%d" % i, [128, 8, 512], BF16) for i in range(2)]
                xin = [sbl("xin%d" % i, [128, D]) for i in range(3)]
                hb = sbl("h", [128, D], BF16)
                hT = [sbl("hT%d" % i, [128, 8, 128], BF16) for i in range(2)]
                kraw = [sbl("kraw%d" % i, [128, 1024], BF16) for i in range(2)]
                kb = sbl("kb", [128, 1024], BF16)
                vv = [sbl("v%d" % i, [128, 2048], BF16) for i in range(2)]
                sg = [sbl("sg%d" % i, [128, 2048], BF16) for i in range(2)]
                chunks = []
                for si in range(len(seqs)):
                    n_ = seqs[si] // 128
                    for n in range(n_ - 1, -1, -1):
                        chunks.append((si, n, n == n_ - 1))

                def loadx(i):
                    si, n, first = chunks[i]
                    r0 = seq_off[si] + n * 128
                    xt = xin[i % 3]
                    S.dma("sp", xt.sem, lambda e: e.dma_start(out=xt.t[:], in_=src[r0:r0 + 128, :]),
                          reads=[srcB], writes=[xt.b])

                def pre(i):
                    norm_T(xin[i % 3], gpre, hb, hT[i % 2], 0, nb())

                def main(i, mid=None):
                    si, n, first = chunks[i]
                    r0 = seq_off[si] + n * 128
                    cg = r0 // 128
                    hTi = hT[i % 2]
                    kr, v_, sg_ = kraw[i % 2], vv[i % 2], sg[i % 2]
                    for grp in range(10):
                        if mid is not None and grp == 2:
                            norm_A(xin[(i + 1) % 3], gpre, hb)
                        if mid is not None and grp == 7:
                            norm_B(hb, hT[(i + 1) % 2], 0, nb())
                        bk = nb()
                        for kc in range(8):
                            wsrc, gcol = (wk, grp) if grp < 6 else (wg, grp - 6)
                            S.op("pe", lambda e, gcol=gcol, kc=kc, bk=bk, wsrc=wsrc: e.matmul(
                                bk.t[:], hTi.t[:, kc, :], wsrc.t[:, kc, gcol * 512:(gcol + 1) * 512],
                                start=(kc == 0), stop=(kc == 7)), reads=[hTi.b, wsrc.b], writes=[bk.b])
                        if grp < 2:
                            S.op("act", lambda e, grp=grp, bk=bk: e.activation(
                                out=kr.t[:, grp * 512:(grp + 1) * 512], in_=bk.t[:], func=AF.Copy),
                                reads=[bk.b], writes=[kr.b])
                            for hh in range(2):
                                h_ = grp * 2 + hh
                                S.op("act", lambda e, grp=grp, bk=bk, hh=hh, h_=h_: e.activation(
                                    out=kb.t[:, h_ * 256:(h_ + 1) * 256], in_=bk.t[:, hh * 256:(hh + 1) * 256],
                                    func=AF.Copy, scale=rtb.t[:, 28 + h_:29 + h_]),
                                    reads=[bk.b, rtb.b], writes=[kb.b])
                        elif grp < 6:
                            g2 = grp - 2
                            S.op("dve", lambda e, g2=g2, bk=bk: e.tensor_copy(out=v_.t[:, g2 * 512:(g2 + 1) * 512], in_=bk.t[:]),
                                 reads=[bk.b], writes=[v_.b])
                        else:
                            g2 = grp - 6
                            S.op("act", lambda e, g2=g2, bk=bk: e.activation(
                                out=sg_.t[:, g2 * 512:(g2 + 1) * 512], in_=bk.t[:], func=AF.Silu),
                                reads=[bk.b], writes=[sg_.b])
                    S.dma("sp", kr.sem, lambda e: e.dma_start(out=KS[r0:r0 + 128, :], in_=kr.t[:]),
                          reads=[kr.b], writes=[dramB["KS"]])
                    S.dma("sp", v_.sem, lambda e: e.dma_start(out=VS[r0:r0 + 128, :], in_=v_.t[:]),
                          reads=[v_.b], writes=[dramB["VS"]])
                    S.dma("sp", sg_.sem, lambda e: e.dma_start(out=SG[r0:r0 + 128, :], in_=sg_.t[:]),
                          reads=[sg_.b], writes=[dramB["SG"]])
                    if first:
                        S.op("dve", lambda e: e.memset(Sb.t[:].rearrange("p a b -> p (a b)"), 0.0), writes=[Sb.b])
                    sbf = Sbf[i % 2]
                    for q4 in range(4):
                        S.op("pool", lambda e, q4=q4: e.tensor_copy(out=sbf.t[:, 2 * q4:2 * q4 + 2, :],
                                                                    in_=Sb.t[:, 2 * q4:2 * q4 + 2, :]),
                             reads=[Sb.b], writes=[sbf.b])
                    S.dma("sp", sbf.sem, lambda e: e.dma_start(out=SBs[cg], in_=sbf.t[:].rearrange("p a b -> p (a b)")),
                          reads=[sbf.b], writes=[dramB["SBs"]])
                    for h_ in range(4):
                        for dc in range(2):
                            bk = nb()
                            S.op("pe", lambda e, h_=h_, dc=dc, bk=bk: e.matmul(
                                bk.t[:], kb.t[:, h_ * 256 + dc * 128:h_ * 256 + (dc + 1) * 128],
                                v_.t[:, h_ * 512:(h_ + 1) * 512], start=True, stop=True),
                                reads=[kb.b, v_.b], writes=[bk.b])
                            S.op("dve", lambda e, h_=h_, dc=dc, bk=bk: e.scalar_tensor_tensor(
                                out=Sb.t[:, h_ * 2 + dc, :], in0=Sb.t[:, h_ * 2 + dc, :], scalar=gC(4 + h_),
                                in1=bk.t[:], op0=ALU.mult, op1=ALU.add),
                                reads=[Sb.b, bk.b, rtb.b], writes=[Sb.b])

                loadx(0)
                if len(chunks) > 1:
                    loadx(1)
                pre(0)
                for i in range(len(chunks)):
                    if i + 2 < len(chunks):
                        loadx(i + 2)
                    main(i, (lambda i=i: pre(i + 1)) if i + 1 < len(chunks) else None)
            S.barrier()
            if "reta_only" in phases:
                return

            with ExitStack() as ps:
                def sbl(name, shape, dt=F32):
                    return Tl(ps.enter_context(nc.sbuf_tensor("s_q" + name, list(shape), dt)), "q" + name, S)
                wq = sbl("wq", [128, 8, 1024], BF16)
                wo = sbl("wo", [128, 16, 1024], BF16)
                gpre = sbl("gpre", [128, D])
                gpost = sbl("gpost", [128, D])
                GW = sbl("GW", [128, 2048])
                GB = sbl("GB", [128, 2048])
                rc = sbl("rc", [128, 6, 128])
                TfT = sbl("TfT", [128, 8, 128])
                TbT = sbl("TbT", [128, 8, 128])
                MT = sbl("MT", [128, 4, 128])
                mtmp = sbl("mtmp", [128, 128])
                load_gain(gpre, gn[(1, "pre_mix")])
                load_gain(gpost, gn[(1, "post_mix")])
                load_gain(GW, gnw)
                load_gain(GB, gnb)
                S.dma("sp", rc.sem, lambda e: e.dma_start(out=rc.t[:], in_=rc_d), writes=[rc.b])
                load_w(wq, "w1in", 0, 8, 0, 1024)
                load_w(wo, "w1out", 0, 16, 0, 1024)
                for kc in range(8):
                    h_ = kc // 2
                    S.op("act", lambda e, kc=kc, h_=h_: e.activation(out=TfT.t[:, kc, :], in_=rc.t[:, 0, :], func=AF.Exp,
                                                                      scale=lg(h_)), reads=[rc.b, rtb.b], writes=[TfT.b])
                    S.op("act", lambda e, kc=kc, h_=h_: e.activation(out=TbT.t[:, kc, :], in_=rc.t[:, 1, :], func=AF.Exp,
                                                                      scale=lg(4 + h_)), reads=[rc.b, rtb.b], writes=[TbT.b])
                for h_ in range(4):
                    S.op("act", lambda e, h_=h_: e.activation(out=mtmp.t[:], in_=rc.t[:, 2, :], func=AF.Exp, scale=lg(h_)),
                         reads=[rc.b, rtb.b], writes=[mtmp.b])
                    S.op("dve", lambda e, h_=h_: e.tensor_tensor(out=MT.t[:, h_, :], in0=mtmp.t[:], in1=rc.t[:, 3, :],
                                                                  op=ALU.mult), reads=[mtmp.b, rc.b], writes=[MT.b])
                    S.op("act", lambda e, h_=h_: e.activation(out=mtmp.t[:], in_=rc.t[:, 4, :], func=AF.Exp, scale=lg(4 + h_)),
                         reads=[rc.b, rtb.b], writes=[mtmp.b])
                    S.op("dve", lambda e, h_=h_: e.tensor_tensor(out=mtmp.t[:], in0=mtmp.t[:], in1=rc.t[:, 5, :],
                                                                  op=ALU.mult), reads=[mtmp.b, rc.b], writes=[mtmp.b])
                    S.op("dve", lambda e, h_=h_: e.tensor_tensor(out=MT.t[:, h_, :], in0=MT.t[:, h_, :], in1=mtmp.t[:],
                                                                  op=ALU.add), reads=[mtmp.b, MT.b], writes=[MT.b])
                Sf = sbl("Sf", [128, 8, 512])
                Sff = sbl("Sff", [128, 8, 512], BF16)
                Sbn = sbl("Sbn", [128, 8, 512], BF16)
                xin = [sbl("xin%d" % i, [128, D]) for i in range(3)]
                hb = sbl("h", [128, D], BF16)
                hT = sbl("hT", [128, 8, 128], BF16)
                kraw = [sbl("kraw%d" % i, [128, 1024], BF16) for i in range(2)]
                vv = [sbl("v%d" % i, [128, 2048], BF16) for i in range(2)]
                sg = [sbl("sg%d" % i, [128, 2048], BF16) for i in range(3)]
                kf = sbl("kf", [128, 1024], BF16)
                qtok = sbl("qtok", [128, 1024], BF16)
                qT = sbl("qT", [128, 8, 128], BF16)
                qfT = sbl("qfT", [128, 8, 128], BF16)
                qbT = sbl("qbT", [128, 8, 128], BF16)
                kT = sbl("kT", [128, 8, 128], BF16)
                Am = sbl("Am", [128, 4, 128], BF16)
                Us = [sbl("U%d" % i, [128, 2048]) for i in range(2)]
                z = sbl("z", [128, 2048], BF16)
                zT = sbl("zT", [128, 16, 128], BF16)
                ot = [sbl("ot%d" % i, [128, D]) for i in range(2)]
                bst = sbl("bst", [128, 4, 6])
                mv = sbl("mv", [128, 4, 2])
                gst = sbl("gst", [128, 12])
                chunks = [(si, n) for si in range(len(seqs)) for n in range(seqs[si] // 128)]
                b4n = [0]

                def nb():
                    b4n[0] += 1
                    return banks[b4n[0] % 4]
                ybs = {}

                def pre(i):
                    si, n = chunks[i]
                    r0 = seq_off[si] + n * 128
                    xt = xin[i % 3]
                    S.dma("sp", xt.sem, lambda e: e.dma_start(out=xt.t[:], in_=src[r0:r0 + 128, :]),
                          reads=[srcB], writes=[xt.b])
                    kr, v_, sg_ = kraw[i % 2], vv[i % 2], sg[i % 3]
                    S.dma("sp", kr.sem, lambda e: e.dma_start(out=kr.t[:], in_=KS[r0:r0 + 128, :]),
                          reads=[dramB["KS"]], writes=[kr.b])
                    S.dma("sp", v_.sem, lambda e: e.dma_start(out=v_.t[:], in_=VS[r0:r0 + 128, :]),
                          reads=[dramB["VS"]], writes=[v_.b])
                    S.dma("sp", sg_.sem, lambda e: e.dma_start(out=sg_.t[:], in_=SG[r0:r0 + 128, :]),
                          reads=[dramB["SG"]], writes=[sg_.b])

                def main(i):
                    si, n = chunks[i]
                    r0 = seq_off[si] + n * 128
                    cg = r0 // 128
                    xt = xin[i % 3]
                    U = Us[i % 2]
                    kr, v_, sg_ = kraw[i % 2], vv[i % 2], sg[i % 3]
                    norm_B(hb, hT, 0, nb())
                    S.dma("sp", Sbn.sem, lambda e: e.dma_start(out=Sbn.t[:].rearrange("p a b -> p (a b)"), in_=SBs[cg]),
                          reads=[dramB["SBs"]], writes=[Sbn.b])
                    if n == 0:
                        S.op("dve", lambda e: e.memset(Sf.t[:].rearrange("p a b -> p (a b)"), 0.0), writes=[Sf.b])
                        S.op("dve", lambda e: e.memset(Sff.t[:].rearrange("p a b -> p (a b)"), 0.0), writes=[Sff.b])
                    for grp in range(2):
                        bk = nb()
                        for kc in range(8):
                            S.op("pe", lambda e, grp=grp, kc=kc, bk=bk: e.matmul(
                                bk.t[:], hT.t[:, kc, :], wq.t[:, kc, grp * 512:(grp + 1) * 512],
                                start=(kc == 0), stop=(kc == 7)), reads=[hT.b, wq.b], writes=[bk.b])
                        S.op("act", lambda e, grp=grp, bk=bk: e.activation(
                            out=qtok.t[:, grp * 512:(grp + 1) * 512], in_=bk.t[:], func=AF.Copy, scale=1.0 / 16),
                            reads=[bk.b], writes=[qtok.b])
                    bq = nb()
                    pq = bq.t[:].bitcast(BF16)
                    for kc in range(8):
                        S.op("pe", lambda e, kc=kc: e.transpose(pq[:, kc * 128:(kc + 1) * 128],
                                                                 qtok.t[:, kc * 128:(kc + 1) * 128], ident.t[:]),
                             reads=[qtok.b, ident.b], writes=[bq.b])
                    pq3 = pq.rearrange("p (k n) -> p k n", k=8)
                    S.op("act", lambda e: e.activation(out=qT.t[:], in_=pq3, func=AF.Copy), reads=[bq.b], writes=[qT.b])
                    S.op("dve", lambda e: e.tensor_tensor(out=qfT.t[:], in0=qT.t[:], in1=TfT.t[:], op=ALU.mult),
                         reads=[qT.b, TfT.b], writes=[qfT.b])
                    S.op("dve", lambda e: e.tensor_tensor(out=qbT.t[:], in0=qT.t[:], in1=TbT.t[:], op=ALU.mult),
                         reads=[qT.b, TbT.b], writes=[qbT.b])
                    bkk = nb()
                    pk_ = bkk.t[:].bitcast(BF16)
                    for kc in range(8):
                        S.op("pe", lambda e, kc=kc: e.transpose(pk_[:, kc * 128:(kc + 1) * 128],
                                                                 kr.t[:, kc * 128:(kc + 1) * 128], ident.t[:]),
                             reads=[kr.b, ident.b], writes=[bkk.b])
                    S.op("act", lambda e: e.activation(out=kT.t[:], in_=pk_.rearrange("p (k n) -> p k n", k=8), func=AF.Copy),
                         reads=[bkk.b], writes=[kT.b])
                    for h_ in range(4):
                        S.op("act", lambda e, h_=h_: e.activation(
                            out=kf.t[:, h_ * 256:(h_ + 1) * 256], in_=kr.t[:, h_ * 256:(h_ + 1) * 256],
                            func=AF.Copy, scale=rtb.t[:, 24 + h_:25 + h_]),
                            reads=[kr.b, rtb.b], writes=[kf.b])
                    ba = nb()
                    for h_ in range(4):
                        for dc in range(2):
                            S.op("pe", lambda e, h_=h_, dc=dc: e.matmul(
                                ba.t[:, h_ * 128:(h_ + 1) * 128], kT.t[:, h_ * 2 + dc, :], qT.t[:, h_ * 2 + dc, :],
                                start=(dc == 0), stop=(dc == 1)), reads=[kT.b, qT.b], writes=[ba.b])
                    S.op("dve", lambda e: e.tensor_tensor(out=Am.t[:], in0=ba.t[:].rearrange("p (h n) -> p h n", h=4),
                                                          in1=MT.t[:], op=ALU.mult), reads=[ba.b, MT.b], writes=[Am.b])
                    yb = []
                    ybs[i] = yb
                    for h_ in range(4):
                        bk = banks[4 + h_]
                        yb.append(bk)
                        S.op("pe", lambda e, h_=h_, bk=bk: e.matmul(bk.t[:], Am.t[:, h_, :], v_.t[:, h_ * 512:(h_ + 1) * 512],
                                                                   start=True, stop=False),
                             reads=[Am.b, v_.b], writes=[bk.b])
                        for dc in range(2):
                            S.op("pe", lambda e, h_=h_, dc=dc, bk=bk: e.matmul(
                                bk.t[:], qfT.t[:, h_ * 2 + dc, :], Sff.t[:, h_ * 2 + dc, :], start=False, stop=False),
                                reads=[qfT.b, Sff.b], writes=[bk.b])
                        for dc in range(2):
                            S.op("pe", lambda e, h_=h_, dc=dc, bk=bk: e.matmul(
                                bk.t[:], qbT.t[:, h_ * 2 + dc, :], Sbn.t[:, h_ * 2 + dc, :], start=False, stop=(dc == 1)),
                                reads=[qbT.b, Sbn.b], writes=[bk.b])

                def part2(i):
                    si, n = chunks[i]
                    U = Us[i % 2]
                    kr, v_, sg_ = kraw[i % 2], vv[i % 2], sg[i % 3]
                    yb = ybs.pop(i)
                    for h_ in range(4):
                        bk = yb[h_]
                        S.op("dve", lambda e, h_=h_, bk=bk: e.bn_stats(out=bst.t[:, h_, :], in_=bk.t[:]),
                             reads=[bk.b], writes=[bst.b])
                        S.op("dve", lambda e, h_=h_: e.bn_aggr(out=mv.t[:, h_, :], in_=bst.t[:, h_, :]),
                             reads=[bst.b], writes=[mv.b])
                    S.op("act", lambda e: e.activation(out=gst.t[:, 0:4], in_=mv.t[:, :, 1], func=AF.Sqrt, bias=EPS, scale=1.0),
                         reads=[mv.b], writes=[gst.b])
                    S.op("dve", lambda e: e.reciprocal(out=gst.t[:, 4:8], in_=gst.t[:, 0:4]), reads=[gst.b], writes=[gst.b])
                    S.op("dve", lambda e: e.scalar_tensor_tensor(out=gst.t[:, 8:12], in0=mv.t[:, :, 0], scalar=-1.0,
                                                                  in1=gst.t[:, 4:8], op0=ALU.mult, op1=ALU.mult),
                         reads=[gst.b, mv.b], writes=[gst.b])
                    for h_ in range(4):
                        S.op("act", lambda e, h_=h_: e.activation(
                            out=U.t[:, h_ * 512:(h_ + 1) * 512], in_=yb[h_].t[:], func=AF.Identity,
                            bias=gst.t[:, 8 + h_:9 + h_], scale=gst.t[:, 4 + h_:5 + h_]),
                            reads=[yb[h_].b, gst.b], writes=[U.b])
                    for h_ in range(4):
                        for dc in range(2):
                            bk = nb()
                            S.op("pe", lambda e, h_=h_, dc=dc, bk=bk: e.matmul(
                                bk.t[:], kf.t[:, h_ * 256 + dc * 128:h_ * 256 + (dc + 1) * 128],
                                v_.t[:, h_ * 512:(h_ + 1) * 512], start=True, stop=True),
                                reads=[kf.b, v_.b], writes=[bk.b])
                            S.op("dve", lambda e, h_=h_, dc=dc, bk=bk: e.scalar_tensor_tensor(
                                out=Sf.t[:, h_ * 2 + dc, :], in0=Sf.t[:, h_ * 2 + dc, :], scalar=gC(h_),
                                in1=bk.t[:], op0=ALU.mult, op1=ALU.add),
                                reads=[Sf.b, bk.b, rtb.b], writes=[Sf.b])
                    S.op("act", lambda e: e.activation(out=Sff.t[:].rearrange("p a b -> p (a b)"),
                                                       in_=Sf.t[:].rearrange("p a b -> p (a b)"), func=AF.Copy),
                         reads=[Sf.b], writes=[Sff.b])
                def mainB(i):
                    si, n = chunks[i]
                    r0 = seq_off[si] + n * 128
                    xt = xin[i % 3]
                    sg_ = sg[i % 3]
                    U = Us[i % 2]
                    for hf in range(2):
                        sl = slice(hf * 1024, (hf + 1) * 1024)
                        S.op("dve", lambda e, sl=sl: e.tensor_tensor(out=U.t[:, sl], in0=U.t[:, sl], in1=GW.t[:, sl], op=ALU.mult),
                             reads=[U.b, GW.b], writes=[U.b])
                        S.op("dve", lambda e, sl=sl: e.tensor_tensor(out=U.t[:, sl], in0=U.t[:, sl], in1=GB.t[:, sl], op=ALU.add),
                             reads=[U.b, GB.b], writes=[U.b])
                    S.op("dve", lambda e: e.tensor_tensor(out=z.t[:], in0=U.t[:], in1=sg_.t[:], op=ALU.mult),
                         reads=[U.b, sg_.b], writes=[z.b])

                def mainB_pe(i):
                    si, n = chunks[i]
                    r0 = seq_off[si] + n * 128
                    xt = xin[i % 3]
                    for half in range(2):
                        bz = nb()
                        pz = bz.t[:].bitcast(BF16)
                        for kc in range(8):
                            S.op("pe", lambda e, kc=kc, half=half, pz=pz, bz=bz: e.transpose(
                                pz[:, kc * 128:(kc + 1) * 128],
                                z.t[:, (half * 8 + kc) * 128:(half * 8 + kc + 1) * 128], ident.t[:]),
                                reads=[z.b, ident.b], writes=[bz.b])
                        S.op("act", lambda e, half=half, pz=pz, bz=bz: e.activation(
                            out=zT.t[:, half * 8:(half + 1) * 8, :], in_=pz.rearrange("p (k n) -> p k n", k=8), func=AF.Copy),
                            reads=[bz.b], writes=[zT.b])
                    po = [nb(), nb()]
                    for hf in range(2):
                        for kc in range(16):
                            S.op("pe", lambda e, hf=hf, kc=kc: e.matmul(
                                po[hf].t[:], zT.t[:, kc, :], wo.t[:, kc, hf * 512:(hf + 1) * 512],
                                start=(kc == 0), stop=(kc == 15)), reads=[zT.b, wo.b], writes=[po[hf].b])
                    post_norm_store(po, xt, gpost, ot[i % 2], dst, dstB, r0)

                pre(0)
                if len(chunks) > 1:
                    pre(1)
                norm_A(xin[0], gpre, hb)
                main(0)
                if len(chunks) > 1:
                    norm_A(xin[1], gpre, hb)
                part2(0)
                for i in range(len(chunks)):
                    if i + 2 < len(chunks):
                        pre(i + 2)
                    mainB(i)
                    if i + 1 < len(chunks):
                        main(i + 1)
                    if i + 2 < len(chunks):
                        norm_A(xin[(i + 2) % 3], gpre, hb)
                    mainB_pe(i)
                    if i + 1 < len(chunks):
                        part2(i + 1)

        cur, curB = x_in, Buf("x")
        S.barrier()
        if "cast" in phases:
            cast_weights(["w0f1", "w0f2", "w1in", "w1out", "w1f1", "w1f2"])
        if "attnkv" in phases:
            attn_kv_phase(cur, curB)
        if "attn" in phases:
            attn_kv_phase(cur, curB)
            S.barrier()
            nxt = XA if len(phases) > 2 else y_out
            nxtB = dramB["XA"] if nxt is XA else dramB["y"]
            attn_q_phase(cur, curB, nxt, nxtB)
            S.barrier()
            cur, curB = nxt, nxtB
        if "ffn0" in phases:
            nxt = XB if ("ret" in phases or "ffn1" in phases) else y_out
            nxtB = dramB["XB"] if nxt is XB else dramB["y"]
            ffn_phase("f0", cur, curB, nxt, nxtB, "w0f1", "w0f2", gn[(0, "pre_ffn")], gn[(0, "post_ffn")])
            S.barrier()
            cur, curB = nxt, nxtB
        if "ret" in phases:
            nxt = XA if "ffn1" in phases else y_out
            nxtB = dramB["XA"] if nxt is XA else dramB["y"]
            ret_phases(cur, curB, nxt, nxtB)
            S.barrier()
            cur, curB = nxt, nxtB
        if "ffn1" in phases:
            ffn_phase("f1", cur, curB, y_out, dramB["y"], "w1f1", "w1f2", gn[(1, "pre_ffn")], gn[(1, "post_ffn")])

        S.barrier()
        scratch = {"KTs": KTs, "Vs": Vs, "KS": KS, "VS": VS, "SG": SG, "SBs": SBs}
        for nm in dbg:
            if nm.startswith("dump_"):
                dsem = S.new_dma_sem()
                S.dma("sp", dsem, lambda e, nm=nm: e.dma_start(out=dbg[nm], in_=scratch[nm[5:]]), writes=[Buf()])
        S.final_wait("sp")
        sems = {}
        for k in list(S.ENGS) + list(S.dma_cnt.keys()):
            sems[k] = es.enter_context(nc.semaphore(k))
        block = es.enter_context(nc.Block())
        S.emit(nc, block, sems)
    return nc


def host_consts():
    c = {}
    c["ident"] = np.eye(128, dtype=np.float32).astype(ml_dtypes.bfloat16)
    c["ones"] = np.ones((128, 128), dtype=np.float32).astype(ml_dtypes.bfloat16)
    kj = np.arange(128, dtype=np.float32)[:, None]
    qi = np.arange(512, dtype=np.float32)[None, :]
    ab = np.zeros((128, 5, 512), np.float32)
    ab[:, 0, :] = qi - kj
    for j in range(4):
        ab[:, 1 + j, :] = np.abs(qi - kj - 128.0 * j)
    c["abase"] = ab
    i = np.arange(128, dtype=np.float32)
    rc = np.zeros((128, 6, 128), np.float32)
    rc[:, 0, :] = (i + 1.0)[None, :]
    rc[:, 1, :] = (128.0 - i)[None, :]
    dif = i[None, :] - i[:, None]
    rc[:, 2, :] = np.maximum(dif, 0.0)
    rc[:, 3, :] = (dif >= 0).astype(np.float32)
    rc[:, 4, :] = np.maximum(-dif, 0.0)
    rc[:, 5, :] = (dif <= 0).astype(np.float32)
    c["rc"] = rc
    cc = np.zeros((128, 2), np.float32)
    cc[:, 0] = 127.0 - i
    cc[:, 1] = i
    c["cc"] = cc
    ka = np.zeros((128, 16, 128), np.float32)
    qa = np.zeros((128, 512), np.float32)
    kjv = np.arange(128, dtype=np.float32)
    qiv = np.arange(512, dtype=np.float32)
    for base in (0, 64):
        qa[base + 0] = 1.0
        qa[base + 1] = 2.0 * np.floor(qiv / 2.0)
        qa[base + 2] = qiv - 2.0 * np.floor(qiv / 2.0)
        for h in range(8):
            for side, sg_ in ((0, 1.0), (1, -1.0)):
                ka[base + 0, h * 2 + side] = sg_ * 8.0 * SLOPES[h] * kjv
                ka[base + 1, h * 2 + side] = -sg_ * 8.0 * SLOPES[h]
                ka[base + 2, h * 2 + side] = -sg_ * 8.0 * SLOPES[h]
    c["ka"] = ka.astype(ml_dtypes.bfloat16)
    c["qa"] = qa.astype(ml_dtypes.bfloat16)
    cbt = np.zeros((128, 8, 32), np.float32)
    for h in range(8):
        cbt[:, h, :] = -SLOPES[h] * 128.0 * np.arange(32, dtype=np.float32)[None, :]
    c["cbt"] = cbt
    return c


def make_in_maps(inputs, x_cores):
    f = lambda a: np.ascontiguousarray(np.asarray(a, dtype=np.float32))
    base = {
        "w0in": f(inputs["l0_da_w_in"]), "w0out": f(inputs["l0_da_w_out"]),
        "w0f1": f(inputs["l0_ffn_w1"]), "w0f2": f(inputs["l0_ffn_w2"]),
        "w1in": f(inputs["l1_ret_w_in"]), "w1out": f(inputs["l1_ret_w_out"]),
        "w1f1": f(inputs["l1_ffn_w1"]), "w1f2": f(inputs["l1_ffn_w2"]),
        "lamv": np.stack([f(inputs["l0_da_lambda_q1"]), f(inputs["l0_da_lambda_k1"]),
                          f(inputs["l0_da_lambda_q2"]), f(inputs["l0_da_lambda_k2"])]),
        "subln": f(inputs["l0_da_subln"]),
        "decay": np.concatenate([f(inputs["l1_ret_decay_fwd"]), f(inputs["l1_ret_decay_bwd"])]),
        "gnw": f(inputs["l1_ret_gn_w"]), "gnb": f(inputs["l1_ret_gn_b"]),
    }
    base["g0_pre_mix"] = f(inputs["l0_norm_pre_mix"])
    base["g0_post_mix"] = f(inputs["l0_norm_post_mix"])
    base["g0_pre_ffn"] = f(inputs["l0_norm_pre_ffn"])
    base["g0_post_ffn"] = f(inputs["l0_norm_post_ffn"])
    base["g1_pre_mix"] = f(inputs["l1_norm_pre_mix"])
    base["g1_post_mix"] = f(inputs["l1_norm_post_mix"])
    base["g1_pre_ffn"] = f(inputs["l1_norm_pre_ffn"])
    base["g1_post_ffn"] = f(inputs["l1_norm_post_ffn"])
    base.update(host_consts())
    maps = []
    for xc in x_cores:
        m = dict(base)
        m["x"] = xc
        maps.append(m)
    return maps


_PROG = {}


def kernel(**inputs):
    xp = np.asarray(inputs["x_prompt"], dtype=np.float32)
    xs = np.asarray(inputs["x_sample"], dtype=np.float32)
    seqs = [2048] * 4 + [4096] * 2
    x_cores = []
    for c in range(NCORES):
        x_cores.append(np.ascontiguousarray(np.concatenate(
            [xp[4 * c:4 * c + 4].reshape(-1, D), xs[2 * c:2 * c + 2].reshape(-1, D)], axis=0)))
    key = tuple(seqs)
    if key not in _PROG:
        _PROG[key] = build_program(seqs)
    nc = _PROG[key]
    res = run_bass_kernel_spmd(nc, make_in_maps(inputs, x_cores), core_ids=list(range(NCORES)))
    yp = np.empty_like(xp)
    ys = np.empty_like(xs)
    for c in range(NCORES):
        y = np.asarray(res.results[c]["y"], dtype=np.float32)
        yp[4 * c:4 * c + 4] = y[:8192].reshape(4, 2048, D)
        ys[2 * c:2 * c + 2] = y[8192:].reshape(2, 4096, D)
    return (yp, ys)
```

```python
import math
from contextlib import ExitStack

import numpy as np
import ml_dtypes

import concourse.bass as bass
import concourse.mybir as mybir
from concourse.bass_utils import run_bass_kernel_spmd

F32 = mybir.dt.float32
BF16 = mybir.dt.bfloat16
AF = mybir.ActivationFunctionType
ALU = mybir.AluOpType

D = 1024
DFF = 4096
EPS = 1e-6
NCORES = 8
LAM_INIT0 = 0.8 - 0.6 * math.exp(-0.3 * 0)
SLOPES = [2.0 ** (-(h + 1)) for h in range(8)]


class Buf:
    __slots__ = ("name", "w", "r", "excl")

    def __init__(self, name="", excl=False):
        self.name = name
        self.w = None
        self.r = {}
        self.excl = excl


class Sched:
    ENGS = ("pe", "act", "dve", "pool", "sp")

    def __init__(self):
        self.ops = {e: [] for e in self.ENGS}
        self.cnt = {e: 0 for e in self.ENGS}
        self.seen = {e: {} for e in self.ENGS}
        self.dma_cnt = {}
        self.n_dma_sems = 0
        self.free_sems = []
        self.cur_sems = []

    def new_dma_sem(self):
        if self.free_sems:
            k = self.free_sems.pop()
        else:
            k = "dma%d" % self.n_dma_sems
            self.n_dma_sems += 1
            self.dma_cnt[k] = 0
        self.cur_sems.append(k)
        return k

    def _deps(self, eng, reads, writes):
        deps = {}

        def add(tok):
            k, v, src = tok
            if src == eng and eng == "pe":
                return
            if v > deps.get(k, 0):
                deps[k] = v
        for b in reads:
            if b.w is not None:
                add(b.w)
            if b.excl:
                for k, (v, src) in b.r.items():
                    if src != eng:
                        add((k, v, src))
        for b in writes:
            if b.w is not None:
                add(b.w)
            for k, (v, src) in b.r.items():
                if src == eng:
                    continue
                add((k, v, src))
        waits = []
        seen = self.seen[eng]
        for k, v in deps.items():
            if seen.get(k, 0) >= v:
                continue
            seen[k] = v
            waits.append((k, v))
        return waits

    def op(self, eng, fn, reads=(), writes=()):
        waits = self._deps(eng, reads, writes)
        self.cnt[eng] += 1
        tok = (eng, self.cnt[eng], eng)
        self.ops[eng].append((fn, waits, (eng, 1)))
        for b in reads:
            b.r[eng] = (tok[1], eng)
        for b in writes:
            b.w = tok
            b.r = {}
        return tok

    def dma(self, eng, sem_key, fn, reads=(), writes=()):
        waits = self._deps(eng, reads, writes)
        self.dma_cnt[sem_key] += 16
        tok = (sem_key, self.dma_cnt[sem_key], "dma")
        self.ops[eng].append((fn, waits, (sem_key, 16)))
        for b in reads:
            b.r[sem_key] = (tok[1], "dma")
        for b in writes:
            b.w = tok
            b.r = {}
        return tok

    def barrier(self):
        toks = [(e, self.cnt[e]) for e in self.ENGS if self.cnt[e] > 0]
        toks += [(k, v) for k, v in self.dma_cnt.items() if v > 0]
        for e in self.ENGS:
            waits = []
            for k, v in toks:
                if k == e:
                    continue
                if self.seen[e].get(k, 0) >= v:
                    continue
                self.seen[e][k] = v
                waits.append((k, v))
            if waits:
                self.ops[e].append((None, waits, None))
        self.free_sems.extend(self.cur_sems)
        self.cur_sems = []

    def final_wait(self, eng):
        waits = [(k, v) for k, v in self.dma_cnt.items() if v > 0]
        self.ops[eng].append((None, waits, None))

    def emit(self, nc, block, sems):
        handles = {"pe": "tensor", "act": "scalar", "dve": "vector", "pool": "gpsimd", "sp": "sync"}
        for en in self.ENGS:
            ops = self.ops[en]
            if not ops:
                continue

            def body(e, ops=ops):
                for fn, waits, inc in ops:
                    for k, v in waits:
                        e.wait_ge(sems[k], v)
                    if fn is not None:
                        ins = fn(e)
                        if inc is not None:
                            ins.then_inc(sems[inc[0]], inc[1])
            getattr(block, handles[en])(body)


class Tl:
    def __init__(self, t, name, S):
        self.t = t
        self.b = Buf(name)
        self.S = S
        self._sem = None

    @property
    def sem(self):
        if self._sem is None:
            self._sem = self.S.new_dma_sem()
        return self._sem


def build_program(seqs, phases=("cast", "attn", "ffn0", "ret", "ffn1"), debug_out=()):
    T = sum(seqs)
    seq_off = [sum(seqs[:i]) for i in range(len(seqs))]
    nc = bass.Bass("TRN2", target_bir_lowering=False)
    S = Sched()

    def din(name, shape, dt=F32):
        return nc.dram_tensor(name, list(shape), dt, kind="ExternalInput").ap()

    def dint(name, shape, dt):
        return nc.dram_tensor(name, list(shape), dt, kind="Internal").ap()

    x_in = din("x", [T, D])
    y_out = nc.dram_tensor("y", [T, D], F32, kind="ExternalOutput").ap()
    wnames = {"w0in": (D, 3072), "w0out": (D, D), "w0f1": (D, DFF), "w0f2": (DFF, D),
              "w1in": (D, 6144), "w1out": (2048, D), "w1f1": (D, DFF), "w1f2": (DFF, D)}
    wf = {k: din(k, v) for k, v in wnames.items()}
    wb = {k: dint(k + "_b", v, BF16) for k, v in wnames.items()}
    gn = {}
    for l in (0, 1):
        for nm in ("pre_mix", "post_mix", "pre_ffn", "post_ffn"):
            gn[(l, nm)] = din("g%d_%s" % (l, nm), [D])
    lamv = din("lamv", [4, 64])
    subln = din("subln", [128])
    decay = din("decay", [8])
    gnw = din("gnw", [2048])
    gnb = din("gnb", [2048])
    ident_d = din("ident", [128, 128], BF16)
    ones_d = din("ones", [128, 128], BF16)
    abase_d = din("abase", [128, 5, 512])
    rc_d = din("rc", [128, 6, 128])
    cc_d = din("cc", [128, 2])
    cbt_d = din("cbt", [128, 8, 32])
    ka_d = din("ka", [128, 16, 128], BF16)
    qa_d = din("qa", [128, 512], BF16)

    XA = dint("XA", [T, D], F32)
    XB = dint("XB", [T, D], F32)
    KTs = dint("KTs", [8, 128, T], BF16)
    Vs = dint("Vs", [8, 128, T // 128, 128], BF16)
    KS = dint("KS", [T, 1024], BF16)
    VS = dint("VS", [T, 2048], BF16)
    SG = dint("SG", [T, 2048], BF16)
    SBs = dint("SBs", [T // 128, 128, 4096], BF16)
    dbg = {}
    for nm, shape in debug_out:
        dbg[nm] = nc.dram_tensor(nm, list(shape), BF16 if nm.startswith("dump_") else F32, kind="ExternalOutput").ap()

    dramB = {k: Buf(k) for k in ["XA", "XB", "KTs", "Vs", "KS", "VS", "SG", "SBs", "y"] + list(wb)}

    es = ExitStack()
    with es:
        def sb(name, shape, dt=F32):
            return Tl(es.enter_context(nc.sbuf_tensor("s_" + name, list(shape), dt)), name, S)

        ident = sb("ident", [128, 128], BF16)
        ones = sb("ones", [128, 128], BF16)
        pairs = [es.enter_context(nc.psum_tensor("pair%d" % i, [128, 2, 512], F32)) for i in range(2)]
        banks = [Tl(pairs[i // 2][:, i % 2, :], "bank%d" % i, S) for i in range(4)]
        banks += [Tl(es.enter_context(nc.psum_tensor("bank%d" % i, [128, 512], F32)), "bank%d" % i, S)
                  for i in range(4, 8)]
        for bk_ in banks:
            bk_.b.excl = True
        junk = sb("junk", [128, 1024], BF16)
        st = sb("st", [128, 128], F32)
        stB = [Buf("st%d" % i) for i in range(64)]
        stn = [0]

        def tiny(n=1):
            i = stn[0] % 64
            stn[0] += 1
            return st.t[:, 2 * i:2 * i + n], stB[i]

        cq = [0]
        dmaq = ("sp", "act", "pool")

        def ldq():
            cq[0] += 1
            return "sp"

        S.dma("sp", ident.sem, lambda e: e.dma_start(out=ident.t[:], in_=ident_d), writes=[ident.b])
        S.dma("sp", ones.sem, lambda e: e.dma_start(out=ones.t[:], in_=ones_d), writes=[ones.b])

        def cast_weights(keys):
            csem = S.new_dma_sem()
            S.cur_sems.remove(csem)
            ncast = 0
            for k in keys:
                r, c = wnames[k]
                step = 128 * max(1, (1024 * 1024) // (c * 128))
                for r0 in range(0, r, step):
                    r1 = min(r, r0 + step)
                    S.dma("pool", csem,
                          lambda e, k=k, r0=r0, r1=r1: e.dma_start(
                              out=wb[k][r0:r1, :].rearrange("r (a c) -> r a c", c=1024),
                              in_=wf[k][r0:r1, :].rearrange("r (a c) -> r a c", c=1024)),
                          writes=[dramB[k]])
                    ncast += 1
                    if ncast % 8 == 0:
                        S.ops["pool"].append((None, [(csem, S.dma_cnt[csem])], None))
            for k in keys:
                dramB[k].w = (csem, S.dma_cnt[csem], "dma")

        if "cast" in phases:
            cast_weights(["w0in", "w0out"])

        def load_w(dst, dname, r0, nkc, c0, ncols, col_off=0):
            src = wb[dname][r0:r0 + nkc * 128, c0:c0 + ncols].rearrange("(k p) n -> p k n", p=128)
            step = max(1, nkc // 4)
            for k0 in range(0, nkc, step):
                k1 = min(nkc, k0 + step)
                S.dma("sp", dst.sem,
                      lambda e, k0=k0, k1=k1: e.dma_start(out=dst.t[:, k0:k1, col_off:col_off + ncols],
                                                          in_=src[:, k0:k1, :]),
                      reads=[dramB[dname]], writes=[dst.b])

        def load_gain(dst, vec):
            S.dma("sp", dst.sem, lambda e: e.dma_start(out=dst.t[:], in_=vec.partition_broadcast(128)),
                  writes=[dst.b])

        def rstd_from_ss(ss_ap, ss_b, scale):
            rs, rs_b = tiny(1)
            S.op("act", lambda e: e.activation(out=rs, in_=ss_ap, func=AF.Sqrt, bias=EPS, scale=scale),
                 reads=[ss_b], writes=[rs_b])
            rr, rr_b = tiny(1)
            S.op("dve", lambda e: e.reciprocal(out=rr, in_=rs), reads=[rs_b], writes=[rr_b])
            return rr, rr_b

        def norm_T(xt, gt, h, hT, tcol, pbank):
            norm_A(xt, gt, h)
            norm_B(h, hT, tcol, pbank)

        def norm_A(xt, gt, h):
            ss, ss_b = tiny(1)
            S.op("act", lambda e: e.activation(out=junk.t[:], in_=xt.t[:], func=AF.Square, accum_out=ss),
                 reads=[xt.b], writes=[ss_b])
            rr, rr_b = rstd_from_ss(ss, ss_b, 1.0 / D)
            S.op("dve", lambda e: e.scalar_tensor_tensor(out=h.t[:], in0=xt.t[:], scalar=rr, in1=gt.t[:],
                                                          op0=ALU.mult, op1=ALU.mult),
                 reads=[xt.b, rr_b, gt.b], writes=[h.b])

        def norm_B(h, hT, tcol, pbank):
            pT = pbank.t[:].bitcast(BF16)
            for kc in range(8):
                S.op("pe", lambda e, kc=kc: e.transpose(pT[:, kc * 128:(kc + 1) * 128],
                                                         h.t[:, kc * 128:(kc + 1) * 128], ident.t[:]),
                     reads=[h.b, ident.b], writes=[pbank.b])
            S.op("act", lambda e: e.activation(out=hT.t[:, 0:8, tcol:tcol + 128],
                                               in_=pT.rearrange("p (k n) -> p k n", k=8), func=AF.Copy),
                 reads=[pbank.b], writes=[hT.b])

        def post_norm_store(po, xt, gt, ot, dst, dstB, row0):
            ss, ss_b = tiny(2)
            for hf in range(2):
                S.op("act", lambda e, hf=hf: e.activation(out=junk.t[:, 0:512], in_=po[hf].t[:], func=AF.Square,
                                                           accum_out=ss[:, hf:hf + 1]),
                     reads=[po[hf].b], writes=[ss_b])
            s1, s1_b = tiny(1)
            S.op("dve", lambda e: e.tensor_tensor(out=s1, in0=ss[:, 0:1], in1=ss[:, 1:2], op=ALU.add),
                 reads=[ss_b], writes=[s1_b])
            rr, rr_b = rstd_from_ss(s1, s1_b, 1.0 / D)
            for hf in range(2):
                S.op("dve", lambda e, hf=hf: e.scalar_tensor_tensor(
                    out=ot.t[:, hf * 512:(hf + 1) * 512], in0=po[hf].t[:], scalar=rr,
                    in1=gt.t[:, hf * 512:(hf + 1) * 512], op0=ALU.mult, op1=ALU.mult),
                    reads=[po[hf].b, rr_b, gt.b], writes=[ot.b])
            S.op("pool", lambda e: e.tensor_tensor(out=ot.t[:], in0=ot.t[:], in1=xt.t[:], op=ALU.add),
                 reads=[ot.b, xt.b], writes=[ot.b])
            S.dma("sp", ot.sem, lambda e: e.dma_start(out=dst[row0:row0 + 128, :], in_=ot.t[:]),
                  reads=[ot.b], writes=[dstB])

        def ffn_phase(pfx, src, srcB, dst, dstB, w1n, w2n, gpre_v, gpost_v):
            with ExitStack() as ps:
                def sbl(name, shape, dt=F32):
                    return Tl(ps.enter_context(nc.sbuf_tensor("s_" + pfx + name, list(shape), dt)), pfx + name, S)
                w1 = sbl("w1", [128, 8, DFF], BF16)
                w2 = sbl("w2", [128, 32, D], BF16)
                gpre = sbl("gpre", [128, D])
                gpost = sbl("gpost", [128, D])
                load_gain(gpre, gpre_v)
                load_gain(gpost, gpost_v)
                load_w(w1, w1n, 0, 8, 0, DFF)
                load_w(w2, w2n, 0, 32, 0, D)
                xin = [sbl("xin%d" % i, [128, D]) for i in range(6)]
                hb = [sbl("h%d" % i, [128, D], BF16) for i in range(2)]
                hT = [sbl("hT%d" % i, [128, 8, 256], BF16) for i in range(2)]
                hid = sbl("hid", [128, 32, 256], BF16)
                hidB = [Buf("hid%d" % i) for i in range(16)]
                rt = [sbl("rt%d" % i, [128, 512], BF16) for i in range(2)]
                ot = [sbl("ot%d" % i, [128, D]) for i in range(2)]
                NB = T // 256
                pbT, pf, pob = banks[0], banks[1:3], banks[3:7]

                def loadx(b):
                    for t in range(2):
                        xt = xin[(b * 2 + t) % 6]
                        r0 = b * 256 + t * 128
                        S.dma("sp", xt.sem, lambda e, xt=xt, r0=r0: e.dma_start(out=xt.t[:], in_=src[r0:r0 + 128, :]),
                              reads=[srcB], writes=[xt.b])

                def pre(b):
                    for t in range(2):
                        norm_T(xin[(b * 2 + t) % 6], gpre, hb[t], hT[b % 2], t * 128, pbT)

                def main(b, mid=None):
                    hTb = hT[b % 2]
                    for mp in range(16):
                        if mid is not None:
                            nb_ = b + 1
                            if mp == 2:
                                norm_A(xin[(nb_ * 2 + 0) % 6], gpre, hb[0])
                            elif mp == 7:
                                norm_B(hb[0], hT[nb_ % 2], 0, pbT)
                            elif mp == 8:
                                norm_A(xin[(nb_ * 2 + 1) % 6], gpre, hb[1])
                            elif mp == 13:
                                norm_B(hb[1], hT[nb_ % 2], 128, pbT)
                        pfb = pf[mp % 2]
                        for mi in range(2):
                            m = 2 * mp + mi
                            for kc in range(8):
                                S.op("pe", lambda e, m=m, kc=kc, mi=mi, pfb=pfb: e.matmul(
                                    pfb.t[:, mi * 256:(mi + 1) * 256], w1.t[:, kc, m * 128:(m + 1) * 128],
                                    hTb.t[:, kc, :], start=(kc == 0), stop=(kc == 7)),
                                    reads=[w1.b, hTb.b], writes=[pfb.b])
                        r = rt[mp % 2]
                        S.op("act", lambda e, pfb=pfb, r=r: e.activation(out=r.t[:], in_=pfb.t[:], func=AF.Relu),
                             reads=[pfb.b], writes=[r.b])
                        S.op("pool", lambda e, mp=mp, r=r: e.tensor_tensor(
                            out=hid.t[:, 2 * mp:2 * mp + 2, :], in0=r.t[:].rearrange("p (a n) -> p a n", a=2),
                            in1=r.t[:].rearrange("p (a n) -> p a n", a=2), op=ALU.mult),
                            reads=[r.b], writes=[hidB[mp]])
                    for t in range(2):
                        po = pob[(t % 2) * 2:(t % 2) * 2 + 2]
                        for hf in range(2):
                            for kc in range(32):
                                S.op("pe", lambda e, t=t, hf=hf, kc=kc, po=po: e.matmul(
                                    po[hf].t[:], hid.t[:, kc, t * 128:(t + 1) * 128],
                                    w2.t[:, kc, hf * 512:(hf + 1) * 512], start=(kc == 0), stop=(kc == 31)),
                                    reads=[hidB[kc // 2], w2.b], writes=[po[hf].b])
                        xt = xin[(b * 2 + t) % 6]
                        post_norm_store(po, xt, gpost, ot[t], dst, dstB, b * 256 + t * 128)

                loadx(0)
                if NB > 1:
                    loadx(1)
                pre(0)
                for b in range(NB):
                    if b + 2 < NB:
                        loadx(b + 2)
                    main(b, (lambda b=b: pre(b + 1)) if b + 1 < NB else None)

        def attn_kv_phase(src, srcB):
            with ExitStack() as ps:
                def sbl(name, shape, dt=F32):
                    return Tl(ps.enter_context(nc.sbuf_tensor("s_a" + name, list(shape), dt)), "a" + name, S)
                wkv = sbl("wkv", [128, 8, 2048], BF16)
                gpre = sbl("gpre", [128, D])
                load_gain(gpre, gn[(0, "pre_mix")])
                load_w(wkv, "w0in", 0, 8, 1024, 2048)
                xin = [sbl("xin%d" % i, [128, D]) for i in range(8)]
                hb = [sbl("h%d" % i, [128, D], BF16) for i in range(2)]
                hT = [sbl("hT%d" % i, [128, 8, 512], BF16) for i in range(2)]
                ktst = [sbl("ktst%d" % i, [128, 8, 512], BF16) for i in range(2)]
                vst = [sbl("vst%d" % i, [128, 4, D], BF16) for i in range(2)]
                NB = T // 512
                pbT = banks[0]
                pk = banks[1:8]
                pkn = [0]

                def nextbank():
                    pkn[0] += 1
                    return pk[pkn[0] % 7]

                def loadx(b):
                    for t in range(4):
                        xt = xin[(b * 4 + t) % 8]
                        r0 = b * 512 + t * 128
                        S.dma("sp", xt.sem, lambda e, xt=xt, r0=r0: e.dma_start(out=xt.t[:], in_=src[r0:r0 + 128, :]),
                              reads=[srcB], writes=[xt.b])

                def pre(b):
                    loadx(b)
                    for t in range(4):
                        norm_T(xin[(b * 4 + t) % 8], gpre, hb[t % 2], hT[b % 2], t * 128, pbT)

                def mid_norm(b, g):
                    if b + 1 >= NB:
                        return
                    t, ph = g // 4, g % 4
                    if ph == 0:
                        norm_A(xin[((b + 1) * 4 + t) % 8], gpre, hb[t % 2])
                    elif ph == 2:
                        norm_B(hb[t % 2], hT[(b + 1) % 2], t * 128, pbT)

                def main(b):
                    hTb = hT[b % 2]
                    kt = ktst[b % 2]
                    vt = vst[b % 2]
                    for hd in range(8):
                        mid_norm(b, hd)
                        bk = nextbank()
                        for kc in range(8):
                            S.op("pe", lambda e, hd=hd, kc=kc, bk=bk: e.matmul(
                                bk.t[:], wkv.t[:, kc, hd * 128:(hd + 1) * 128], hTb.t[:, kc, :],
                                start=(kc == 0), stop=(kc == 7)), reads=[wkv.b, hTb.b], writes=[bk.b])
                        eng = "act" if hd % 2 == 0 else "dve"
                        if eng == "act":
                            S.op("act", lambda e, hd=hd, bk=bk: e.activation(out=kt.t[:, hd, :], in_=bk.t[:], func=AF.Copy),
                                 reads=[bk.b], writes=[kt.b])
                        else:
                            S.op("dve", lambda e, hd=hd, bk=bk: e.tensor_copy(out=kt.t[:, hd, :], in_=bk.t[:]),
                                 reads=[bk.b], writes=[kt.b])
                    S.dma("sp", kt.sem, lambda e, b=b, kt=kt: e.dma_start(
                        out=KTs[:, :, b * 512:(b + 1) * 512].rearrange("h d t -> d h t"), in_=kt.t[:]),
                        reads=[kt.b], writes=[dramB["KTs"]])
                    for t in range(4):
                        for hf in range(2):
                            mid_norm(b, 8 + t * 2 + hf)
                            bk = nextbank()
                            for kc in range(8):
                                S.op("pe", lambda e, t=t, hf=hf, kc=kc, bk=bk: e.matmul(
                                    bk.t[:], hTb.t[:, kc, t * 128:(t + 1) * 128],
                                    wkv.t[:, kc, 1024 + hf * 512:1024 + (hf + 1) * 512],
                                    start=(kc == 0), stop=(kc == 7)), reads=[wkv.b, hTb.b], writes=[bk.b])
                            if hf == 0:
                                S.op("act", lambda e, t=t, hf=hf, bk=bk: e.activation(
                                    out=vt.t[:, t, hf * 512:(hf + 1) * 512], in_=bk.t[:], func=AF.Copy),
                                    reads=[bk.b], writes=[vt.b])
                            else:
                                S.op("dve", lambda e, t=t, hf=hf, bk=bk: e.tensor_copy(
                                    out=vt.t[:, t, hf * 512:(hf + 1) * 512], in_=bk.t[:]),
                                    reads=[bk.b], writes=[vt.b])
                    for t in range(4):
                        S.dma("sp", vt.sem, lambda e, b=b, vt=vt, t=t: e.dma_start(
                            out=Vs[:, :, b * 4 + t, :].rearrange("h p e -> p h e"),
                            in_=vt.t[:, t, :].rearrange("p (h e) -> p h e", h=8)),
                            reads=[vt.b], writes=[dramB["Vs"]])

                pre(0)
                for b in range(NB):
                    if b + 1 < NB:
                        loadx(b + 1)
                    main(b)

        def attn_q_phase(src, srcB, dst, dstB):
            with ExitStack() as ps:
                def sbl(name, shape, dt=F32):
                    return Tl(ps.enter_context(nc.sbuf_tensor("s_b" + name, list(shape), dt)), "b" + name, S)
                wq = sbl("wq", [128, 8, D], BF16)
                wo = sbl("wo", [128, 8, D], BF16)
                gpre = sbl("gpre", [128, D])
                gpost = sbl("gpost", [128, D])
                abase = sbl("abase", [128, 5, 512])
                kaug = sbl("kaug", [128, 16, 128], BF16)
                qaug = sbl("qaug", [128, 512], BF16)
                S.dma("sp", kaug.sem, lambda e: e.dma_start(out=kaug.t[:], in_=ka_d), writes=[kaug.b])
                S.dma("sp", qaug.sem, lambda e: e.dma_start(out=qaug.t[:], in_=qa_d), writes=[qaug.b])
                lamt = sbl("lamt", [128, 256])
                lprod = sbl("lprod", [128, 128])
                cst = sbl("cst", [128, 8])
                load_gain(gpre, gn[(0, "pre_mix")])
                load_gain(gpost, gn[(0, "post_mix")])
                S.dma("sp", abase.sem, lambda e: e.dma_start(out=abase.t[:], in_=abase_d), writes=[abase.b])
                S.dma("sp", lamt.sem, lambda e: e.dma_start(
                    out=lamt.t[:], in_=lamv.rearrange("a b -> (a b)").partition_broadcast(128)), writes=[lamt.b])
                S.dma("sp", cst.sem, lambda e: e.dma_start(out=cst.t[:, 5:6], in_=subln.rearrange("(p o) -> p o", o=1)),
                      writes=[cst.b])
                load_w(wq, "w0in", 0, 8, 0, 1024)
                load_w(wo, "w0out", 0, 8, 0, 1024)
                S.op("dve", lambda e: e.tensor_tensor(out=lprod.t[:, 0:64], in0=lamt.t[:, 0:64], in1=lamt.t[:, 64:128],
                                                      op=ALU.mult), reads=[lamt.b], writes=[lprod.b])
                S.op("dve", lambda e: e.tensor_tensor(out=lprod.t[:, 64:128], in0=lamt.t[:, 128:192],
                                                      in1=lamt.t[:, 192:256], op=ALU.mult),
                     reads=[lamt.b], writes=[lprod.b])
                for i in range(2):
                    S.op("act", lambda e, i=i: e.activation(out=junk.t[:, 0:64], in_=lprod.t[:, i * 64:(i + 1) * 64],
                                                            func=AF.Identity, accum_out=cst.t[:, i:i + 1]),
                         reads=[lprod.b], writes=[cst.b])
                S.op("act", lambda e: e.activation(out=cst.t[:, 2:4], in_=cst.t[:, 0:2], func=AF.Exp),
                     reads=[cst.b], writes=[cst.b])
                S.op("dve", lambda e: e.tensor_tensor(out=cst.t[:, 4:5], in0=cst.t[:, 3:4], in1=cst.t[:, 2:3],
                                                      op=ALU.subtract), reads=[cst.b], writes=[cst.b])
                S.op("dve", lambda e: e.tensor_scalar(out=cst.t[:, 4:5], in0=cst.t[:, 4:5], scalar1=-LAM_INIT0,
                                                      scalar2=None, op0=ALU.add), reads=[cst.b], writes=[cst.b])
                S.op("dve", lambda e: e.tensor_scalar(out=cst.t[:, 6:7], in0=cst.t[:, 5:6], scalar1=(1.0 - LAM_INIT0),
                                                      scalar2=None, op0=ALU.mult), reads=[cst.b], writes=[cst.b])
                nlam = cst.t[:, 4:5]
                gsc = cst.t[:, 6:7]

                xin = [sbl("xin%d" % i, [128, D]) for i in range(4)]
                xr = [sbl("xr%d" % i, [128, D]) for i in range(2)]
                hb = [sbl("h%d" % i, [128, D], BF16) for i in range(2)]
                hT = sbl("hT", [128, 8, 512], BF16)
                QT = [sbl("QT%d" % i, [128, 8, 512], BF16) for i in range(2)]
                kh = [sbl("kh%d" % i, [128, max(seqs)], BF16) for i in range(2)]
                vh = [sbl("vh%d" % i, [128, max(seqs) // 128, 128], BF16) for i in range(2)]
                scs = [sbl("scs%d" % i, [128, 2, 512]) for i in range(1)]
                EsA = [sbl("EsA%d" % i, [128, 512]) for i in range(2)]
                onesf = sbl("onesf", [128, 128])
                S.op("dve", lambda e: e.memset(onesf.t[:], 1.0), writes=[onesf.b])
                ex = [sbl("ex%d" % i, [128, 2, 512], BF16) for i in range(4)]
                R = [sbl("R%d" % i, [128, 512]) for i in range(2)]
                t01 = [sbl("t01%d" % i, [128, 512]) for i in range(2)]
                osb = [sbl("osb%d" % i, [128, 512]) for i in range(2)]
                sq = [sbl("sq%d" % i, [128, 512], BF16) for i in range(2)]
                rs = sbl("rs", [128, 512])
                rr = sbl("rr", [128, 512])
                aT = [sbl("aT%d" % i, [128, 8, 512], BF16) for i in range(2)]
                ot = [sbl("ot%d" % i, [128, D]) for i in range(2)]
                scb = [banks[0:2], banks[2:4]]
                Ob = banks[4:6]
                Zb = banks[6:8]
                cnt = {"u": 0, "hd": 0}

                def pre(si, qb):
                    tok0 = seq_off[si] + qb * 512
                    for t in range(4):
                        xt = xin[t]
                        r0 = tok0 + t * 128
                        S.dma("sp", xt.sem, lambda e, xt=xt, r0=r0: e.dma_start(out=xt.t[:], in_=src[r0:r0 + 128, :]),
                              reads=[srcB], writes=[xt.b])
                        norm_T(xt, gpre, hb[t % 2], hT, t * 128, banks[6])
                    qt = QT[cnt["q"] % 2]
                    for hd in range(8):
                        bk = banks[7] if hd % 2 == 0 else banks[6]
                        for kc in range(8):
                            S.op("pe", lambda e, hd=hd, kc=kc, bk=bk: e.matmul(
                                bk.t[:], wq.t[:, kc, hd * 128:(hd + 1) * 128], hT.t[:, kc, :],
                                start=(kc == 0), stop=(kc == 7)), reads=[wq.b, hT.b], writes=[bk.b])
                        if hd % 2 == 0:
                            S.op("act", lambda e, hd=hd, bk=bk: e.activation(out=qt.t[:, hd, :], in_=bk.t[:], func=AF.Copy),
                                 reads=[bk.b], writes=[qt.b])
                        else:
                            S.op("dve", lambda e, hd=hd, bk=bk: e.tensor_copy(out=qt.t[:, hd, :], in_=bk.t[:]),
                                 reads=[bk.b], writes=[qt.b])
                    cnt["q"] += 1
                    return qt

                def issue_load(g):
                    if g >= len(headlist):
                        return
                    si, qb, hd = headlist[g]
                    Sq = seqs[si]
                    nkc = Sq // 128
                    k_t, v_t = kh[g % 2], vh[g % 2]
                    S.dma("sp", k_t.sem, lambda e: e.dma_start(
                        out=k_t.t[:, 0:Sq], in_=KTs[hd, :, seq_off[si]:seq_off[si] + Sq]),
                        reads=[dramB["KTs"]], writes=[k_t.b])
                    S.dma("sp", v_t.sem, lambda e: e.dma_start(
                        out=v_t.t[:, 0:nkc, :], in_=Vs[hd, :, seq_off[si] // 128:seq_off[si] // 128 + nkc, :]),
                        reads=[dramB["Vs"]], writes=[v_t.b])

                def claim_bank():
                    u = cnt["u"]
                    cnt["u"] += 1
                    return scb[u % 2][0]

                def sched_pre(pend, nxt):
                    si2, qb2 = nxt
                    tok0 = seq_off[si2] + qb2 * 512
                    qtn = QT[cnt["q"] % 2]
                    cnt["q"] += 1
                    for t in range(4):
                        def fa(t=t):
                            xt = xin[t]
                            r0 = tok0 + t * 128
                            S.dma("sp", xt.sem, lambda e: e.dma_start(out=xt.t[:], in_=src[r0:r0 + 128, :]),
                                  reads=[srcB], writes=[xt.b])
                            norm_A(xt, gpre, hb[t % 2])
                        def fb(t=t):
                            norm_B(hb[t % 2], hT, t * 128, claim_bank())
                        pend.append((2 + 8 * t, fa))
                        pend.append((7 + 8 * t, fb))
                    for hd in range(8):
                        def fq(hd=hd):
                            bk = claim_bank()
                            for kc in range(8):
                                S.op("pe", lambda e, kc=kc: e.matmul(
                                    bk.t[:], wq.t[:, kc, hd * 128:(hd + 1) * 128], hT.t[:, kc, :],
                                    start=(kc == 0), stop=(kc == 7)), reads=[wq.b, hT.b], writes=[bk.b])
                            S.op("dve", lambda e: e.tensor_copy(out=qtn.t[:, hd, :], in_=bk.t[:]),
                                 reads=[bk.b], writes=[qtn.b])
                        pend.append((36 + 3 * hd, fq))
                    return qtn

                def main(si, qb, qt, at, gbase, nxt=None):
                    Sq = seqs[si]
                    nkc = Sq // 128
                    units = [(hd, kc) for hd in range(8) for kc in range(nkc)]
                    DPIPE = 2
                    ets = {}
                    pend = []
                    qtn = sched_pre(pend, nxt) if nxt is not None else None

                    def stageA(ui):
                        hd, kc = units[ui]
                        g = gbase + hd
                        k_t = kh[g % 2]
                        slope = SLOPES[hd]
                        u = cnt["u"]
                        cnt["u"] += 1
                        sb2 = scb[u % 2]
                        sc = scs[u % len(scs)]
                        et = ex[ui % 4]
                        ets[ui] = et
                        n = qb * 4 - kc
                        if n >= 1:
                            bidx, sgn, cb = 0, -8.0 * slope, -slope * 128.0 * n
                        elif n <= -4:
                            bidx, sgn, cb = 0, 8.0 * slope, slope * 128.0 * n
                        else:
                            bidx, sgn, cb = 1 - n, -8.0 * slope, 0.0
                        offdiag = (n >= 1 or n <= -4)
                        for c in range(2):
                            S.op("pe", lambda e, c=c: e.matmul(
                                sb2[c].t[:], k_t.t[c * 64:(c + 1) * 64, kc * 128:(kc + 1) * 128],
                                qt.t[c * 64:(c + 1) * 64, hd, :], start=True, stop=(not offdiag)),
                                reads=[k_t.b, qt.b], writes=[sb2[c].b])
                        if offdiag:
                            aidx = hd * 2 + (0 if n >= 1 else 1)
                            for c in range(2):
                                S.op("pe", lambda e, c=c: e.matmul(
                                    sb2[c].t[:], kaug.t[c * 64:(c + 1) * 64, aidx, :],
                                    qaug.t[c * 64:(c + 1) * 64, :], start=False, stop=True),
                                    reads=[kaug.b, qaug.b], writes=[sb2[c].b])
                            S.op("act", lambda e: e.activation(
                                out=et.t[:], in_=pairs[u % 2][:], func=AF.Exp, bias=float(cb), scale=0.125),
                                reads=[sb2[0].b, sb2[1].b], writes=[et.b])
                        else:
                            for c in range(2):
                                S.op("dve", lambda e, c=c: e.scalar_tensor_tensor(
                                    out=sc.t[:, c, :], in0=abase.t[:, bidx, :], scalar=sgn, in1=sb2[c].t[:],
                                    op0=ALU.mult, op1=ALU.add), reads=[abase.b, sb2[c].b], writes=[sc.b])
                            S.op("act", lambda e: e.activation(
                                out=et.t[:], in_=sc.t[:], func=AF.Exp, bias=float(cb), scale=0.125),
                                reads=[sc.b], writes=[et.b])

                    def stageB(ui, step):
                        hd, kc = units[ui]
                        g = gbase + hd
                        v_t = vh[g % 2]
                        et = ets.pop(ui)
                        for c in range(2):
                            S.op("pe", lambda e, c=c: e.matmul(
                                Ob[c].t[:], v_t.t[:, kc, :], et.t[:, c, :], start=(kc == 0), stop=(kc == nkc - 1)),
                                reads=[v_t.b, et.b], writes=[Ob[c].b])
                        es0 = EsA[hd % 2]
                        if kc == 0:
                            S.op("dve", lambda e: e.tensor_copy(out=es0.t[:], in_=et.t[:, 0, :]),
                                 reads=[et.b], writes=[es0.b])
                        else:
                            S.op("dve", lambda e: e.tensor_tensor(out=es0.t[:], in0=es0.t[:], in1=et.t[:, 0, :], op=ALU.add),
                                 reads=[es0.b, et.b], writes=[es0.b])
                        S.op("pe", lambda e: e.matmul(
                            Zb[1].t[:], ones.t[:], et.t[:, 1, :], start=(kc == 0), stop=(kc == nkc - 1)),
                            reads=[ones.b, et.b], writes=[Zb[1].b])
                        if kc != nkc - 1:
                            return
                        o_t = osb[hd % 2]
                        sq_t = sq[hd % 2]
                        for c in range(2):
                            S.op("dve", lambda e, c=c: e.tensor_copy(out=t01[c].t[:], in_=Ob[c].t[:]),
                                 reads=[Ob[c].b], writes=[t01[c].b])
                        S.op("dve", lambda e: e.tensor_copy(out=R[1].t[:], in_=Zb[1].t[:]), reads=[Zb[1].b], writes=[R[1].b])
                        issue_load(g + 2)

                        def fin1b():
                            S.op("pe", lambda e: e.matmul(Zb[0].t[:], onesf.t[:], es0.t[:], start=True, stop=True),
                                 reads=[onesf.b, es0.b], writes=[Zb[0].b])
                            S.op("act", lambda e: e.activation(out=R[0].t[:], in_=Zb[0].t[:], func=AF.Ln),
                                 reads=[Zb[0].b], writes=[R[0].b])
                            S.op("act", lambda e: e.activation(out=R[1].t[:], in_=R[1].t[:], func=AF.Ln),
                                 reads=[R[1].b], writes=[R[1].b])
                            for c in range(2):
                                S.op("act", lambda e, c=c: e.activation(out=R[c].t[:], in_=R[c].t[:], func=AF.Exp, scale=-1.0),
                                     reads=[R[c].b], writes=[R[c].b])
                                S.op("dve", lambda e, c=c: e.tensor_tensor(out=t01[c].t[:], in0=t01[c].t[:], in1=R[c].t[:],
                                                                           op=ALU.mult),
                                     reads=[t01[c].b, R[c].b], writes=[t01[c].b])
                            S.op("dve", lambda e: e.scalar_tensor_tensor(out=o_t.t[:], in0=t01[1].t[:], scalar=nlam,
                                                                          in1=t01[0].t[:], op0=ALU.mult, op1=ALU.add),
                                 reads=[t01[0].b, t01[1].b, cst.b], writes=[o_t.b])
                            S.op("act", lambda e: e.activation(out=sq_t.t[:], in_=o_t.t[:], func=AF.Square),
                                 reads=[o_t.b], writes=[sq_t.b])
                        pend.append((step + 2, fin1b))

                        def fin2():
                            u = cnt["u"]
                            cnt["u"] += 1
                            bk = scb[u % 2][0]
                            S.op("pe", lambda e: e.matmul(bk.t[:], ones.t[:], sq_t.t[:], start=True, stop=True),
                                 reads=[ones.b, sq_t.b], writes=[bk.b])
                            S.op("act", lambda e: e.activation(out=rs.t[:], in_=bk.t[:], func=AF.Ln, bias=EPS,
                                                               scale=1.0 / 128), reads=[bk.b], writes=[rs.b])
                            S.op("act", lambda e: e.activation(out=rr.t[:], in_=rs.t[:], func=AF.Exp, scale=-0.5),
                                 reads=[rs.b], writes=[rr.b])
                            S.op("dve", lambda e: e.scalar_tensor_tensor(
                                out=at.t[:, hd, :], in0=o_t.t[:], scalar=gsc, in1=rr.t[:], op0=ALU.mult, op1=ALU.mult),
                                reads=[o_t.b, rr.b, cst.b], writes=[at.b])
                        pend.append((step + 7, fin2))

                    nU = len(units)
                    for step in range(nU + DPIPE):
                        if step < nU:
                            stageA(step)
                        if step - DPIPE >= 0:
                            stageB(step - DPIPE, step)
                        pend.sort(key=lambda x: x[0])
                        while pend and pend[0][0] <= step:
                            pend.pop(0)[1]()
                    pend.sort(key=lambda x: x[0])
                    while pend:
                        pend.pop(0)[1]()
                    return qtn

                def post(si, qb, at):
                    tok0 = seq_off[si] + qb * 512
                    for t in range(4):
                        po = scb[t % 2]
                        xt = xr[t % 2]
                        r0 = tok0 + t * 128
                        S.dma("sp", xt.sem, lambda e, xt=xt, r0=r0: e.dma_start(out=xt.t[:], in_=src[r0:r0 + 128, :]),
                              reads=[srcB], writes=[xt.b])
                        for hf in range(2):
                            for hd in range(8):
                                S.op("pe", lambda e, t=t, hf=hf, hd=hd, po=po: e.matmul(
                                    po[hf].t[:], at.t[:, hd, t * 128:(t + 1) * 128],
                                    wo.t[:, hd, hf * 512:(hf + 1) * 512], start=(hd == 0), stop=(hd == 7)),
                                    reads=[at.b, wo.b], writes=[po[hf].b])
                        post_norm_store(po, xt, gpost, ot[t % 2], dst, dstB, r0)

                cnt["q"] = 0
                blocks = [(si, qb) for si in range(len(seqs)) for qb in range(seqs[si] // 512)]
                headlist = [(si, qb, hd) for (si, qb) in blocks for hd in range(8)]
                issue_load(0)
                issue_load(1)
                qts = {0: pre(*blocks[0])}
                for i, (si, qb) in enumerate(blocks):
                    at = aT[i % 2]
                    qts[i + 1] = main(si, qb, qts[i], at, i * 8, blocks[i + 1] if i + 1 < len(blocks) else None)
                    post(si, qb, at)

        def ret_phases(src, srcB, dst, dstB):
            bkn = [0]

            def nb():
                bkn[0] += 1
                return banks[bkn[0] % 8]

            rtb = sb("rtb", [128, 40])
            ccs = sb("ccs", [128, 2])
            S.dma("sp", rtb.sem, lambda e: e.dma_start(out=rtb.t[:, 0:8], in_=decay.partition_broadcast(128)),
                  writes=[rtb.b])
            S.dma("sp", ccs.sem, lambda e: e.dma_start(out=ccs.t[:], in_=cc_d), writes=[ccs.b])
            S.op("act", lambda e: e.activation(out=rtb.t[:, 8:16], in_=rtb.t[:, 0:8], func=AF.Exp),
                 reads=[rtb.b], writes=[rtb.b])
            S.op("act", lambda e: e.activation(out=rtb.t[:, 16:24], in_=rtb.t[:, 8:16], func=AF.Ln, bias=1.0, scale=-1.0),
                 reads=[rtb.b], writes=[rtb.b])
            S.op("act", lambda e: e.activation(out=rtb.t[:, 24:28], in_=rtb.t[:, 16:20], func=AF.Exp, scale=ccs.t[:, 0:1]),
                 reads=[rtb.b, ccs.b], writes=[rtb.b])
            S.op("act", lambda e: e.activation(out=rtb.t[:, 28:32], in_=rtb.t[:, 20:24], func=AF.Exp, scale=ccs.t[:, 1:2]),
                 reads=[rtb.b, ccs.b], writes=[rtb.b])
            S.op("act", lambda e: e.activation(out=rtb.t[:, 32:40], in_=rtb.t[:, 16:24], func=AF.Exp, scale=128.0),
                 reads=[rtb.b], writes=[rtb.b])
            if "ret_setup_only" in phases:
                return
            lg = lambda i: rtb.t[:, 16 + i:17 + i]
            gC = lambda i: rtb.t[:, 32 + i:33 + i]

            with ExitStack() as ps:
                def sbl(name, shape, dt=F32):
                    return Tl(ps.enter_context(nc.sbuf_tensor("s_r" + name, list(shape), dt)), "r" + name, S)
                wk = sbl("wkv", [128, 8, 3072], BF16)
                wg = sbl("wg", [128, 8, 2048], BF16)
                gpre = sbl("gpre", [128, D])
                load_gain(gpre, gn[(1, "pre_mix")])
                load_w(wk, "w1in", 0, 8, 1024, 3072)
                load_w(wg, "w1in", 0, 8, 4096, 2048)
                Sb = sbl("Sb", [128, 8, 512])
                Sbf = [sbl("Sbf%d" % i, [128, 8, 512], BF16) for i in range(2)]
                xin = [sbl("xin%d" % i, [128, D]) for i in range(3)]
                hb = sbl("h", [128, D], BF16)
                hT = [sbl("hT%d" % i, [128, 8, 128], BF16) for i in range(2)]
                kraw = [sbl("kraw%d" % i, [128, 1024], BF16) for i in range(2)]
                kb = sbl("kb", [128, 1024], BF16)
                vv = [sbl("v%d" % i, [128, 2048], BF16) for i in range(2)]
                sg = [sbl("sg%d" % i, [128, 2048], BF16) for i in range(2)]
                chunks = []
                for si in range(len(seqs)):
                    n_ = seqs[si] // 128
                    for n in range(n_ - 1, -1, -1):
                        chunks.append((si, n, n == n_ - 1))

                def loadx(i):
                    si, n, first = chunks[i]
                    r0 = seq_off[si] + n * 128
                    xt = xin[i % 3]
                    S.dma("sp", xt.sem, lambda e: e.dma_start(out=xt.t[:], in_=src[r0:r0 + 128, :]),
                          reads=[srcB], writes=[xt.b])

                def pre(i):
                    norm_T(xin[i % 3], gpre, hb, hT[i % 2], 0, nb())

                def main(i, mid=None):
                    si, n, first = chunks[i]
                    r0 = seq_off[si] + n * 128
                    cg = r0 // 128
                    hTi = hT[i % 2]
                    kr, v_, sg_ = kraw[i % 2], vv[i % 2], sg[i % 2]
                    for grp in range(10):
                        if mid is not None and grp == 2:
                            norm_A(xin[(i + 1) % 3], gpre, hb)
                        if mid is not None and grp == 7:
                            norm_B(hb, hT[(i + 1) % 2], 0, nb())
                        bk = nb()
                        for kc in range(8):
                            wsrc, gcol = (wk, grp) if grp < 6 else (wg, grp - 6)
                            S.op("pe", lambda e, gcol=gcol, kc=kc, bk=bk, wsrc=wsrc: e.matmul(
                                bk.t[:], hTi.t[:, kc, :], wsrc.t[:, kc, gcol * 512:(gcol + 1) * 512],
                                start=(kc == 0), stop=(kc == 7)), reads=[hTi.b, wsrc.b], writes=[bk.b])
                        if grp < 2:
                            S.op("act", lambda e, grp=grp, bk=bk: e.activation(
                                out=kr.t[:, grp * 512:(grp + 1) * 512], in_=bk.t[:], func=AF.Copy),
                                reads=[bk.b], writes=[kr.b])
                            for hh in range(2):
                                h_ = grp * 2 + hh
                                S.op("act", lambda e, grp=grp, bk=bk, hh=hh, h_=h_: e.activation(
                                    out=kb.t[:, h_ * 256:(h_ + 1) * 256], in_=bk.t[:, hh * 256:(hh + 1) * 256],
                                    func=AF.Copy, scale=rtb.t[:, 28 + h_:29 + h_]),
                                    reads=[bk.b, rtb.b], writes=[kb.b])
                        elif grp < 6:
                            g2 = grp - 2
                            S.op("dve", lambda e, g2=g2, bk=bk: e.tensor_copy(out=v_.t[:, g2 * 512:(g2 + 1) * 512], in_=bk.t[:]),
                                 reads=[bk.b], writes=[v_.b])
                        else:
                            g2 = grp - 6
                            S.op("act", lambda e, g2=g2, bk=bk: e.activation(
                                out=sg_.t[:, g2 * 512:(g2 + 1) * 512], in_=bk.t[:], func=AF.Silu),
                                reads=[bk.b], writes=[sg_.b])
                    S.dma("sp", kr.sem, lambda e: e.dma_start(out=KS[r0:r0 + 128, :], in_=kr.t[:]),
                          reads=[kr.b], writes=[dramB["KS"]])
                    S.dma("sp", v_.sem, lambda e: e.dma_start(out=VS[r0:r0 + 128, :], in_=v_.t[:]),
                          reads=[v_.b], writes=[dramB["VS"]])
                    S.dma("sp", sg_.sem, lambda e: e.dma_start(out=SG[r0:r0 + 128, :], in_=sg_.t[:]),
                          reads=[sg_.b], writes=[dramB["SG"]])
                    if first:
                        S.op("dve", lambda e: e.memset(Sb.t[:].rearrange("p a b -> p (a b)"), 0.0), writes=[Sb.b])
                    sbf = Sbf[i % 2]
                    for q4 in range(4):
                        S.op("pool", lambda e, q4=q4: e.tensor_copy(out=sbf.t[:, 2 * q4:2 * q4 + 2, :],
                                                                    in_=Sb.t[:, 2 * q4:2 * q4 + 2, :]),
                             reads=[Sb.b], writes=[sbf.b])
                    S.dma("sp", sbf.sem, lambda e: e.dma_start(out=SBs[cg], in_=sbf.t[:].rearrange("p a b -> p (a b)")),
                          reads=[sbf.b], writes=[dramB["SBs"]])
                    for h_ in range(4):
                        for dc in range(2):
                            bk = nb()
                            S.op("pe", lambda e, h_=h_, dc=dc, bk=bk: e.matmul(
                                bk.t[:], kb.t[:, h_ * 256 + dc * 128:h_ * 256 + (dc + 1) * 128],
                                v_.t[:, h_ * 512:(h_ + 1) * 512], start=True, stop=True),
                                reads=[kb.b, v_.b], writes=[bk.b])
                            S.op("dve", lambda e, h_=h_, dc=dc, bk=bk: e.scalar_tensor_tensor(
                                out=Sb.t[:, h_ * 2 + dc, :], in0=Sb.t[:, h_ * 2 + dc, :], scalar=gC(4 + h_),
                                in1=bk.t[:], op0=ALU.mult, op1=ALU.add),
                                reads=[Sb.b, bk.b, rtb.b], writes=[Sb.b])

                loadx(0)
                if len(chunks) > 1:
                    loadx(1)
                pre(0)
                for i in range(len(chunks)):
                    if i + 2 < len(chunks):
                        loadx(i + 2)
                    main(i, (lambda i=i: pre(i + 1)) if i + 1 < len(chunks) else None)
            S.barrier()
            if "reta_only" in phases:
                return

            with ExitStack() as ps:
                def sbl(name, shape, dt=F32):
                    return Tl(ps.enter_context(nc.sbuf_tensor("s_q" + name, list(shape), dt)), "q" + name, S)
                wq = sbl("wq", [128, 8, 1024], BF16)
                wo = sbl("wo", [128, 16, 1024], BF16)
                gpre = sbl("gpre", [128, D])
                gpost = sbl("gpost", [128, D])
                GW = sbl("GW", [128, 2048])
                GB = sbl("GB", [128, 2048])
                rc = sbl("rc", [128, 6, 128])
                TfT = sbl("TfT", [128, 8, 128])
                TbT = sbl("TbT", [128, 8, 128])
                MT = sbl("MT", [128, 4, 128])
                mtmp = sbl("mtmp", [128, 128])
                load_gain(gpre, gn[(1, "pre_mix")])
                load_gain(gpost, gn[(1, "post_mix")])
                load_gain(GW, gnw)
                load_gain(GB, gnb)
                S.dma("sp", rc.sem, lambda e: e.dma_start(out=rc.t[:], in_=rc_d), writes=[rc.b])
                load_w(wq, "w1in", 0, 8, 0, 1024)
                load_w(wo, "w1out", 0, 16, 0, 1024)
                for kc in range(8):
                    h_ = kc // 2
                    S.op("act", lambda e, kc=kc, h_=h_: e.activation(out=TfT.t[:, kc, :], in_=rc.t[:, 0, :], func=AF.Exp,
                                                                      scale=lg(h_)), reads=[rc.b, rtb.b], writes=[TfT.b])
                    S.op("act", lambda e, kc=kc, h_=h_: e.activation(out=TbT.t[:, kc, :], in_=rc.t[:, 1, :], func=AF.Exp,
                                                                      scale=lg(4 + h_)), reads=[rc.b, rtb.b], writes=[TbT.b])
                for h_ in range(4):
                    S.op("act", lambda e, h_=h_: e.activation(out=mtmp.t[:], in_=rc.t[:, 2, :], func=AF.Exp, scale=lg(h_)),
                         reads=[rc.b, rtb.b], writes=[mtmp.b])
                    S.op("dve", lambda e, h_=h_: e.tensor_tensor(out=MT.t[:, h_, :], in0=mtmp.t[:], in1=rc.t[:, 3, :],
                                                                  op=ALU.mult), reads=[mtmp.b, rc.b], writes=[MT.b])
                    S.op("act", lambda e, h_=h_: e.activation(out=mtmp.t[:], in_=rc.t[:, 4, :], func=AF.Exp, scale=lg(4 + h_)),
                         reads=[rc.b, rtb.b], writes=[mtmp.b])
                    S.op("dve", lambda e, h_=h_: e.tensor_tensor(out=mtmp.t[:], in0=mtmp.t[:], in1=rc.t[:, 5, :],
                                                                  op=ALU.mult), reads=[mtmp.b, rc.b], writes=[mtmp.b])
                    S.op("dve", lambda e, h_=h_: e.tensor_tensor(out=MT.t[:, h_, :], in0=MT.t[:, h_, :], in1=mtmp.t[:],
                                                                  op=ALU.add), reads=[mtmp.b, MT.b], writes=[MT.b])
                Sf = sbl("Sf", [128, 8, 512])
                Sff = sbl("Sff", [128, 8, 512], BF16)
                Sbn = sbl("Sbn", [128, 8, 512], BF16)
                xin = [sbl("xin%d" % i, [128, D]) for i in range(3)]
                hb = sbl("h", [128, D], BF16)
                hT = sbl("hT", [128, 8, 128], BF16)
                kraw = [sbl("kraw%d" % i, [128, 1024], BF16) for i in range(2)]
                vv = [sbl("v%d" % i, [128, 2048], BF16) for i in range(2)]
                sg = [sbl("sg%d" % i, [128, 2048], BF16) for i in range(3)]
                kf = sbl("kf", [128, 1024], BF16)
                qtok = sbl("qtok", [128, 1024], BF16)
                qT = sbl("qT", [128, 8, 128], BF16)
                qfT = sbl("qfT", [128, 8, 128], BF16)
                qbT = sbl("qbT", [128, 8, 128], BF16)
                kT = sbl("kT", [128, 8, 128], BF16)
                Am = sbl("Am", [128, 4, 128], BF16)
                Us = [sbl("U%d" % i, [128, 2048]) for i in range(2)]
                z = sbl("z", [128, 2048], BF16)
                zT = sbl("zT", [128, 16, 128], BF16)
                ot = [sbl("ot%d" % i, [128, D]) for i in range(2)]
                bst = sbl("bst", [128, 4, 6])
                mv = sbl("mv", [128, 4, 2])
                gst = sbl("gst", [128, 12])
                chunks = [(si, n) for si in range(len(seqs)) for n in range(seqs[si] // 128)]
                b4n = [0]

                def nb():
                    b4n[0] += 1
                    return banks[b4n[0] % 4]
                ybs = {}

                def pre(i):
                    si, n = chunks[i]
                    r0 = seq_off[si] + n * 128
                    xt = xin[i % 3]
                    S.dma("sp", xt.sem, lambda e: e.dma_start(out=xt.t[:], in_=src[r0:r0 + 128, :]),
                          reads=[srcB], writes=[xt.b])
                    kr, v_, sg_ = kraw[i % 2], vv[i % 2], sg[i % 3]
                    S.dma("sp", kr.sem, lambda e: e.dma_start(out=kr.t[:], in_=KS[r0:r0 + 128, :]),
                          reads=[dramB["KS"]], writes=[kr.b])
                    S.dma("sp", v_.sem, lambda e: e.dma_start(out=v_.t[:], in_=VS[r0:r0 + 128, :]),
                          reads=[dramB["VS"]], writes=[v_.b])
                    S.dma("sp", sg_.sem, lambda e: e.dma_start(out=sg_.t[:], in_=SG[r0:r0 + 128, :]),
                          reads=[dramB["SG"]], writes=[sg_.b])

                def main(i):
                    si, n = chunks[i]
                    r0 = seq_off[si] + n * 128
                    cg = r0 // 128
                    xt = xin[i % 3]
                    U = Us[i % 2]
                    kr, v_, sg_ = kraw[i % 2], vv[i % 2], sg[i % 3]
                    norm_B(hb, hT, 0, nb())
                    S.dma("sp", Sbn.sem, lambda e: e.dma_start(out=Sbn.t[:].rearrange("p a b -> p (a b)"), in_=SBs[cg]),
                          reads=[dramB["SBs"]], writes=[Sbn.b])
                    if n == 0:
                        S.op("dve", lambda e: e.memset(Sf.t[:].rearrange("p a b -> p (a b)"), 0.0), writes=[Sf.b])
                        S.op("dve", lambda e: e.memset(Sff.t[:].rearrange("p a b -> p (a b)"), 0.0), writes=[Sff.b])
                    for grp in range(2):
                        bk = nb()
                        for kc in range(8):
                            S.op("pe", lambda e, grp=grp, kc=kc, bk=bk: e.matmul(
                                bk.t[:], hT.t[:, kc, :], wq.t[:, kc, grp * 512:(grp + 1) * 512],
                                start=(kc == 0), stop=(kc == 7)), reads=[hT.b, wq.b], writes=[bk.b])
                        S.op("act", lambda e, grp=grp, bk=bk: e.activation(
                            out=qtok.t[:, grp * 512:(grp + 1) * 512], in_=bk.t[:], func=AF.Copy, scale=1.0 / 16),
                            reads=[bk.b], writes=[qtok.b])
                    bq = nb()
                    pq = bq.t[:].bitcast(BF16)
                    for kc in range(8):
                        S.op("pe", lambda e, kc=kc: e.transpose(pq[:, kc * 128:(kc + 1) * 128],
                                                                 qtok.t[:, kc * 128:(kc + 1) * 128], ident.t[:]),
                             reads=[qtok.b, ident.b], writes=[bq.b])
                    pq3 = pq.rearrange("p (k n) -> p k n", k=8)
                    S.op("act", lambda e: e.activation(out=qT.t[:], in_=pq3, func=AF.Copy), reads=[bq.b], writes=[qT.b])
                    S.op("dve", lambda e: e.tensor_tensor(out=qfT.t[:], in0=qT.t[:], in1=TfT.t[:], op=ALU.mult),
                         reads=[qT.b, TfT.b], writes=[qfT.b])
                    S.op("dve", lambda e: e.tensor_tensor(out=qbT.t[:], in0=qT.t[:], in1=TbT.t[:], op=ALU.mult),
                         reads=[qT.b, TbT.b], writes=[qbT.b])
                    bkk = nb()
                    pk_ = bkk.t[:].bitcast(BF16)
                    for kc in range(8):
                        S.op("pe", lambda e, kc=kc: e.transpose(pk_[:, kc * 128:(kc + 1) * 128],
                                                                 kr.t[:, kc * 128:(kc + 1) * 128], ident.t[:]),
                             reads=[kr.b, ident.b], writes=[bkk.b])
                    S.op("act", lambda e: e.activation(out=kT.t[:], in_=pk_.rearrange("p (k n) -> p k n", k=8), func=AF.Copy),
                         reads=[bkk.b], writes=[kT.b])
                    for h_ in range(4):
                        S.op("act", lambda e, h_=h_: e.activation(
                            out=kf.t[:, h_ * 256:(h_ + 1) * 256], in_=kr.t[:, h_ * 256:(h_ + 1) * 256],
                            func=AF.Copy, scale=rtb.t[:, 24 + h_:25 + h_]),
                            reads=[kr.b, rtb.b], writes=[kf.b])
                    ba = nb()
                    for h_ in range(4):
                        for dc in range(2):
                            S.op("pe", lambda e, h_=h_, dc=dc: e.matmul(
                                ba.t[:, h_ * 128:(h_ + 1) * 128], kT.t[:, h_ * 2 + dc, :], qT.t[:, h_ * 2 + dc, :],
                                start=(dc == 0), stop=(dc == 1)), reads=[kT.b, qT.b], writes=[ba.b])
                    S.op("dve", lambda e: e.tensor_tensor(out=Am.t[:], in0=ba.t[:].rearrange("p (h n) -> p h n", h=4),
                                                          in1=MT.t[:], op=ALU.mult), reads=[ba.b, MT.b], writes=[Am.b])
                    yb = []
                    ybs[i] = yb
                    for h_ in range(4):
                        bk = banks[4 + h_]
                        yb.append(bk)
                        S.op("pe", lambda e, h_=h_, bk=bk: e.matmul(bk.t[:], Am.t[:, h_, :], v_.t[:, h_ * 512:(h_ + 1) * 512],
                                                                   start=True, stop=False),
                             reads=[Am.b, v_.b], writes=[bk.b])
                        for dc in range(2):
                            S.op("pe", lambda e, h_=h_, dc=dc, bk=bk: e.matmul(
                                bk.t[:], qfT.t[:, h_ * 2 + dc, :], Sff.t[:, h_ * 2 + dc, :], start=False, stop=False),
                                reads=[qfT.b, Sff.b], writes=[bk.b])
                        for dc in range(2):
                            S.op("pe", lambda e, h_=h_, dc=dc, bk=bk: e.matmul(
                                bk.t[:], qbT.t[:, h_ * 2 + dc, :], Sbn.t[:, h_ * 2 + dc, :], start=False, stop=(dc == 1)),
                                reads=[qbT.b, Sbn.b], writes=[bk.b])

                def part2(i):
                    si, n = chunks[i]
                    U = Us[i % 2]
                    kr, v_, sg_ = kraw[i % 2], vv[i % 2], sg[i % 3]
                    yb = ybs.pop(i)
                    for h_ in range(4):
                        bk = yb[h_]
                        S.op("dve", lambda e, h_=h_, bk=bk: e.bn_stats(out=bst.t[:, h_, :], in_=bk.t[:]),
                             reads=[bk.b], writes=[bst.b])
                        S.op("dve", lambda e, h_=h_: e.bn_aggr(out=mv.t[:, h_, :], in_=bst.t[:, h_, :]),
                             reads=[bst.b], writes=[mv.b])
                    S.op("act", lambda e: e.activation(out=gst.t[:, 0:4], in_=mv.t[:, :, 1], func=AF.Sqrt, bias=EPS, scale=1.0),
                         reads=[mv.b], writes=[gst.b])
                    S.op("dve", lambda e: e.reciprocal(out=gst.t[:, 4:8], in_=gst.t[:, 0:4]), reads=[gst.b], writes=[gst.b])
                    S.op("dve", lambda e: e.scalar_tensor_tensor(out=gst.t[:, 8:12], in0=mv.t[:, :, 0], scalar=-1.0,
                                                                  in1=gst.t[:, 4:8], op0=ALU.mult, op1=ALU.mult),
                         reads=[gst.b, mv.b], writes=[gst.b])
                    for h_ in range(4):
                        S.op("act", lambda e, h_=h_: e.activation(
                            out=U.t[:, h_ * 512:(h_ + 1) * 512], in_=yb[h_].t[:], func=AF.Identity,
                            bias=gst.t[:, 8 + h_:9 + h_], scale=gst.t[:, 4 + h_:5 + h_]),
                            reads=[yb[h_].b, gst.b], writes=[U.b])
                    for h_ in range(4):
                        for dc in range(2):
                            bk = nb()
                            S.op("pe", lambda e, h_=h_, dc=dc, bk=bk: e.matmul(
                                bk.t[:], kf.t[:, h_ * 256 + dc * 128:h_ * 256 + (dc + 1) * 128],
                                v_.t[:, h_ * 512:(h_ + 1) * 512], start=True, stop=True),
                                reads=[kf.b, v_.b], writes=[bk.b])
                            S.op("dve", lambda e, h_=h_, dc=dc, bk=bk: e.scalar_tensor_tensor(
                                out=Sf.t[:, h_ * 2 + dc, :], in0=Sf.t[:, h_ * 2 + dc, :], scalar=gC(h_),
                                in1=bk.t[:], op0=ALU.mult, op1=ALU.add),
                                reads=[Sf.b, bk.b, rtb.b], writes=[Sf.b])
                    S.op("act", lambda e: e.activation(out=Sff.t[:].rearrange("p a b -> p (a b)"),
                                                       in_=Sf.t[:].rearrange("p a b -> p (a b)"), func=AF.Copy),
                         reads=[Sf.b], writes=[Sff.b])
                def mainB(i):
                    si, n = chunks[i]
                    r0 = seq_off[si] + n * 128
                    xt = xin[i % 3]
                    sg_ = sg[i % 3]
                    U = Us[i % 2]
                    for hf in range(2):
                        sl = slice(hf * 1024, (hf + 1) * 1024)
                        S.op("dve", lambda e, sl=sl: e.tensor_tensor(out=U.t[:, sl], in0=U.t[:, sl], in1=GW.t[:, sl], op=ALU.mult),
                             reads=[U.b, GW.b], writes=[U.b])
                        S.op("dve", lambda e, sl=sl: e.tensor_tensor(out=U.t[:, sl], in0=U.t[:, sl], in1=GB.t[:, sl], op=ALU.add),
                             reads=[U.b, GB.b], writes=[U.b])
                    S.op("dve", lambda e: e.tensor_tensor(out=z.t[:], in0=U.t[:], in1=sg_.t[:], op=ALU.mult),
                         reads=[U.b, sg_.b], writes=[z.b])

                def mainB_pe(i):
                    si, n = chunks[i]
                    r0 = seq_off[si] + n * 128
                    xt = xin[i % 3]
                    for half in range(2):
                        bz = nb()
                        pz = bz.t[:].bitcast(BF16)
                        for kc in range(8):
                            S.op("pe", lambda e, kc=kc, half=half, pz=pz, bz=bz: e.transpose(
                                pz[:, kc * 128:(kc + 1) * 128],
                                z.t[:, (half * 8 + kc) * 128:(half * 8 + kc + 1) * 128], ident.t[:]),
                                reads=[z.b, ident.b], writes=[bz.b])
                        S.op("act", lambda e, half=half, pz=pz, bz=bz: e.activation(
                            out=zT.t[:, half * 8:(half + 1) * 8, :], in_=pz.rearrange("p (k n) -> p k n", k=8), func=AF.Copy),
                            reads=[bz.b], writes=[zT.b])
                    po = [nb(), nb()]
                    for hf in range(2):
                        for kc in range(16):
                            S.op("pe", lambda e, hf=hf, kc=kc: e.matmul(
                                po[hf].t[:], zT.t[:, kc, :], wo.t[:, kc, hf * 512:(hf + 1) * 512],
                                start=(kc == 0), stop=(kc == 15)), reads=[zT.b, wo.b], writes=[po[hf].b])
                    post_norm_store(po, xt, gpost, ot[i % 2], dst, dstB, r0)

                pre(0)
                if len(chunks) > 1:
                    pre(1)
                norm_A(xin[0], gpre, hb)
                main(0)
                if len(chunks) > 1:
                    norm_A(xin[1], gpre, hb)
                part2(0)
                for i in range(len(chunks)):
                    if i + 2 < len(chunks):
                        pre(i + 2)
                    mainB(i)
                    if i + 1 < len(chunks):
                        main(i + 1)
                    if i + 2 < len(chunks):
                        norm_A(xin[(i + 2) % 3], gpre, hb)
                    mainB_pe(i)
                    if i + 1 < len(chunks):
                        part2(i + 1)

        cur, curB = x_in, Buf("x")
        S.barrier()
        if "cast" in phases:
            cast_weights(["w0f1", "w0f2", "w1in", "w1out", "w1f1", "w1f2"])
        if "attnkv" in phases:
            attn_kv_phase(cur, curB)
        if "attn" in phases:
            attn_kv_phase(cur, curB)
            S.barrier()
            nxt = XA if len(phases) > 2 else y_out
            nxtB = dramB["XA"] if nxt is XA else dramB["y"]
            attn_q_phase(cur, curB, nxt, nxtB)
            S.barrier()
            cur, curB = nxt, nxtB
        if "ffn0" in phases:
            nxt = XB if ("ret" in phases or "ffn1" in phases) else y_out
            nxtB = dramB["XB"] if nxt is XB else dramB["y"]
            ffn_phase("f0", cur, curB, nxt, nxtB, "w0f1", "w0f2", gn[(0, "pre_ffn")], gn[(0, "post_ffn")])
            S.barrier()
            cur, curB = nxt, nxtB
        if "ret" in phases:
            nxt = XA if "ffn1" in phases else y_out
            nxtB = dramB["XA"] if nxt is XA else dramB["y"]
            ret_phases(cur, curB, nxt, nxtB)
            S.barrier()
            cur, curB = nxt, nxtB
        if "ffn1" in phases:
            ffn_phase("f1", cur, curB, y_out, dramB["y"], "w1f1", "w1f2", gn[(1, "pre_ffn")], gn[(1, "post_ffn")])

        S.barrier()
        scratch = {"KTs": KTs, "Vs": Vs, "KS": KS, "VS": VS, "SG": SG, "SBs": SBs}
        for nm in dbg:
            if nm.startswith("dump_"):
                dsem = S.new_dma_sem()
                S.dma("sp", dsem, lambda e, nm=nm: e.dma_start(out=dbg[nm], in_=scratch[nm[5:]]), writes=[Buf()])
        S.final_wait("sp")
        sems = {}
        for k in list(S.ENGS) + list(S.dma_cnt.keys()):
            sems[k] = es.enter_context(nc.semaphore(k))
        block = es.enter_context(nc.Block())
        S.emit(nc, block, sems)
    return nc


def host_consts():
    c = {}
    c["ident"] = np.eye(128, dtype=np.float32).astype(ml_dtypes.bfloat16)
    c["ones"] = np.ones((128, 128), dtype=np.float32).astype(ml_dtypes.bfloat16)
    kj = np.arange(128, dtype=np.float32)[:, None]
    qi = np.arange(512, dtype=np.float32)[None, :]
    ab = np.zeros((128, 5, 512), np.float32)
    ab[:, 0, :] = qi - kj
    for j in range(4):
        ab[:, 1 + j, :] = np.abs(qi - kj - 128.0 * j)
    c["abase"] = ab
    i = np.arange(128, dtype=np.float32)
    rc = np.zeros((128, 6, 128), np.float32)
    rc[:, 0, :] = (i + 1.0)[None, :]
    rc[:, 1, :] = (128.0 - i)[None, :]
    dif = i[None, :] - i[:, None]
    rc[:, 2, :] = np.maximum(dif, 0.0)
    rc[:, 3, :] = (dif >= 0).astype(np.float32)
    rc[:, 4, :] = np.maximum(-dif, 0.0)
    rc[:, 5, :] = (dif <= 0).astype(np.float32)
    c["rc"] = rc
    cc = np.zeros((128, 2), np.float32)
    cc[:, 0] = 127.0 - i
    cc[:, 1] = i
    c["cc"] = cc
    ka = np.zeros((128, 16, 128), np.float32)
    qa = np.zeros((128, 512), np.float32)
    kjv = np.arange(128, dtype=np.float32)
    qiv = np.arange(512, dtype=np.float32)
    for base in (0, 64):
        qa[base + 0] = 1.0
        qa[base + 1] = 2.0 * np.floor(qiv / 2.0)
        qa[base + 2] = qiv - 2.0 * np.floor(qiv / 2.0)
        for h in range(8):
            for side, sg_ in ((0, 1.0), (1, -1.0)):
                ka[base + 0, h * 2 + side] = sg_ * 8.0 * SLOPES[h] * kjv
                ka[base + 1, h * 2 + side] = -sg_ * 8.0 * SLOPES[h]
                ka[base + 2, h * 2 + side] = -sg_ * 8.0 * SLOPES[h]
    c["ka"] = ka.astype(ml_dtypes.bfloat16)
    c["qa"] = qa.astype(ml_dtypes.bfloat16)
    cbt = np.zeros((128, 8, 32), np.float32)
    for h in range(8):
        cbt[:, h, :] = -SLOPES[h] * 128.0 * np.arange(32, dtype=np.float32)[None, :]
    c["cbt"] = cbt
    return c


def make_in_maps(inputs, x_cores):
    f = lambda a: np.ascontiguousarray(np.asarray(a, dtype=np.float32))
    base = {
        "w0in": f(inputs["l0_da_w_in"]), "w0out": f(inputs["l0_da_w_out"]),
        "w0f1": f(inputs["l0_ffn_w1"]), "w0f2": f(inputs["l0_ffn_w2"]),
        "w1in": f(inputs["l1_ret_w_in"]), "w1out": f(inputs["l1_ret_w_out"]),
        "w1f1": f(inputs["l1_ffn_w1"]), "w1f2": f(inputs["l1_ffn_w2"]),
        "lamv": np.stack([f(inputs["l0_da_lambda_q1"]), f(inputs["l0_da_lambda_k1"]),
                          f(inputs["l0_da_lambda_q2"]), f(inputs["l0_da_lambda_k2"])]),
        "subln": f(inputs["l0_da_subln"]),
        "decay": np.concatenate([f(inputs["l1_ret_decay_fwd"]), f(inputs["l1_ret_decay_bwd"])]),
        "gnw": f(inputs["l1_ret_gn_w"]), "gnb": f(inputs["l1_ret_gn_b"]),
    }
    base["g0_pre_mix"] = f(inputs["l0_norm_pre_mix"])
    base["g0_post_mix"] = f(inputs["l0_norm_post_mix"])
    base["g0_pre_ffn"] = f(inputs["l0_norm_pre_ffn"])
    base["g0_post_ffn"] = f(inputs["l0_norm_post_ffn"])
    base["g1_pre_mix"] = f(inputs["l1_norm_pre_mix"])
    base["g1_post_mix"] = f(inputs["l1_norm_post_mix"])
    base["g1_pre_ffn"] = f(inputs["l1_norm_pre_ffn"])
    base["g1_post_ffn"] = f(inputs["l1_norm_post_ffn"])
    base.update(host_consts())
    maps = []
    for xc in x_cores:
        m = dict(base)
        m["x"] = xc
        maps.append(m)
    return maps


_PROG = {}


def kernel(**inputs):
    xp = np.asarray(inputs["x_prompt"], dtype=np.float32)
    xs = np.asarray(inputs["x_sample"], dtype=np.float32)
    seqs = [2048] * 4 + [4096] * 2
    x_cores = []
    for c in range(NCORES):
        x_cores.append(np.ascontiguousarray(np.concatenate(
            [xp[4 * c:4 * c + 4].reshape(-1, D), xs[2 * c:2 * c + 2].reshape(-1, D)], axis=0)))
    key = tuple(seqs)
    if key not in _PROG:
        _PROG[key] = build_program(seqs)
    nc = _PROG[key]
    res = run_bass_kernel_spmd(nc, make_in_maps(inputs, x_cores), core_ids=list(range(NCORES)))
    yp = np.empty_like(xp)
    ys = np.empty_like(xs)
    for c in range(NCORES):
        y = np.asarray(res.results[c]["y"], dtype=np.float32)
        yp[4 * c:4 * c + 4] = y[:8192].reshape(4, 2048, D)
        ys[2 * c:2 * c + 2] = y[8192:].reshape(2, 4096, D)
    return (yp, ys)
```
